# Optimizing a Trainium2 kernel written in Bass

```python
import jax, jax.numpy as jnp
from jax import lax
import numpy as np


D_MODEL = 1024
BATCH = 8
SEQ = 4096
DEPTH = 2

CTX_LEN = 256
GRID_W = 64
BLOCK = 128
WINDOW = 128
ROPE_BASE = 10000.0
EPS = 1e-6
N_MOD = 6
D_FF = 4 * D_MODEL

A_HEADS = 4
A_HEAD_DIM = D_MODEL // 8
A_WIDTH = A_HEADS * A_HEAD_DIM
A_GATES = 4 * A_HEADS
B_HEADS = 8
B_KV_HEADS = 2
B_HEAD_DIM = D_MODEL // 16
B_WIDTH = B_HEADS * B_HEAD_DIM
B_KV_WIDTH = B_KV_HEADS * B_HEAD_DIM
AB_IN = 4 * A_WIDTH + A_GATES + B_WIDTH + 2 * B_KV_WIDTH
AB_OUT = A_WIDTH + B_WIDTH
C_HEADS = 8
C_KV_HEADS = 2
C_HEAD_DIM = D_MODEL // 8
C_WIDTH = C_HEADS * C_HEAD_DIM
C_KV_WIDTH = C_KV_HEADS * C_HEAD_DIM
C_IN = C_WIDTH + 2 * C_KV_WIDTH

kernel_name = 'hybrid_mlstm_swa_axial_gqa_dit'


def rms_norm(x, g):
    xf = x.astype(jnp.float32)
    y = xf * lax.rsqrt(jnp.mean(xf * xf, axis=-1, keepdims=True) + EPS)
    return (y * g.astype(jnp.float32)).astype(x.dtype)


def modulate(x, g, mod, i):
    return rms_norm(x, g) * (1 + mod[:, :, i + 1]) + mod[:, :, i]


def sq_relu_mlp(h, w1, w2):
    return jnp.square(jax.nn.relu(h @ w1)) @ w2


def axial_rope_tables(n_tokens, head_dim):
    rows = n_tokens // GRID_W
    row = jnp.broadcast_to(jnp.arange(rows)[:, None], (rows, GRID_W)).reshape(n_tokens).astype(jnp.float32)
    col = jnp.broadcast_to(jnp.arange(GRID_W)[None, :], (rows, GRID_W)).reshape(n_tokens).astype(jnp.float32)
    pairs_per_axis = head_dim // 4
    inv_freq = ROPE_BASE ** (-jnp.arange(pairs_per_axis, dtype=jnp.float32) / pairs_per_axis)
    ang = jnp.concatenate([row[:, None] * inv_freq, col[:, None] * inv_freq], axis=-1)
    return jnp.cos(ang), jnp.sin(ang)


def apply_rope(x, cos, sin):
    half = x.shape[-1] // 2
    xf = x.astype(jnp.float32)
    x1, x2 = xf[..., :half], xf[..., half:]
    cs, sn = cos[:, None, :], sin[:, None, :]
    return jnp.concatenate([x1 * cs - x2 * sn, x2 * cs + x1 * sn], axis=-1).astype(x.dtype)


def gqa_attend(q, k, v, sink=None):
    b, tq, hq, d = q.shape
    hkv = k.shape[2]
    g = hq // hkv
    s = jnp.einsum('bqhgd,bkhd->bhgqk', q.reshape(b, tq, hkv, g, d), k).astype(jnp.float32) * d ** -0.5
    if sink is None:
        p = jax.nn.softmax(s, axis=-1)
    else:
        s_sink = jnp.broadcast_to(sink.astype(jnp.float32).reshape(1, hkv, g, 1, 1), s.shape[:-1] + (1,))
        p = jax.nn.softmax(jnp.concatenate([s, s_sink], axis=-1), axis=-1)[..., :-1]
    o = jnp.einsum('bhgqk,bkhd->bqhgd', p.astype(v.dtype), v)
    return o.reshape(b, tq, hq * d)


def banded_window_attention(q, k, v, k_ctx, v_ctx, sink):
    b, t, hq, d = q.shape
    hkv = k.shape[2]
    g = hq // hkv
    nb = t // BLOCK
    n_ctx = k_ctx.shape[1]
    nw = 3 * BLOCK
    scale = d ** -0.5
    qb = jnp.moveaxis(q.reshape(b, nb, BLOCK, hkv, g, d), 1, 0)

    def band(a):
        a = jnp.pad(a, ((0, 0), (BLOCK, BLOCK), (0, 0), (0, 0))).reshape(b, nb + 2, BLOCK, hkv, d)
        a = jnp.concatenate([a[:, :-2], a[:, 1:-1], a[:, 2:]], axis=2)
        return jnp.moveaxis(a, 1, 0)

    kw, vw = band(k), band(v)
    qpos = jnp.arange(nb)[:, None, None] * BLOCK + jnp.arange(BLOCK)[None, :, None]
    kpos = (jnp.arange(nb)[:, None, None] - 1) * BLOCK + jnp.arange(nw)[None, None, :]
    valid = (jnp.abs(kpos - qpos) <= WINDOW) & (kpos >= 0) & (kpos < t)
    sink_l = sink.astype(jnp.float32).reshape(1, hkv, g, 1, 1)

    def one_block(args):
        qn, kn, vn, valid_n = args
        s_win = jnp.einsum('bqhgd,bkhd->bhgqk', qn, kn).astype(jnp.float32) * scale
        s_win = jnp.where(valid_n, s_win, -jnp.inf)
        s_ctx = jnp.einsum('bqhgd,bchd->bhgqc', qn, k_ctx).astype(jnp.float32) * scale
        s_sink = jnp.broadcast_to(sink_l, s_win.shape[:-1] + (1,))
        p = jax.nn.softmax(jnp.concatenate([s_win, s_ctx, s_sink], axis=-1), axis=-1).astype(v.dtype)
        o = (jnp.einsum('bhgqk,bkhd->bqhgd', p[..., :nw], vn)
             + jnp.einsum('bhgqc,bchd->bqhgd', p[..., nw:nw + n_ctx], v_ctx))
        return o.reshape(b, BLOCK, hq * d)

    o = lax.map(one_block, (qb, kw, vw, valid))
    return jnp.moveaxis(o, 0, 1).reshape(b, t, hq * d)


def mlstm_dir(q, k, v, log_i, log_f, state):
    n_chunks = q.shape[2] // BLOCK

    def to_chunks(a):
        a = a.reshape(a.shape[:2] + (n_chunks, BLOCK) + a.shape[3:])
        return jnp.moveaxis(a, 2, 0)

    lower = jnp.tril(jnp.ones((BLOCK, BLOCK), dtype=bool))

    def step(carry, inp):
        c_mat, n_vec, m = carry
        qc, kc, vc, ic, fc = inp
        bcum = jnp.cumsum(fc, axis=-1)
        d_log = jnp.where(lower, bcum[..., :, None] - bcum[..., None, :] + ic[..., None, :], -jnp.inf)
        inter_log = bcum + m[..., None]
        m_t = jnp.maximum(inter_log, jnp.max(d_log, axis=-1))
        inter_w = jnp.exp(inter_log - m_t)
        s = jnp.einsum('bhtd,bhsd->bhts', qc, kc) * jnp.exp(d_log - m_t[..., None])
        num = inter_w[..., None] * jnp.einsum('bhtd,bhde->bhte', qc, c_mat) + jnp.einsum('bhts,bhse->bhte', s, vc)
        den = inter_w * jnp.einsum('bhtd,bhd->bht', qc, n_vec) + jnp.sum(s, axis=-1)
        h = num / jnp.maximum(jnp.abs(den), jnp.exp(-m_t))[..., None]
        b_last = bcum[..., -1]
        w_log = b_last[..., None] - bcum + ic
        m_new = jnp.maximum(b_last + m, jnp.max(w_log, axis=-1))
        carry_w = jnp.exp(b_last + m - m_new)
        w = jnp.exp(w_log - m_new[..., None])
        c_new = carry_w[..., None, None] * c_mat + jnp.einsum('bhs,bhsd,bhse->bhde', w, kc, vc)
        n_new = carry_w[..., None] * n_vec + jnp.einsum('bhs,bhsd->bhd', w, kc)
        return (c_new, n_new, m_new), h

    state, h = lax.scan(step, state, (to_chunks(q), to_chunks(k), to_chunks(v), to_chunks(log_i), to_chunks(log_f)))
    h = jnp.moveaxis(h, 0, 2)
    return h.reshape(h.shape[:2] + (-1, h.shape[-1])), state


def mlstm_bidir(q, k, v, gates, state_f, state_b):
    li_f, lf_f, li_b, lf_b = gates
    h_f, st_f = mlstm_dir(q, k, v, li_f, lf_f, state_f)
    flip = lambda a: jnp.flip(a, axis=2)
    h_b, st_b = mlstm_dir(flip(q), flip(k), flip(v), flip(li_b), flip(lf_b), state_b)
    return h_f + flip(h_b), st_f, st_b


def ab_mixer(h_ctx, h_lat, w_in, gate_b, a_norm_g, q_norm_g, k_norm_g, sink, w_out, need_ctx):
    batch = h_lat.shape[0]
    sizes = [A_WIDTH] * 4 + [A_GATES, B_WIDTH, B_KV_WIDTH]
    bounds = [int(s) for s in np.cumsum(sizes)]

    def project(h):
        t = h.shape[1]
        qa, ka, va, oa, ga, qb, kb, vb = jnp.split(h @ w_in, bounds, axis=-1)
        heads_a = lambda a: a.reshape(batch, t, A_HEADS, A_HEAD_DIM).transpose(0, 2, 1, 3).astype(jnp.float32)
        qa, ka, va = heads_a(qa), heads_a(ka) * A_HEAD_DIM ** -0.5, heads_a(va)
        g = (ga + gate_b).astype(jnp.float32).reshape(batch, t, 4, A_HEADS).transpose(2, 0, 3, 1)
        gates = (g[0], jax.nn.log_sigmoid(g[1]), g[2], jax.nn.log_sigmoid(g[3]))
        qb = rms_norm(qb.reshape(batch, t, B_HEADS, B_HEAD_DIM), q_norm_g)
        kb = rms_norm(kb.reshape(batch, t, B_KV_HEADS, B_HEAD_DIM), k_norm_g)
        vb = vb.reshape(batch, t, B_KV_HEADS, B_HEAD_DIM)
        return (qa, ka, va, gates), oa, (qb, kb, vb)

    def mlstm_out(h, oa):
        h = rms_norm(h.transpose(0, 2, 1, 3), a_norm_g.reshape(A_HEADS, A_HEAD_DIM))
        return (h.reshape(h.shape[0], h.shape[1], A_WIDTH) * jax.nn.sigmoid(oa)).astype(oa.dtype)

    a_c, oa_c, (qb_c, kb_c, vb_c) = project(h_ctx)
    a_l, oa_l, (qb_l, kb_l, vb_l) = project(h_lat)
    zero = (jnp.zeros((batch, A_HEADS, A_HEAD_DIM, A_HEAD_DIM), jnp.float32),
            jnp.zeros((batch, A_HEADS, A_HEAD_DIM), jnp.float32),
            jnp.zeros((batch, A_HEADS), jnp.float32))
    ha_c, st_f, st_b = mlstm_bidir(*a_c, zero, zero)
    ha_l, _, _ = mlstm_bidir(*a_l, st_f, st_b)
    cos, sin = axial_rope_tables(h_lat.shape[1], B_HEAD_DIM)
    qb_l, kb_l = apply_rope(qb_l, cos, sin), apply_rope(kb_l, cos, sin)
    ob_l = banded_window_attention(qb_l, kb_l, vb_l, kb_c, vb_c, sink)
    y_lat = jnp.concatenate([mlstm_out(ha_l, oa_l), ob_l.astype(oa_l.dtype)], axis=-1) @ w_out
    y_ctx = None
    if need_ctx:
        ob_c = gqa_attend(qb_c, kb_c, vb_c, sink)
        y_ctx = jnp.concatenate([mlstm_out(ha_c, oa_c), ob_c.astype(oa_c.dtype)], axis=-1) @ w_out
    return y_ctx, y_lat


def c_mixer(h_ctx, h_lat, w_in, q_norm_g, k_norm_g, w_out, need_ctx):
    def project(h):
        b, t, _ = h.shape
        q, k, v = jnp.split(h @ w_in, [C_WIDTH, C_WIDTH + C_KV_WIDTH], axis=-1)
        q = rms_norm(q.reshape(b, t, C_HEADS, C_HEAD_DIM), q_norm_g)
        k = rms_norm(k.reshape(b, t, C_KV_HEADS, C_HEAD_DIM), k_norm_g)
        return q, k, v.reshape(b, t, C_KV_HEADS, C_HEAD_DIM)

    q_c, k_c, v_c = project(h_ctx)
    q_l, k_l, v_l = project(h_lat)
    b, t = h_lat.shape[:2]
    cos, sin = axial_rope_tables(t, C_HEAD_DIM)
    q_l, k_l = apply_rope(q_l, cos, sin), apply_rope(k_l, cos, sin)
    k_all = jnp.concatenate([k_c, k_l], axis=1)
    v_all = jnp.concatenate([v_c, v_l], axis=1)
    q_blocks = jnp.moveaxis(q_l.reshape(b, t // BLOCK, BLOCK, C_HEADS, C_HEAD_DIM), 1, 0)
    o = lax.map(lambda qb: gqa_attend(qb, k_all, v_all), q_blocks)
    y_lat = jnp.moveaxis(o, 0, 1).reshape(b, t, C_WIDTH) @ w_out
    y_ctx = gqa_attend(q_c, k_c, v_c) @ w_out if need_ctx else None
    return y_ctx, y_lat


def setup_inputs(seed: int = 0) -> dict:
    key = jax.random.key(seed)
    ks = jax.random.split(key, 24)
    n_even = (DEPTH + 1) // 2
    n_odd = DEPTH // 2
    f32 = jnp.float32
    nrm = lambda k, shape, scale: jax.random.normal(k, shape, f32) * scale
    gain = lambda k, shape: 1.0 + 0.05 * jax.random.normal(k, shape, f32)
    gate_b = jnp.concatenate([
        nrm(ks[9], (n_even, A_HEADS), 0.1),
        jax.random.uniform(ks[10], (n_even, A_HEADS), f32, 3.0, 6.0),
        nrm(ks[11], (n_even, A_HEADS), 0.1),
        jax.random.uniform(ks[12], (n_even, A_HEADS), f32, 3.0, 6.0)], axis=-1)
    return {
        'x': nrm(ks[0], (BATCH, SEQ, D_MODEL), 1.0),
        'c': nrm(ks[1], (BATCH, D_MODEL), 1.0),
        'ctx': nrm(ks[2], (BATCH, CTX_LEN, D_MODEL), 1.0),
        'c_ctx': nrm(ks[3], (D_MODEL,), 1.0),
        'ada_w': nrm(ks[4], (DEPTH, D_MODEL, N_MOD * D_MODEL), 0.5 * D_MODEL ** -0.5),
        'ada_b': nrm(ks[5], (DEPTH, N_MOD * D_MODEL), 0.02),
        'norm1_g': gain(ks[6], (DEPTH, D_MODEL)),
        'norm2_g': gain(ks[7], (DEPTH, D_MODEL)),
        'ab_w_in': nrm(ks[8], (n_even, D_MODEL, AB_IN), D_MODEL ** -0.5),
        'ab_gate_b': gate_b,
        'mlstm_norm_g': gain(ks[13], (n_even, A_WIDTH)),
        'swa_q_norm_g': gain(ks[14], (n_even, B_HEAD_DIM)),
        'swa_k_norm_g': gain(ks[15], (n_even, B_HEAD_DIM)),
        'swa_sink': nrm(ks[16], (n_even, B_HEADS), 0.5),
        'ab_w_out': nrm(ks[17], (n_even, AB_OUT, D_MODEL), AB_OUT ** -0.5),
        'c_w_in': nrm(ks[18], (n_odd, D_MODEL, C_IN), D_MODEL ** -0.5),
        'c_q_norm_g': gain(ks[19], (n_odd, C_HEAD_DIM)),
        'c_k_norm_g': gain(ks[20], (n_odd, C_HEAD_DIM)),
        'c_w_out': nrm(ks[21], (n_odd, C_WIDTH, D_MODEL), C_WIDTH ** -0.5),
        'mlp_w1': nrm(ks[22], (DEPTH, D_MODEL, D_FF), D_MODEL ** -0.5),
        'mlp_w2': nrm(ks[23], (DEPTH, D_FF, D_MODEL), D_FF ** -0.5),
    }


def reference(x, c, ctx, c_ctx, ada_w, ada_b, norm1_g, norm2_g, ab_w_in, ab_gate_b, mlstm_norm_g,
              swa_q_norm_g, swa_k_norm_g, swa_sink, ab_w_out, c_w_in, c_q_norm_g, c_k_norm_g, c_w_out,
              mlp_w1, mlp_w2):
    batch = x.shape[0]
    for layer in range(DEPTH):
        last = layer == DEPTH - 1
        mod_l = (jax.nn.silu(c) @ ada_w[layer] + ada_b[layer]).reshape(batch, 1, N_MOD, D_MODEL)
        mod_c = (jax.nn.silu(c_ctx) @ ada_w[layer] + ada_b[layer]).reshape(1, 1, N_MOD, D_MODEL)
        h_l = modulate(x, norm1_g[layer], mod_l, 0)
        h_c = modulate(ctx, norm1_g[layer], mod_c, 0)
        j = layer // 2
        if layer % 2 == 0:
            y_c, y_l = ab_mixer(h_c, h_l, ab_w_in[j], ab_gate_b[j], mlstm_norm_g[j], swa_q_norm_g[j],
                                swa_k_norm_g[j], swa_sink[j], ab_w_out[j], not last)
        else:
            y_c, y_l = c_mixer(h_c, h_l, c_w_in[j], c_q_norm_g[j], c_k_norm_g[j], c_w_out[j], not last)
        x = x + mod_l[:, :, 2] * y_l
        x = x + mod_l[:, :, 5] * sq_relu_mlp(modulate(x, norm2_g[layer], mod_l, 3), mlp_w1[layer], mlp_w2[layer])
        if not last:
            ctx = ctx + mod_c[:, :, 2] * y_c
            ctx = ctx + mod_c[:, :, 5] * sq_relu_mlp(modulate(ctx, norm2_g[layer], mod_c, 3), mlp_w1[layer], mlp_w2[layer])
    return x
```

```python
import numpy as np
from contextlib import ExitStack
import concourse.bass as bass
import concourse.mybir as mybir
from concourse.bass_utils import run_bass_kernel_spmd
import ml_dtypes

F32 = mybir.dt.float32
BF16 = mybir.dt.bfloat16
AF = mybir.ActivationFunctionType
ALU = mybir.AluOpType

D = 1024
T_LAT = 4096
T_CTX = 256
T = T_LAT + T_CTX
NT = T // 128
EPS = 1e-6
NEG = -30000.0
DEBUG_OUT = False


class Prog:
    ENG = ["pe", "act", "dve", "pool", "sp"]
    NSLOT = 16
    NEPOCH = 6
    EPOCH_LEN = 16000

    def __init__(self, nc, st):
        self.nc = nc
        self.ops = {e: [] for e in self.ENG}
        self.cnt = {e: 0 for e in self.ENG}
        self.epoch = {e: 0 for e in self.ENG}
        self.waited = {e: {} for e in self.ENG}
        self.ndma = 0
        self.dma_cnt = {}
        self.rcnt = 0
        self.sems = {}
        for e in self.ENG:
            for ep in range(self.NEPOCH):
                self.sems[(e, ep)] = st.enter_context(nc.semaphore("s_%s_%d" % (e, ep)))
        for s in range(self.NSLOT):
            self.sems[("dma", s)] = st.enter_context(nc.semaphore("s_dma_%d" % s))
            self.sems[("dmp", s)] = st.enter_context(nc.semaphore("s_dmp_%d" % s))
        self.ndma_p = 0
        self.sems[("recip", 0)] = st.enter_context(nc.semaphore("s_recip"))
        self.all_dma = []
        self.pending = []
        self.pending_toks = set()

    def _waits(self, en, waits, force=False):
        ws = []
        for tok in waits:
            if tok is None:
                continue
            if isinstance(tok, list):
                ws += self._waits(en, tok, force)
                continue
            key, val = tok
            if key[0] == en and not force:
                continue
            if self.waited[en].get(key, 0) >= val:
                continue
            self.waited[en][key] = val
            ws.append((key, val))
        return ws

    def op(self, en, fn, waits=(), inc=True, recip=False, force=False):
        ws = self._waits(en, waits, force=(recip or force))
        tok = None
        if recip:
            self.rcnt += 1
            tok = (("recip", 0), self.rcnt)
        elif inc:
            if self.cnt[en] >= self.EPOCH_LEN:
                self.epoch[en] += 1
                self.cnt[en] = 0
            self.cnt[en] += 1
            tok = ((en, self.epoch[en]), self.cnt[en])
        self.ops[en].append((ws, fn, tok))
        return tok

    def dma(self, out, in_, waits=(), en="sp", store=None, **kw):
        if store is None:
            store = "dram" in type(out.tensor).__name__.lower()
        if en == "pool":
            slot = self.ndma_p % self.NSLOT
            self.ndma_p += 1
            key = ("dmp", slot)
            prev = self.dma_cnt.get(key, 0)
            w = [t for t in self._flat(waits)]
            if prev > 0:
                w.append((key, prev))
            self.dma_cnt[key] = prev + 16
            tok = (key, prev + 16)
            self.all_dma.append(tok)
            self.ops["pool"].append((self._waits("pool", w), ("dma", out, in_, kw), tok))
            return tok
        slot = self.ndma % self.NSLOT
        self.ndma += 1
        key = ("dma", slot)
        prev = self.dma_cnt.get(key, 0)
        w = [t for t in self._flat(waits)]
        if prev > 0:
            w.append((key, prev))
        self.dma_cnt[key] = prev + 16
        tok = (key, prev + 16)
        self.all_dma.append(tok)
        rec = (w, ("dma", out, in_, kw), tok)
        if store:
            self.pending.append(rec)
            self.pending_toks.add(tok)
        else:
            if any(t in self.pending_toks for t in w):
                self.flush_stores()
            self._push_dma(en, rec)
            self.flush_stores()
        return tok

    def _flat(self, waits):
        for t in waits:
            if t is None:
                continue
            if isinstance(t, list):
                yield from self._flat(t)
            else:
                yield t

    def _push_dma(self, en, rec):
        w, fn, tok = rec
        self.ops[en].append((self._waits(en, w), fn, tok))

    def flush_stores(self):
        for rec in self.pending:
            self._push_dma("sp", rec)
        self.pending = []
        self.pending_toks = set()

    def mm(self, out, lhsT, rhs, start=True, stop=True, waits=(), inc=False):
        return self.op("pe", lambda e: e.matmul(out, lhsT=lhsT, rhs=rhs, start=start, stop=stop), waits, inc)

    def transpose(self, out, in_, ident, waits=(), inc=True):
        return self.op("pe", lambda e: e.transpose(out=out, in_=in_, identity=ident), waits, inc)

    def act(self, out, in_, func, bias=None, scale=None, accum_out=None, waits=(), inc=True, force=False):
        kw = {}
        if bias is not None:
            kw["bias"] = bias
        if scale is not None:
            kw["scale"] = scale
        if accum_out is not None:
            kw["accum_out"] = accum_out
        return self.op("act", lambda e: e.activation(out=out, in_=in_, func=func, **kw), waits, inc, force=force)

    def ts(self, en, out, in0, s1, s2=None, op0=ALU.mult, op1=None, waits=(), inc=True):
        if op1 is None:
            return self.op(en, lambda e: e.tensor_scalar(out=out, in0=in0, scalar1=s1, scalar2=None, op0=op0), waits, inc)
        return self.op(en, lambda e: e.tensor_scalar(out=out, in0=in0, scalar1=s1, scalar2=s2, op0=op0, op1=op1), waits, inc)

    def tt(self, en, out, in0, in1, op, waits=(), inc=True):
        return self.op(en, lambda e: e.tensor_tensor(out=out, in0=in0, in1=in1, op=op), waits, inc)

    def stt(self, out, in0, scalar, in1, op0, op1, waits=(), inc=True):
        return self.op("dve", lambda e: e.scalar_tensor_tensor(out=out, in0=in0, scalar=scalar, in1=in1, op0=op0, op1=op1), waits, inc)

    def copy(self, en, out, in_, waits=(), inc=True):
        if en == "act":
            return self.act(out, in_, AF.Copy, waits=waits, inc=inc)
        return self.op(en, lambda e: e.tensor_copy(out=out, in_=in_), waits, inc)

    def memset(self, en, ap, val, waits=(), inc=True):
        return self.op(en, lambda e: e.memset(ap, val), waits, inc)

    def recip(self, out, in_, waits=()):
        return self.op("dve", lambda e: e.reciprocal(out=out, in_=in_), waits, recip=True)

    def emit(self, extra_final=()):
        nc = self.nc
        self.flush_stores()
        finals = list(self.all_dma) + [t for t in extra_final if t is not None]
        self.all_dma = []
        fin = {en: self._waits(en, finals) for en in self.ENG}
        ops = self.ops
        self.ops = {e: [] for e in self.ENG}
        sems = self.sems
        with nc.Block() as block:
            def run(en, eng):
                for ws, fn, tok in ops[en]:
                    for key, val in ws:
                        eng.wait_ge(sems[key], val)
                    if isinstance(fn, tuple):
                        _, out, in_, kw = fn
                        ins = eng.dma_start(out=out, in_=in_, **kw)
                        ins.then_inc(sems[tok[0]], 16)
                    else:
                        ins = fn(eng)
                        if tok is not None:
                            ins.then_inc(sems[tok[0]], 1)
                for key, val in fin[en]:
                    eng.wait_ge(sems[key], val)

            @block.tensor
            def _(e):
                run("pe", e)

            @block.scalar
            def _(e):
                run("act", e)

            @block.vector
            def _(e):
                run("dve", e)

            @block.gpsimd
            def _(e):
                run("pool", e)

            @block.sync
            def _(e):
                run("sp", e)


class Slot:
    def __init__(self, buf):
        self.buf = buf
        self.free = []


class Ring:
    def __init__(self, bufs):
        self.slots = [b if isinstance(b, Slot) else Slot(b) for b in bufs]
        self.i = 0

    @property
    def bufs(self):
        return [s_.buf for s_ in self.slots]

    def get(self):
        k = self.i % len(self.slots)
        self.i += 1
        return self.slots[k].buf, self.slots[k].free, k

    def rel(self, k, toks):
        self.slots[k].free = [t for t in toks if t is not None]


class Alloc:
    def __init__(self, nc, st):
        self.nc, self.st, self.n = nc, st, 0

    def sb(self, shape, dt, name=None):
        self.n += 1
        return self.st.enter_context(self.nc.sbuf_tensor("%s_%d_%d" % (name or "t", id(self) % 9973, self.n), list(shape), dt))

    def ps(self, shape=(128, 512), dt=F32, name=None):
        self.n += 1
        return self.st.enter_context(self.nc.psum_tensor("%s_%d_%d" % (name or "p", id(self) % 9973, self.n), list(shape), dt))

    def ring(self, n, shape, dt, name=None, psum=False):
        return Ring([(self.ps(shape, dt, name) if psum else self.sb(shape, dt, name)) for _ in range(n)])


class WStream:
    def __init__(self, P, wfr, wbr, srcs, depth=2, eng="act"):
        self.P, self.wfr, self.wbr, self.srcs, self.depth, self.eng = P, wfr, wbr, srcs, depth, eng
        self.q = []
        self.i = 0

    def _issue(self):
        P = self.P
        wf, ff, fk = self.wfr.get()
        ld = P.dma(wf[:], self.srcs[self.i], waits=ff)
        wb, bf_, bk = self.wbr.get()
        tc = P.copy(self.eng, wb[:], wf[:], waits=[ld] + bf_)
        self.wfr.rel(fk, [tc])
        self.q.append((wb, tc, bk))
        self.i += 1

    def next(self):
        while len(self.q) < self.depth + 1 and self.i < len(self.srcs):
            self._issue()
        return self.q.pop(0)


def subgroups(t0, t1, maxn=512):
    out = []
    t = t0
    while t < t1:
        n = min(maxn, t1 - t)
        out.append((t, n))
        t += n
    return out


def block_mod(nc, P, Dm):
    with ExitStack() as st:
        A = Alloc(nc, st)
        cv = A.sb([128, 8, 2], F32)
        sg = A.sb([128, 8, 2], F32)
        sc = A.sb([128, 8, 2], F32)
        adabr = A.sb([2, 2, 6144], F32)
        ng = A.sb([128, 2, 2, 8], F32)
        modrow = A.sb([2, 2, 6144], F32)
        ident = A.sb([128, 128], F32)
        modsb = A.sb([128, 2, 48, 2], F32)
        modv = A.sb([128, 2, 6, 8, 2], F32)
        awr = A.ring(2, [128, 8, 1536], F32, "aw")
        psr = A.ring(2, [128, 512], F32, "psmr", psum=True)
        pst = A.ps([128, 2, 48, 2], F32)
        l_cv = P.dma(cv[:], Dm["CV"])
        l_ab = [P.dma(adabr[:, l, :], Dm["ADABR"][l:l + 1, :].partition_broadcast(2)) for l in range(2)]
        l_ng = P.dma(ng[:], Dm["NG"])
        l_id = P.dma(ident[:], Dm["IDENT"])
        t_sc = P.tt("dve", sc[:], cv[:], sg[:], ALU.mult, waits=[P.act(sg[:], cv[:], AF.Sigmoid, waits=[l_cv])])
        t_ev = None
        for l in range(2):
            for q in range(4):
                aw, wfree, k = awr.get()
                src = Dm["ADAW"][l].rearrange("(kc p) n -> p kc n", p=128)
                ld = [P.dma(aw[:, 0:4, :], src[:, 0:4, q * 1536:(q + 1) * 1536], waits=wfree),
                      P.dma(aw[:, 4:8, :], src[:, 4:8, q * 1536:(q + 1) * 1536], waits=wfree)]
                for j in range(3):
                    ps, pf, pk = psr.get()
                    for kc in range(8):
                        tmm = P.mm(ps[0:2, :], sc[:, kc, :], aw[:, kc, j * 512:(j + 1) * 512], start=(kc == 0), stop=(kc == 7),
                                   waits=ld + [t_sc] + pf, inc=(kc == 7))
                    c0 = q * 1536 + j * 512
                    t_ev = P.tt("dve", modrow[:, l, c0:c0 + 512], ps[0:2, :], adabr[:, l, c0:c0 + 512], ALU.add, waits=[tmm] + l_ab)
                    psr.rel(pk, [t_ev])
                awr.rel(k, [tmm])
        t_T = None
        for l in range(2):
            for ch in range(48):
                t_T = P.transpose(pst[:, l, ch, :], modrow[:, l, ch * 128:(ch + 1) * 128], ident[0:2, 0:2], waits=[t_ev, l_id], inc=(l == 1 and ch == 47))
        t_ms = P.copy("dve", modsb[:].rearrange("p l c j -> p (l c j)"), pst[:].rearrange("p l c j -> p (l c j)"), waits=[t_T])
        for l in range(2):
            for j in range(2):
                for (kind, c0) in ((0, 0), (2, 16), (3, 24), (5, 40)):
                    P.copy("dve", modv[:, l, kind, :, j], modsb[:, l, c0:c0 + 8, j], inc=False)
                P.stt(modv[:, l, 1, :, j], modsb[:, l, 8:16, j], 1.0, ng[:, l, 0, :], ALU.add, ALU.mult, waits=[l_ng], inc=False)
                tk = P.stt(modv[:, l, 4, :, j], modsb[:, l, 32:40, j], 1.0, ng[:, l, 1, :], ALU.add, ALU.mult)
        P.dma(Dm["MODV"], modv[:], waits=[tk])
        P.emit()


def norm_modulate(nc, P, xsrc, hT, modv, kind_shift, kind_gs, sgs, consts, l_consts, col0=0):
    with ExitStack() as st:
        A = Alloc(nc, st)
        xr = A.ring(3, [128, 8, 512], F32, "x")
        sqr = A.ring(3, [128, 512], BF16, "sq")
        rsr = A.ring(3, [128, 512], F32, "rs")
        tmr = A.ring(3, [128, 512], F32, "tm")
        pss = A.ring(2, [128, 512], F32, "pss", psum=True)
        onesb = A.sb([128, 128], BF16, "onesb")
        t_ob = P.memset("pool", onesb[:], 1.0)
        epsb = consts["eps"]
        xs = xsrc.rearrange("(kc p) t -> p kc t", p=128)
        last = [None, None]

        def stage1(t0, n, j):
            xt, xfree, xk = xr.get()
            ld = P.dma(xt[:, :, 0:n], xs[:, :, t0:t0 + n], waits=xfree)
            ps, pfree, pk = pss.get()
            for kc in range(8):
                sq, sfree, sk = sqr.get()
                tq = P.act(sq[:, 0:n], xt[:, kc, 0:n], AF.Square, waits=[ld] + sfree)
                tm_ = P.mm(ps[:, 0:n], onesb[:], sq[:, 0:n], start=(kc == 0), stop=(kc == 7), waits=[tq, t_ob, l_consts] + pfree, inc=True)
                sqr.rel(sk, [tm_])
            rs, rfree, rk = rsr.get()
            P.act(rs[:, 0:n], ps[:, 0:n], AF.Ln, bias=epsb[:, 0:1], scale=1.0 / D, waits=[tm_] + rfree, inc=False)
            t_rs = P.act(rs[:, 0:n], rs[:, 0:n], AF.Exp, scale=-0.5)
            pss.rel(pk, [t_rs])
            return (t0, n, j, xt, xk, ld, rs, rk, t_rs)

        def stage2(S):
            t0, n, j, xt, xk, ld, rs, rk, t_rs = S
            for kc in range(8):
                tm, tfree, tk = tmr.get()
                t1 = P.stt(tm[:, 0:n], xt[:, kc, 0:n], modv[:, kind_gs, kc, j:j + 1], rs[:, 0:n], ALU.mult, ALU.mult, waits=[t_rs, ld] + tfree)
                if kc < 2:
                    t2 = P.ts("pool", hT[:, kc, t0 - col0:t0 - col0 + n], tm[:, 0:n], modv[:, kind_shift, kc, j:j + 1], op0=ALU.add, waits=[t1])
                else:
                    t2 = P.act(hT[:, kc, t0 - col0:t0 - col0 + n], tm[:, 0:n], AF.Identity, bias=modv[:, kind_shift, kc, j:j + 1], waits=[t1])
                tmr.rel(tk, [t2])
                last[0 if kc < 2 else 1] = t2
            xr.rel(xk, [t1])
            rsr.rel(rk, [t1])

        prev = None
        for sg in sgs:
            cur = stage1(*sg)
            if prev is not None:
                stage2(prev)
            prev = cur
        stage2(prev)
        P.emit()
    return last


def load_consts(nc, P, A, Dm, names):
    out, toks = {}, []
    for nm in names:
        src = Dm[nm]
        t = A.sb(list(src.shape), F32, nm)
        toks.append(P.dma(t[:], src))
        out[nm] = t
    return out, toks


IN_PARTS = ("tm", "copy", "rope")
ROPE_LVL = 4


def block_inproj(nc, P, Dm, l):
    xsrc = Dm["XIN"] if l == 0 else Dm["X1"]
    with ExitStack() as st:
        A = Alloc(nc, st)
        hT = A.sb([128, 8, T], BF16, "hT")
        ntm = 1680 if l == 0 else 256
        WT = A.sb([128, 8, ntm], BF16, "WT")
        modv = A.sb([128, 6, 8, 2], F32, "modv")
        ones = A.sb([128, 128], F32, "ones")
        epsb = A.sb([128, 1], F32, "eps")
        l_mod = P.dma(modv[:], Dm["MODV"][:, l])
        l_ones = P.dma(ones[:], Dm["ONES"])
        t_eps = P.memset("pool", epsb[:], EPS)
        consts = {"ones": ones, "eps": epsb}
        with ExitStack() as st2:
            A2 = Alloc(nc, st2)
            wsr = A2.ring(2, [128, ntm], F32, "wts")
            wt_toks = []
            for kc in range(8):
                ws, wfree, wk = wsr.get()
                ld = P.dma(ws[:], Dm["WT%d" % l][:, kc, :], waits=wfree)
                tk = P.copy("act", WT[:, kc, :], ws[:], waits=[ld])
                wsr.rel(wk, [tk])
                wt_toks.append(tk)
            sgs = [(0, 256, 1)] + [(t0, n, 0) for (t0, n) in subgroups(T_CTX, T)]
            t_h = norm_modulate(nc, P, xsrc, hT, modv, 0, 1, sgs, consts, [l_mod, l_ones, t_eps])
        with ExitStack() as st3:
            A3 = Alloc(nc, st3)
            dh = 64 if l == 0 else 128
            cosT = A3.sb([128, T], F32, "cos")
            sinT = A3.sb([128, T], F32, "sin")
            blk = A3.sb([128, 128], F32, "blk")
            gq = A3.sb([128, 4], F32, "gq")
            l_c = [P.dma(cosT[:], Dm["COS%d" % l]), P.dma(sinT[:], Dm["SIN%d" % l]), P.dma(blk[:], Dm["BLK%d" % l]),
                   P.dma(gq[:], Dm["QKG%d" % l])]
            wfr = A3.ring(3, [128, 1024], F32, "wf")
            wbr = A3.ring(3, [128, 1024], BF16, "wb")
            psr = A3.ring(3, [128, 512], F32, "psp", psum=True)
            ps2r = A3.ring(2, [128, 512], F32, "ps2", psum=True)
            stg = A3.ring(3, [128, 512], BF16, "stg")
            sqr = A3.ring(2, [128, 512], BF16, "sq")
            blkb = A3.sb([128, 128], BF16, "blkb")
            l_c.append(P.copy("pool", blkb[:], blk[:], waits=l_c))
            gqn = A3.sb([128, 4], F32, "gqn")
            l_c.append(P.ts("pool", gqn[:], gq[:], -1.0, op0=ALU.mult, waits=l_c))
            rsr = A3.ring(2, [128, 512], F32, "rs")
            ar = A3.ring(2, [128, 512], F32, "a")
            a0r = A3.ring(2, [128, 512], F32, "a0")
            br = A3.ring(2, [128, 512], F32, "b")
            tokd = [t_h] + wt_toks
            if "tm" not in IN_PARTS:
                pass
            elif l == 0:
                gb = A3.sb([128, 16], F32, "gb")
                l_gb = P.dma(gb[:], Dm["GATEB"].partition_broadcast(128))
                s_ka = A3.ring(2, [128, 512], BF16, "ska")
                s_va = A3.ring(2, [128, 512], BF16, "sva")
                s_oa = A3.ring(2, [128, 512], F32, "soa")
                s_gv = A3.ring(2, [128, 16], F32, "sga")
                s_vb = A3.ring(2, [128, 128], BF16, "svb")
                for c in range(NT):
                    tsl = slice(c * 128, (c + 1) * 128)
                    for gi, (c0, ncol) in enumerate(((0, 512), (512, 512), (1024, 512), (1536, 144))):
                        ps, pfree, pk = psr.get()
                        for kc in range(8):
                            tmm = P.mm(ps[:, 0:ncol], hT[:, kc, tsl], WT[:, kc, c0:c0 + ncol], start=(kc == 0), stop=(kc == 7),
                                       waits=tokd + pfree, inc=(kc == 7))
                        if gi == 0:
                            sbuf, sfree, sk = s_ka.get()
                            te = P.act(sbuf[:], ps[:, 0:512], AF.Copy, scale=128.0 ** -0.5, waits=[tmm] + sfree)
                            s_ka.rel(sk, [P.dma(Dm["KA"][tsl, :], sbuf[:], waits=[te])])
                        elif gi == 1:
                            sbuf, sfree, sk = s_va.get()
                            te = P.copy("dve", sbuf[:], ps[:, 0:512], waits=[tmm] + sfree)
                            s_va.rel(sk, [P.dma(Dm["VA"][tsl, :], sbuf[:], waits=[te])])
                        elif gi == 2:
                            sbuf, sfree, sk = s_oa.get()
                            te = P.act(sbuf[:], ps[:, 0:512], AF.Sigmoid, waits=[tmm] + sfree)
                            s_oa.rel(sk, [P.dma(Dm["SOA"][tsl, :], sbuf[:], waits=[te])])
                        else:
                            sbuf, sfree, sk = s_gv.get()
                            P.tt("dve", sbuf[:], ps[:, 0:16], gb[:], ALU.add, waits=[tmm, l_gb] + sfree, inc=False)
                            sb2, sfree2, sk2 = s_vb.get()
                            te = P.copy("dve", sb2[:], ps[:, 16:144], waits=sfree2)
                            s_gv.rel(sk, [P.dma(Dm["GA"][tsl, :], sbuf[:], waits=[te])])
                            s_vb.rel(sk2, [P.dma(Dm["VB"][tsl, :], sb2[:], waits=[te])])
                        psr.rel(pk, [te])
            else:
                s_v = A3.ring(2, [128, 256], BF16, "sv")
                for c in range(NT):
                    tsl = slice(c * 128, (c + 1) * 128)
                    ps, pfree, pk = psr.get()
                    for kc in range(8):
                        tmm = P.mm(ps[:, 0:256], hT[:, kc, tsl], WT[:, kc, :], start=(kc == 0), stop=(kc == 7), waits=tokd + pfree, inc=(kc == 7))
                    sbuf, sfree, sk = s_v.get()
                    te = P.copy("dve", sbuf[:], ps[:, 0:256], waits=[tmm] + sfree)
                    s_v.rel(sk, [P.dma(Dm["VC"][tsl, :], sbuf[:], waits=[te])])
                    psr.rel(pk, [te])
            if l == 0:
                specs = ([("copy", "QA_T", i, 1.0, True) for i in range(4)] + [("copy", "KA_T", i, 128.0 ** -0.5, True) for i in range(4)]
                         + [("rope", "QB_T", i, 0, True) for i in range(4)] + [("rope", "KB_T", i, 2, True) for i in range(2)])
            else:
                specs = [("rope", "QC_T", i, 0, False) for i in range(8)] + [("rope", "KC_T", i, 2, True) for i in range(2)]
            sg_all = [(0, 256)] + subgroups(T_CTX, T)
            ei = 0
            wst = WStream(P, wfr, wbr, [Dm["WF%d" % l][ci] for ci in range(len(specs))], depth=2)

            def rope_stage2(R):
                (n, t0, g0, a0, a0k, t_a0, sq, sqk, tq, dst_, di_, dcol) = R
                ps2, p2free, p2k = ps2r.get()
                tm2 = P.mm(ps2[:, 0:n], blkb[:], sq[:, 0:n], waits=[tq] + l_c + p2free, inc=True)
                sqr.rel(sqk, [tm2])
                rs, rfree, rk = rsr.get()
                P.act(rs[:, 0:n], ps2[:, 0:n], AF.Ln, bias=epsb[:, 0:1], scale=1.0 / dh, waits=[tm2] + rfree, inc=False)
                t_rs = P.act(rs[:, 0:n], rs[:, 0:n], AF.Exp, scale=-0.5)
                ps2r.rel(p2k, [t_rs])
                a, afree, ak = ar.get()
                b, bfree, bk = br.get()
                P.stt(a[:, 0:n], a0[:, 0:n], gq[:, g0:g0 + 1], cosT[:, t0:t0 + n], ALU.mult, ALU.mult, waits=[t_a0] + l_c + afree, inc=False)
                P.stt(b[0:64, 0:n], a0[64:128, 0:n], gqn[64:128, g0:g0 + 1], sinT[64:128, t0:t0 + n], ALU.mult, ALU.mult, waits=bfree, inc=False)
                t_b = P.stt(b[64:128, 0:n], a0[0:64, 0:n], gqn[0:64, g0:g0 + 1], sinT[0:64, t0:t0 + n], ALU.mult, ALU.mult)
                a0r.rel(a0k, [t_b, tq])
                t_ab = P.tt("dve", a[:, 0:n], a[:, 0:n], b[:, 0:n], ALU.add)
                ob, ofree, ok = stg.get()
                te = P.tt("pool", ob[:, 0:n], a[:, 0:n], rs[:, 0:n], ALU.mult, waits=[t_rs, t_ab] + ofree)
                ar.rel(ak, [te])
                br.rel(bk, [te])
                rsr.rel(rk, [te])
                td = P.dma(Dm[dst_][di_ * 128:(di_ + 1) * 128, dcol:dcol + n], ob[:, 0:n], waits=[te])
                stg.rel(ok, [td])

            pend_rope = None
            for ci, (kind, dst, di, par, with_ctx) in enumerate(specs):
                wb, tcast, wbk = wst.next()
                last_mm = None
                for (t0, n) in (sg_all if with_ctx else sg_all[1:]):
                    ps, pfree, pk = psr.get()
                    for kc in range(8):
                        tmm = P.mm(ps[:, 0:n], wb[:, kc * 128:(kc + 1) * 128], hT[:, kc, t0:t0 + n], start=(kc == 0), stop=(kc == 7),
                                   waits=[tcast, t_h] + pfree, inc=(kc == 7))
                    last_mm = tmm
                    dcol = t0 if with_ctx else t0 - T_CTX
                    if kind == "copy":
                        ob, ofree, ok = stg.get()
                        ei += 1
                        if ei % 2 == 0:
                            te = P.act(ob[:, 0:n], ps[:, 0:n], AF.Copy, scale=par, waits=[tmm] + ofree)
                        else:
                            te = P.ts("dve", ob[:, 0:n], ps[:, 0:n], par, op0=ALU.mult, waits=[tmm] + ofree)
                        psr.rel(pk, [te])
                        td = P.dma(Dm[dst][di * 128:(di + 1) * 128, dcol:dcol + n], ob[:, 0:n], waits=[te])
                        stg.rel(ok, [td])
                    else:
                        a0, a0free, a0k = a0r.get()
                        t_a0 = P.copy("act", a0[:, 0:n], ps[:, 0:n], waits=[tmm] + a0free)
                        psr.rel(pk, [t_a0])
                        sq, sqfree, sqk = sqr.get()
                        tq = P.act(sq[:, 0:n], a0[:, 0:n], AF.Square, waits=[t_a0] + sqfree)
                        if pend_rope is not None:
                            rope_stage2(pend_rope)
                        pend_rope = (n, t0, par, a0, a0k, t_a0, sq, sqk, tq, dst, di, dcol)
                wbr.rel(wbk, [last_mm])
            if pend_rope is not None:
                rope_stage2(pend_rope)
            P.emit()


def block_mlstm(nc, P, Dm):
    with ExitStack() as st:
        A = Alloc(nc, st)
        C, tl = load_consts(nc, P, A, Dm, ["ONES", "UTRI", "LTRI", "IDENT"])
        ones, utri, ltri, ident = C["ONES"], C["UTRI"], C["LTRI"], C["IDENT"]
        idb = A.sb([128, 128], BF16, "idb")
        t_idb = P.copy("pool", idb[:], ident[:], waits=tl)
        G = A.sb([128, NT, 16], F32, "G")
        LF = A.sb([128, 2, NT, 4], F32, "LF")
        LI = A.sb([128, 2, NT, 4], F32, "LI")
        CUM = A.sb([128, 2, NT, 4], F32, "CUM")
        COLF = A.sb([128, 2, NT, 4], F32, "COLF")
        ECOL = A.sb([128, 2, NT, 4], F32, "ECOL")
        CARRY = A.sb([128, 2, NT, 4], F32, "CARRY")
        mlg = A.sb([128, 512], F32, "mlg")
        epsb = A.sb([128, 1], F32, "eps")
        oneb = A.sb([128, 1], F32, "oneb")
        P.memset("pool", epsb[:], EPS, inc=False)
        t_c1 = P.memset("pool", oneb[:], 1.0)
        l_g = P.dma(G[:], Dm["GA"].rearrange("(c p) g -> p c g", p=128))
        l_mlg = P.dma(mlg[:], Dm["MLNG"].partition_broadcast(128))
        for d_, (ci, cf) in enumerate(((0, 4), (8, 12))):
            P.act(LF[:, d_], G[:, :, cf:cf + 4], AF.Exp, scale=-1.0, waits=[l_g], inc=False)
            P.act(LF[:, d_], LF[:, d_], AF.Ln, bias=oneb[:, 0:1], scale=1.0, waits=[t_c1], inc=False)
            t_lf = P.act(LF[:, d_], LF[:, d_], AF.Copy, scale=-1.0)
            t_li = P.copy("dve", LI[:, d_], G[:, :, ci:ci + 4], waits=[l_g])
        psg = A.ps([128, 512], F32, "psg")
        LFf = LF[:].rearrange("p d c h -> p d (c h)")
        P.mm(psg[:, 0:136], utri[:], LFf[:, 0, :], waits=[t_lf] + tl)
        P.mm(psg[:, 136:272], ltri[:], LFf[:, 1, :])
        t_cum = P.mm(psg[:, 272:408], ones[:], LFf[:, 0, :], inc=True)
        CUMf = CUM[:].rearrange("p d c h -> p (d c h)")
        t_cc = P.copy("dve", CUMf, psg[:, 0:272], waits=[t_cum])
        CARf = CARRY[:].rearrange("p d c h -> p d (c h)")
        t_e1 = P.act(CARf[:, 0, :], psg[:, 272:408], AF.Exp, waits=[t_cum, t_cc])
        t_cum2 = P.mm(psg[:, 272:408], ones[:], LFf[:, 1, :], waits=[t_e1, t_cc], inc=True)
        t_e2 = P.act(CARf[:, 1, :], psg[:, 272:408], AF.Exp, waits=[t_cum2])
        COLFf = COLF[:].rearrange("p d c h -> p (d c h)")
        LIf = LI[:].rearrange("p d c h -> p (d c h)")
        ECOLf = ECOL[:].rearrange("p d c h -> p (d c h)")
        P.tt("dve", COLFf, LIf, CUMf, ALU.subtract, waits=[t_li], inc=False)
        t_colf0 = P.copy("dve", ECOLf, CUMf)
        t_colf = P.act(COLFf, COLFf, AF.Exp, waits=[t_colf0])
        t_ecol = P.act(ECOLf, ECOLf, AF.Exp, scale=-1.0)
        gate_toks = [t_colf, t_ecol, t_e1, t_e2, t_idb]

        qTr = A.ring(2, [128, T], BF16, "qT")
        kTr = A.ring(2, [128, T], BF16, "kT")
        Ktr = A.ring(2, [128, NT, 128], BF16, "Kt")
        Var = A.ring(2, [128, NT, 136], BF16, "Va")
        SOr = A.ring(1, [128, NT, 128], F32, "SO")
        Vc = A.sb([128, 2, NT, 136], BF16, "Vc")
        CB = A.sb([128, 2, NT, 136], BF16, "CB")
        Cst = A.sb([128, 2, 3, 136], F32, "Cst")
        OUTr = A.ring(2, [128, T], BF16, "OUTS")
        sA, sB = Slot(A.ps([128, 512], F32, "psA")), Slot(A.ps([128, 512], F32, "psB"))
        psU = Ring([sA, sB])
        psS = A.ring(2, [128, 512], F32, "psS", psum=True)
        psO0 = Ring([Slot(A.ps([128, 512], F32, "psO0")), sA])
        psO1 = Ring([Slot(A.ps([128, 512], F32, "psO1")), sB])
        psT = A.ring(1, [128, 128], BF16, "psT", psum=True)
        Pfr = A.ring(2, [128, 128], BF16, "Pf")
        Pbr = A.ring(2, [128, 128], BF16, "Pb")
        Hfr = A.ring(2, [128, 128], F32, "Hf")
        hr = A.ring(2, [128, 128], F32, "h")
        hnr = A.ring(2, [128, 128], BF16, "hn")
        gsr = A.ring(2, [128, 128], F32, "gs")
        jkr = A.ring(2, [128, 128], F32, "jk")
        smr = A.ring(4, [128, 8], F32, "sm")
        vc_free = []
        for h in range(4):
            qT, f1, k1 = qTr.get()
            kT, f2, k2 = kTr.get()
            Kt, f3, k3 = Ktr.get()
            Va, f4, k4 = Var.get()
            SO, f5, k5 = SOr.get()
            hs = slice(h * 128, (h + 1) * 128)
            l_q = P.dma(qT[:], Dm["QA_T"][hs, :], waits=f1)
            l_k = P.dma(kT[:], Dm["KA_T"][hs, :], waits=f2)
            l_kt = P.dma(Kt[:], Dm["KA"].rearrange("(c p) n -> p c n", p=128)[:, :, hs], waits=f3)
            l_va = P.dma(Va[:, :, 0:128], Dm["VA"].rearrange("(c p) n -> p c n", p=128)[:, :, hs], waits=f4)
            t_on = P.memset("pool", Va[:, :, 128:129], 1.0, waits=f4)
            l_so = P.dma(SO[:], Dm["SOA"].rearrange("(c p) n -> p c n", p=128)[:, :, hs], waits=f5)
            vtok = {}
            for d_ in range(2):
                tv = P.tt("dve", Vc[:, d_, :, 0:129], Va[:, :, 0:129], COLF[:, d_, :, h:h + 1].broadcast_to([128, NT, 129]), ALU.mult,
                          waits=[l_va, t_on] + gate_toks + vc_free)
                for c in range(NT):
                    vtok[(d_, c)] = tv
            orders = [list(range(NT)), [1, 0] + list(range(NT - 1, 1, -1))]
            t0s = [P.memset("dve", Cst[:, d_, 0, 0:129], 0.0, waits=vc_free) for d_ in range(2)]
            P.memset("pool", CB[:, 0, orders[0][0], 0:129], 0.0, waits=vc_free, inc=False)
            t_cb0 = P.memset("pool", CB[:, 1, orders[1][0], 0:129], 0.0, waits=vc_free)
            cbtok = {(0, orders[0][0]): t_cb0, (1, orders[1][0]): t_cb0}
            cbhist = {0: [], 1: []}
            for i in range(NT - 1):
                for d_ in range(2):
                    c = orders[d_][i]
                    cn = orders[d_][i + 1]
                    pu, pufree, puk = psU.get()
                    tmm = P.mm(pu[:, 0:129], Kt[:, c, :], Vc[:, d_, c, 0:129], waits=[l_kt, vtok[(d_, c)]] + pufree, inc=True)
                    src = Cst[:, d_, i % 3, 0:129]
                    dst = Cst[:, d_, (i + 1) % 3, 0:129]
                    P.tt("dve", dst, src, pu[:, 0:129], ALU.add, waits=[tmm] + ([cbhist[d_][i - 3]] if i >= 3 else []), inc=False)
                    t_s = P.ts("dve", dst, dst, CARRY[:, d_, c, h:h + 1], op0=ALU.mult, waits=gate_toks)
                    psU.rel(puk, [t_s])
                    t_cb = P.copy("act", CB[:, d_, cn, 0:129], dst, waits=[t_s])
                    cbhist[d_].append(t_cb)
                    cbtok[(d_, cn)] = t_cb
            OUTS, fo, ko = OUTr.get()
            rd = []
            stA, stB = {}, {}

            def stageA(c):
                tsl = slice(c * 128, (c + 1) * 128)
                pS, psfree, psk = psS.get()
                t_S = P.mm(pS[:, 0:128], kT[:, tsl], qT[:, tsl], waits=[l_q, l_k] + psfree, inc=True)
                Pf, pff, pfk = Pfr.get()
                Pb, pbf, pbk = Pbr.get()
                P.tt("dve", Pf[:], pS[:, 0:128], utri[:], ALU.mult, waits=[t_S] + pff, inc=False)
                t_pb = P.tt("dve", Pb[:], pS[:, 0:128], ltri[:], ALU.mult, waits=pbf)
                psS.rel(psk, [t_pb])
                pO0, pof0, pok0 = psO0.get()
                pO1, pof1, pok1 = psO1.get()
                P.mm(pO0[:, 0:129], Pf[:], Vc[:, 0, c, 0:129], start=True, stop=False, waits=[t_pb, vtok[(0, c)]] + pof0)
                t_o0 = P.mm(pO0[:, 0:129], qT[:, tsl], CB[:, 0, c, 0:129], start=False, stop=True, waits=[cbtok[(0, c)]], inc=True)
                P.mm(pO1[:, 0:129], Pb[:], Vc[:, 1, c, 0:129], start=True, stop=False, waits=[t_pb, vtok[(1, c)]] + pof1)
                t_o1 = P.mm(pO1[:, 0:129], qT[:, tsl], CB[:, 1, c, 0:129], start=False, stop=True, waits=[cbtok[(1, c)]], inc=True)
                Pfr.rel(pfk, [t_o0])
                Pbr.rel(pbk, [t_o1])
                stA[c] = (pO0, pok0, pO1, pok1, t_o0, t_o1)
                return [t_o0, t_o1]

            def stageB(c):
                pO0, pok0, pO1, pok1, t_o0, t_o1 = stA.pop(c)
                sm, smf, smk = smr.get()
                rt = []
                for d_, (pOd, tk_) in enumerate(((pO0, t_o0), (pO1, t_o1))):
                    P.ts("dve", sm[:, 2 * d_:2 * d_ + 1], pOd[:, 128:129], -1.0, op0=ALU.mult, waits=[tk_] + smf, inc=False)
                    t_m = P.stt(sm[:, 2 * d_:2 * d_ + 1], sm[:, 2 * d_:2 * d_ + 1], ECOL[:, d_, c, h:h + 1], pOd[:, 128:129], ALU.max, ALU.max)
                    rt.append(P.recip(sm[:, 2 * d_ + 1:2 * d_ + 2], sm[:, 2 * d_:2 * d_ + 1], waits=[t_m]))
                Hf, hff, hfk = Hfr.get()
                hh, hhf, hhk = hr.get()
                t_hf = P.ts("dve", Hf[:], pO0[:, 0:128], sm[:, 1:2], op0=ALU.mult, waits=[rt[0]] + hff)
                psO0.rel(pok0, [t_hf])
                t_h = P.stt(hh[:], pO1[:, 0:128], sm[:, 3:4], Hf[:], ALU.mult, ALU.add, waits=[rt[1]] + hhf)
                psO1.rel(pok1, [t_h])
                Hfr.rel(hfk, [t_h])
                jk, jkf, jkk = jkr.get()
                t_sq = P.act(jk[:], hh[:], AF.Square, waits=[t_h] + jkf)
                stB[c] = (sm, smk, hh, hhk, jk, jkk, t_sq)

            def stageC(c):
                tsl = slice(c * 128, (c + 1) * 128)
                sm, smk, hh, hhk, jk, jkk, t_sq = stB.pop(c)
                t_ss = P.op("dve", lambda e, o_=sm[:, 4:5], i_=jk[:]: e.tensor_reduce(out=o_, in_=i_, axis=mybir.AxisListType.X, op=ALU.add), [t_sq])
                t_ln = P.act(sm[:, 5:6], sm[:, 4:5], AF.Ln, bias=epsb[:, 0:1], scale=1.0 / 128, waits=[t_ss])
                t_r = P.act(sm[:, 5:6], sm[:, 5:6], AF.Exp, scale=-0.5, waits=[t_ln], force=True)
                jkr.rel(jkk, [t_ss])
                gs, gsf, gsk = gsr.get()
                t_gs = P.tt("pool", gs[:], SO[:, c, :], mlg[:, hs], ALU.mult, waits=[l_so, l_mlg] + gsf)
                hn, hnf, hnk = hnr.get()
                t_hn = P.stt(hn[:], hh[:], sm[:, 5:6], gs[:], ALU.mult, ALU.mult, waits=[t_r, t_gs] + hnf)
                hr.rel(hhk, [t_hn])
                gsr.rel(gsk, [t_hn])
                smr.rel(smk, [t_hn])
                pT, ptf, ptk = psT.get()
                t_T = P.transpose(pT[:], hn[:], idb[:], waits=[t_hn, t_idb] + ptf)
                hnr.rel(hnk, [t_T])
                t_ev = P.copy("act", OUTS[:, tsl], pT[:], waits=[t_T] + fo)
                psT.rel(ptk, [t_ev])
                return t_ev, t_gs

            for c in range(NT + 2):
                if c < NT:
                    rd = stageA(c)
                if 0 <= c - 1 < NT:
                    stageB(c - 1)
                if 0 <= c - 2 < NT:
                    t_ev, t_gs = stageC(c - 2)
            vc_free = rd
            td = P.dma(Dm["MIX_T"][hs, :], OUTS[:], waits=[t_ev])
            OUTr.rel(ko, [td])
            qTr.rel(k1, rd)
            kTr.rel(k2, rd)
            Ktr.rel(k3, rd)
            Var.rel(k4, rd)
            SOr.rel(k5, [t_gs])
        if "D_G" in Dm:
            for i_, tl_ in enumerate((CUM, COLF, ECOL, CARRY, LF, LI)):
                P.dma(Dm["D_G"][:, i_, :], tl_[:].rearrange("p d c h -> p (d c h)"), waits=[td])
            P.dma(Dm["D_VC"].rearrange("p (a n) -> p a n", n=129), Vc[:].rearrange("p d c n -> p (d c) n")[:, :, 0:129], waits=[td])
            P.dma(Dm["D_CB"].rearrange("p (a n) -> p a n", n=129), CB[:].rearrange("p d c n -> p (d c) n")[:, :, 0:129], waits=[td])
            for i_, rg in enumerate((Pfr, Pbr, hnr)):
                for j_ in range(2):
                    P.dma(Dm["D_B"][:, i_, j_, :], rg.bufs[j_][:], waits=[td])
            for i_, rg in enumerate((Hfr, hr, gsr, jkr)):
                for j_ in range(2):
                    P.dma(Dm["D_F"][:, i_, j_, :], rg.bufs[j_][:], waits=[td])
            for j_ in range(4):
                P.dma(Dm["D_S"][:, j_, 0:6], smr.bufs[j_][:, 0:6], waits=[td])
        P.emit()


def block_window(nc, P, Dm):
    with ExitStack() as st:
        A = Alloc(nc, st)
        C, tl = load_consts(nc, P, A, Dm, ["IDENT", "NEGP", "NEGN", "HMASK", "SINKP"])
        idb = A.sb([128, 128], BF16, "idb")
        negp = A.sb([128, 128], BF16, "negp")
        negn = A.sb([128, 128], BF16, "negn")
        esink = A.sb([128, 4], F32, "esink")
        P.copy("pool", idb[:], C["IDENT"][:], waits=tl, inc=False)
        P.copy("pool", negp[:], C["NEGP"][:], inc=False)
        t_c = P.copy("pool", negn[:], C["NEGN"][:])
        t_es = P.act(esink[:], C["SINKP"][:], AF.Exp, waits=tl)
        QT = A.sb([128, 4, T], BF16, "QT")
        KTd = A.sb([128, 2, T], BF16, "KTd")
        KTm = A.sb([128, 2, 2, T], BF16, "KTm")
        VBt = A.sb([128, NT, 128], BF16, "VBt")
        VAe = A.sb([128, 2, 2, NT, 128], BF16, "VAe")
        ONe = A.sb([128, 2, 128], BF16, "ONe")
        OB = A.sb([128, 4, T], BF16, "OB")
        l_q = [P.dma(QT[:, i, :], Dm["QB_T"][i * 128:(i + 1) * 128, :]) for i in range(4)]
        l_k = [P.dma(KTd[:, i, :], Dm["KB_T"][i * 128:(i + 1) * 128, :]) for i in range(2)]
        l_v = P.dma(VBt[:], Dm["VB"].rearrange("(c p) n -> p c n", p=128))
        prep = []
        for kv in range(2):
            for e in range(2):
                en = "pool" if e == 0 else "dve"
                prep.append(P.ts(en, KTm[:, kv, e, :], KTd[:, kv, :], C["HMASK"][:, e:e + 1], op0=ALU.mult, waits=l_k + tl))
        P.memset("pool", VAe[:].rearrange("p a b c d -> p (a b c d)"), 0.0, inc=False)
        P.memset("pool", ONe[:].rearrange("p a b -> p (a b)"), 0.0, inc=False)
        for e in range(2):
            P.memset("pool", ONe[:, e, e * 64:(e + 1) * 64], 1.0, inc=False)
            for kv in range(2):
                tk = P.copy("pool", VAe[:, kv, e, :, e * 64:(e + 1) * 64], VBt[:, :, kv * 64:(kv + 1) * 64], waits=[l_v])
        prep.append(tk)
        prep += [t_c, t_es]
        psL = A.ring(2, [128, 512], F32, "psL", psum=True)
        psX = A.ring(2, [128, 512], F32, "psX", psum=True)
        psO = A.ring(2, [128, 512], F32, "psO", psum=True)
        psD = A.ring(2, [128, 512], F32, "psD", psum=True)
        PLr = A.ring(3, [128, 384], BF16, "PL")
        PXr = A.ring(3, [128, 256], BF16, "PX")
        dsr = A.ring(2, [128, 128], F32, "ds")
        rr = A.ring(2, [128, 128], F32, "rr")

        items = []
        for n in range(NT):
            for i in range(4):
                for e in range(2):
                    items.append((n, i, e))

        def kbs_of(n):
            if n < 2:
                return [], [0, 1]
            L = [kb for kb in (n - 1, n, n + 1) if 2 <= kb < NT]
            return L, [0, 1]

        def emit_S(it):
            n, i, e = it
            kv = i // 2
            qs = slice(n * 128, (n + 1) * 128)
            L, X = kbs_of(n)
            pl = plk = None
            tL = None
            if L:
                pl, plf, plk = psL.get()
                for idx, kb in enumerate(L):
                    masked = kb != n
                    tL = P.mm(pl[:, idx * 128:(idx + 1) * 128], KTm[:, kv, e, kb * 128:(kb + 1) * 128], QT[:, i, qs], start=True, stop=not masked,
                              waits=l_q + prep + plf, inc=(not masked and idx == len(L) - 1))
                    if masked:
                        tL = P.mm(pl[:, idx * 128:(idx + 1) * 128], idb[:], (negp if kb == n - 1 else negn)[:], start=False, stop=True,
                                  inc=(idx == len(L) - 1))
            px, pxf, pxk = psX.get()
            for idx, kb in enumerate(X):
                tX = P.mm(px[:, idx * 128:(idx + 1) * 128], KTm[:, kv, e, kb * 128:(kb + 1) * 128], QT[:, i, qs], start=True, stop=True,
                          waits=l_q + prep + pxf, inc=(idx == 1))
            return dict(it=it, L=L, X=X, pl=pl, plk=plk, tL=tL, px=px, pxk=pxk, tX=tX)

        def emit_exp(S):
            if S["L"]:
                PL, f, k = PLr.get()
                nl = len(S["L"]) * 128
                S["tPL"] = P.act(PL[:, 0:nl], S["pl"][:, 0:nl], AF.Exp, scale=0.125, waits=[S["tL"]] + f)
                psL.rel(S["plk"], [S["tPL"]])
                S["PL"], S["PLk"] = PL, k
            PX, f, k = PXr.get()
            S["tPX"] = P.act(PX[:], S["px"][:, 0:256], AF.Exp, scale=0.125, waits=[S["tX"]] + f)
            psX.rel(S["pxk"], [S["tPX"]])
            S["PX"], S["PXk"] = PX, k

        cur = {}

        def emit_PV(S):
            n, i, e = S["it"]
            kv = i // 2
            if e == 0:
                po, pof, pok = psO.get()
                pd, pdf, pdk = psD.get()
                cur["po"], cur["pok"], cur["pof"] = po, pok, pof + pdf
                cur["pd"], cur["pdk"] = pd, pdk
            po = cur["po"]
            pd = cur["pd"]
            seq = [("L", idx, kb) for idx, kb in enumerate(S["L"])] + [("X", idx, kb) for idx, kb in enumerate(S["X"])]
            last = None
            for j, (reg, idx, kb) in enumerate(seq):
                src = (S["PL"] if reg == "L" else S["PX"])[:, idx * 128:(idx + 1) * 128]
                wt = [S["tPL"] if reg == "L" else S["tPX"]] + (cur["pof"] if (e == 0 and j == 0) else [])
                first = (e == 0 and j == 0)
                fin = (e == 1 and j == len(seq) - 1)
                P.mm(po[:, 0:128], VAe[:, kv, e, kb, :], src, start=first, stop=fin, waits=wt)
                last = P.mm(pd[:, 0:128], ONe[:, e, :], src, start=first, stop=fin, inc=(j == len(seq) - 1))
            if S["L"]:
                PLr.rel(S["PLk"], [last])
            PXr.rel(S["PXk"], [last])
            if e == 1:
                ds, f, k = dsr.get()
                t_ds = P.ts("dve", ds[:], pd[:, 0:128], esink[:, i:i + 1], op0=ALU.add, waits=[last] + f)
                psD.rel(cur["pdk"], [t_ds])
                r_, f2, k2 = rr.get()
                t_r = P.recip(r_[:], ds[:], waits=f2)
                t_o = P.tt("dve", OB[:, i, n * 128:(n + 1) * 128], po[:, 0:128], r_[:], ALU.mult, waits=[t_r])
                psO.rel(cur["pok"], [t_o])
                dsr.rel(k, [t_r])
                rr.rel(k2, [t_o])
                cur["last_o"] = t_o

        prev = None
        for it in items:
            S = emit_S(it)
            emit_exp(S)
            if prev is not None:
                emit_PV(prev)
            prev = S
        emit_PV(prev)
        for i in range(4):
            P.dma(Dm["MIX_T"][512 + i * 128:512 + (i + 1) * 128, :], OB[:, i, :], waits=[cur["last_o"]])
        P.emit()


def block_outmlp(nc, P, Dm, l):
    xsrc = Dm["XIN"] if l == 0 else Dm["X1"]
    mix = Dm["MIX_T"] if l == 0 else Dm["MIX1_T"]
    dst = Dm["X1"] if l == 0 else Dm["OUT_T"]
    if l == 0:
        groups = [[(0, 256, 1), (256, 512, 0), (768, 320, 0)]] + [[(g * 1088, 384, 0), (g * 1088 + 384, 384, 0), (g * 1088 + 768, 320, 0)] for g in range(1, 4)]
        mixoff, dstoff = 0, 0
    else:
        groups = [[(T_CTX + g * 1024, 512, 0), (T_CTX + g * 1024 + 512, 512, 0)] for g in range(4)]
        mixoff, dstoff = T_CTX, T_CTX
    GT = 1088
    with ExitStack() as st:
        A = Alloc(nc, st)
        modv = A.sb([128, 6, 8, 2], F32, "modv")
        ones = A.sb([128, 128], F32, "ones")
        epsb = A.sb([128, 1], F32, "eps")
        l_mod = P.dma(modv[:], Dm["MODV"][:, l])
        l_ones = P.dma(ones[:], Dm["ONES"])
        t_eps = P.memset("pool", epsb[:], EPS)
        lc = [l_mod, l_ones, t_eps]
        wout = A.sb([128, 8, 1024], BF16, "wout")
        wfr = A.ring(3, [128, 1024], F32, "wf")
        wbr = A.ring(3, [128, 1024], BF16, "wb")
        wo_t = []
        for kc in range(8):
            wf, f, k = wfr.get()
            ld = P.dma(wf[:], Dm["WOUT%d" % l][:, kc, :], waits=f)
            tk = P.copy("act", wout[:, kc, :], wf[:], waits=[ld])
            wfr.rel(k, [tk])
            wo_t.append(tk)
        xmid = A.sb([128, 8, GT], F32, "xmid")
        h2T = A.sb([128, 8, GT], BF16, "h2T")
        h1 = A.sb([128, 32, GT], BF16, "h1")
        mxr = A.ring(2, [128, 8, 512], BF16, "mx")
        sqr = A.ring(2, [128, 512], BF16, "sq")
        rsr = A.ring(3, [128, 512], F32, "rs")
        onesb = A.sb([128, 128], BF16, "onesb")
        t_ob = P.memset("pool", onesb[:], 1.0)
        tmr = A.ring(2, [128, 512], F32, "tm")
        rlr = A.ring(2, [128, 512], F32, "rl")
        psr = A.ring(4, [128, 512], F32, "ps", psum=True)
        pss = A.ring(2, [128, 512], F32, "pss", psum=True)
        xm_free = []
        h1_free = []
        h2_free = []
        srcs = []
        for grp in groups:
            srcs += [Dm["W1_%d" % l][f_] for f_ in range(32)]
            srcs += [Dm["W2_%d" % l][nch, q] for nch in range(8) for q in range(4)]
        wst = WStream(P, wfr, wbr, srcs, depth=2)
        def load_x(grp_, waits_by_nch):
            g0_ = grp_[0][0]
            nt_ = grp_[-1][0] + grp_[-1][1] - g0_
            return [P.dma(xmid[:, nch_, 0:nt_], xsrc[nch_ * 128:(nch_ + 1) * 128, g0_:g0_ + nt_], waits=waits_by_nch[nch_]) for nch_ in range(8)]

        def load_x_one(grp_, nch_, waits_):
            g0_ = grp_[0][0]
            nt_ = grp_[-1][0] + grp_[-1][1] - g0_
            return P.dma(xmid[:, nch_, 0:nt_], xsrc[nch_ * 128:(nch_ + 1) * 128, g0_:g0_ + nt_], waits=waits_, en="pool")

        ldx_n = load_x(groups[0], [[] for _ in range(8)])
        for gi_, grp in enumerate(groups):
            g0 = grp[0][0]
            h2_toks = []
            ldx_next = [None] * 8

            def a_stage1(t0, n, j):
                o = t0 - g0
                mx, mf, mk = mxr.get()
                ldm = P.dma(mx[:, :, 0:n], mix.rearrange("(kc p) t -> p kc t", p=128)[:, :, t0 - mixoff:t0 - mixoff + n], waits=mf)
                ps2, p2f, p2k = pss.get()
                for nch in range(8):
                    ps, pf, pk = psr.get()
                    for kc in range(8):
                        tmm = P.mm(ps[:, 0:n], wout[:, kc, nch * 128:(nch + 1) * 128], mx[:, kc, 0:n], start=(kc == 0), stop=(kc == 7),
                                   waits=[ldm] + wo_t + pf, inc=(kc == 7))
                    t_x = P.stt(xmid[:, nch, o:o + n], ps[:, 0:n], modv[:, 2, nch, j:j + 1], xmid[:, nch, o:o + n], ALU.mult, ALU.add, waits=[tmm, ldx_n[nch]] + lc)
                    psr.rel(pk, [t_x])
                    sq, sf, sk = sqr.get()
                    tq = P.act(sq[:, 0:n], xmid[:, nch, o:o + n], AF.Square, waits=[t_x] + sf)
                    tm2 = P.mm(ps2[:, 0:n], onesb[:], sq[:, 0:n], start=(nch == 0), stop=(nch == 7), waits=[tq, t_ob] + lc + p2f, inc=True)
                    sqr.rel(sk, [tm2])
                mxr.rel(mk, [tmm])
                rs, rf, rk = rsr.get()
                P.act(rs[:, 0:n], ps2[:, 0:n], AF.Ln, bias=epsb[:, 0:1], scale=1.0 / D, waits=[tm2] + rf, inc=False)
                t_rs = P.act(rs[:, 0:n], rs[:, 0:n], AF.Exp, scale=-0.5)
                pss.rel(p2k, [t_rs])
                return (o, n, j, rs, rk, t_rs)

            def a_stage2(S):
                o, n, j, rs, rk, t_rs = S
                toks = [None, None]
                for kc in range(8):
                    tm, tf, tk = tmr.get()
                    t1 = P.stt(tm[:, 0:n], xmid[:, kc, o:o + n], modv[:, 4, kc, j:j + 1], rs[:, 0:n], ALU.mult, ALU.mult, waits=[t_rs] + tf)
                    if kc < 2:
                        t2 = P.ts("pool", h2T[:, kc, o:o + n], tm[:, 0:n], modv[:, 3, kc, j:j + 1], op0=ALU.add, waits=[t1] + h2_free)
                    else:
                        t2 = P.act(h2T[:, kc, o:o + n], tm[:, 0:n], AF.Identity, bias=modv[:, 3, kc, j:j + 1], waits=[t1] + h2_free)
                    tmr.rel(tk, [t2])
                    toks[0 if kc < 2 else 1] = t2
                rsr.rel(rk, [t1])
                h2_toks.append(toks)

            prev_a = None
            for sgd in grp:
                cur_a = a_stage1(*sgd)
                if prev_a is not None:
                    a_stage2(prev_a)
                prev_a = cur_a
            a_stage2(prev_a)
            last1 = None
            for f_ in range(32):
                wb, tc_, bk = wst.next()
                for si, (t0, n, j) in enumerate(grp):
                    o = t0 - g0
                    ps, pf, pk = psr.get()
                    for kc in range(8):
                        tmm = P.mm(ps[:, 0:n], wb[:, kc * 128:(kc + 1) * 128], h2T[:, kc, o:o + n], start=(kc == 0), stop=(kc == 7),
                                   waits=[tc_, h2_toks[si]] + pf, inc=(kc == 7))
                    rl, rf, rk = rlr.get()
                    t_r = P.act(rl[:, 0:n], ps[:, 0:n], AF.Relu, waits=[tmm] + rf)
                    psr.rel(pk, [t_r])
                    en = "dve"
                    t_h1 = P.tt(en, h1[:, f_, o:o + n], rl[:, 0:n], rl[:, 0:n], ALU.mult, waits=[t_r] + h1_free)
                    rlr.rel(rk, [t_h1])
                    last1 = t_h1
                wbr.rel(bk, [tmm])
            h2_free = [tmm]
            for nch in range(8):
                st_toks = []
                pst = []
                for si, (t0, n, j) in enumerate(grp):
                    ps, pf, pk = psr.get()
                    pst.append((ps, pf, pk))
                for q in range(4):
                    wb, tc_, bk = wst.next()
                    for si, (t0, n, j) in enumerate(grp):
                        o = t0 - g0
                        ps, pf, pk = pst[si]
                        for fi in range(8):
                            f_ = q * 8 + fi
                            tmm = P.mm(ps[:, 0:n], wb[:, fi * 128:(fi + 1) * 128], h1[:, f_, o:o + n], start=(f_ == 0), stop=(f_ == 31),
                                       waits=[tc_, last1] + pf, inc=(fi == 7))
                    wbr.rel(bk, [tmm])
                for si, (t0, n, j) in enumerate(grp):
                    o = t0 - g0
                    ps, pf, pk = pst[si]
                    t_x = P.stt(xmid[:, nch, o:o + n], ps[:, 0:n], modv[:, 5, nch, j:j + 1], xmid[:, nch, o:o + n], ALU.mult, ALU.add, waits=[tmm])
                    psr.rel(pk, [t_x])
                    if l == 0 or j == 0:
                        st_toks.append(P.dma(dst[nch * 128:(nch + 1) * 128, t0 - dstoff:t0 - dstoff + n], xmid[:, nch, o:o + n], waits=[t_x], en="pool"))
                    else:
                        st_toks.append(t_x)
                if gi_ + 1 < len(groups):
                    ldx_next[nch] = load_x_one(groups[gi_ + 1], nch, st_toks)
            h1_free = [tmm]
            ldx_n = ldx_next
        P.emit()


def block_fullattn(nc, P, Dm):
    SHIFT = -4.0
    with ExitStack() as st:
        A = Alloc(nc, st)
        KT = A.sb([128, 2, T], BF16, "KT")
        V = A.sb([128, NT, 256], BF16, "V")
        onb = A.sb([128, 128], BF16, "onb")
        shb = A.sb([128, 1], F32, "shb")
        t_on = P.memset("pool", onb[:], 1.0)
        t_sh = P.memset("pool", shb[:], SHIFT)
        l_k = [P.dma(KT[:, i, :], Dm["KC_T"][i * 128:(i + 1) * 128, :]) for i in range(2)]
        l_v = P.dma(V[:], Dm["VC"].rearrange("(c p) n -> p c n", p=128))
        lc = l_k + [l_v, t_on, t_sh]
        qr = A.ring(2, [128, 512], BF16, "q")
        psS = A.ring(2, [128, 1024], F32, "psS", psum=True)
        psO = A.ring(2, [128, 512], F32, "psO", psum=True)
        psD = A.ring(2, [128, 512], F32, "psD", psum=True)
        PTr = A.ring(3, [128, 1024], BF16, "PT")
        rr = A.ring(2, [128, 512], F32, "r")
        obr = A.ring(2, [128, 512], BF16, "ob")
        scale = 128.0 ** -0.5
        onf = A.sb([128, 128], F32, "onf")
        t_onf = P.memset("pool", onf[:], 1.0)
        accr = A.ring(2, [128, 1024], F32, "acc")
        NP = NT // 2
        for h in range(8):
            kv = h // 4
            for g in range(8):
                q, qf, qk = qr.get()
                l_q = P.dma(q[:], Dm["QC_T"][h * 128:(h + 1) * 128, g * 512:(g + 1) * 512], waits=qf)
                po, pof, pok = psO.get()
                pd, pdf, pdk = psD.get()
                acc, accf, acck = accr.get()
                pend = None
                st8 = {"first_d": True, "first_a": True, "t_acc": None}

                def pv(pend, last):
                    PT, tk_, j_, ptk = pend
                    rel = []
                    for hf in range(2):
                        kb_ = 2 * j_ + hf
                        first = (kb_ == 0)
                        t_ = P.mm(po[:], V[:, kb_, kv * 128:(kv + 1) * 128], PT[:, hf * 512:(hf + 1) * 512], start=first, stop=(last and hf == 1),
                                  waits=[tk_] + (pof if first else []), inc=(hf == 1))
                    rel.append(t_)
                    if j_ % 3 == 0:
                        for hf in range(2):
                            t2_ = P.mm(pd[:], onb[:], PT[:, hf * 512:(hf + 1) * 512], start=st8["first_d"], stop=False,
                                       waits=(pdf if st8["first_d"] else []), inc=(hf == 1))
                            st8["first_d"] = False
                        rel.append(t2_)
                    else:
                        if st8["first_a"]:
                            t2_ = P.copy("dve", acc[:], PT[:], waits=[tk_] + accf)
                            st8["first_a"] = False
                        else:
                            t2_ = P.tt("dve", acc[:], acc[:], PT[:], ALU.add, waits=[tk_])
                        st8["t_acc"] = t2_
                        rel.append(t2_)
                    PTr.rel(ptk, rel)
                    return t_
                for j in range(NP):
                    ps, psf, psk = psS.get()
                    for hf in range(2):
                        kb = 2 * j + hf
                        t_s = P.mm(ps[:, hf * 512:(hf + 1) * 512], KT[:, kv, kb * 128:(kb + 1) * 128], q[:], waits=[l_q] + lc + psf, inc=(hf == 1))
                    PT, ptf, ptk = PTr.get()
                    t_e = P.act(PT[:], ps[:], AF.Exp, bias=shb[:, 0:1], scale=scale, waits=[t_s] + ptf)
                    psS.rel(psk, [t_e])
                    if pend is not None:
                        pv(pend, False)
                    pend = (PT, t_e, j, ptk)
                pv(pend, True)
                P.mm(pd[:], onf[:], acc[:, 0:512], start=False, stop=False, waits=[st8["t_acc"], t_onf])
                t_last = P.mm(pd[:], onf[:], acc[:, 512:1024], start=False, stop=True, inc=True)
                accr.rel(acck, [t_last])
                qr.rel(qk, [t_last])
                r_, rf, rk = rr.get()
                t_r = P.recip(r_[:], pd[:], waits=[t_last] + rf)
                psD.rel(pdk, [t_r])
                ob, of, ok = obr.get()
                t_o = P.tt("dve", ob[:], po[:], r_[:], ALU.mult, waits=[t_r] + of)
                psO.rel(pok, [t_o])
                rr.rel(rk, [t_o])
                td = P.dma(Dm["MIX1_T"][h * 128:(h + 1) * 128, g * 512:(g + 1) * 512], ob[:], waits=[t_o])
                obr.rel(ok, [td])
        P.emit()


IN_SPECS = {
    "XIN": ([1024, T], F32), "CV": ([128, 8, 2], F32), "ADAW": ([2, 1024, 6144], F32), "ADABR": ([2, 6144], F32),
    "NG": ([128, 2, 2, 8], F32),
    "WF0": ([14, 128, 1024], F32), "WT0": ([128, 8, 1680], F32), "GATEB": ([1, 16], F32), "MLNG": ([1, 512], F32),
    "QKG0": ([128, 4], F32), "SINKP": ([128, 4], F32), "COS0": ([128, T], F32), "SIN0": ([128, T], F32), "BLK0": ([128, 128], F32),
    "WOUT0": ([128, 8, 1024], F32), "W1_0": ([32, 128, 1024], F32), "W2_0": ([8, 4, 128, 1024], F32),
    "WF1": ([10, 128, 1024], F32), "WT1": ([128, 8, 256], F32), "QKG1": ([128, 4], F32), "COS1": ([128, T], F32), "SIN1": ([128, T], F32),
    "BLK1": ([128, 128], F32), "WOUT1": ([128, 8, 1024], F32), "W1_1": ([32, 128, 1024], F32), "W2_1": ([8, 4, 128, 1024], F32),
    "ONES": ([128, 128], F32), "IDENT": ([128, 128], F32), "UTRI": ([128, 128], F32), "LTRI": ([128, 128], F32),
    "NEGP": ([128, 128], F32), "NEGN": ([128, 128], F32), "HMASK": ([128, 2], F32),
}
SCRATCH = {
    "MODV": ([128, 2, 6, 8, 2], F32),
    "QA_T": ([512, T], BF16), "KA_T": ([512, T], BF16), "KA": ([T, 512], BF16), "VA": ([T, 512], BF16), "SOA": ([T, 512], F32),
    "GA": ([T, 16], F32), "VB": ([T, 128], BF16), "QB_T": ([512, T], BF16), "KB_T": ([256, T], BF16),
    "MIX_T": ([1024, T], BF16), "X1": ([1024, T], F32),
    "QC_T": ([1024, T_LAT], BF16), "KC_T": ([256, T], BF16), "VC": ([T, 256], BF16), "MIX1_T": ([1024, T_LAT], BF16),
}
STAGE_INPUTS = {
    "mod": ["CV", "ADAW", "ADABR", "NG", "IDENT"],
    "in0": ["XIN", "ONES", "WF0", "WT0", "GATEB", "QKG0", "COS0", "SIN0", "BLK0"],
    "mlstm": ["ONES", "UTRI", "LTRI", "IDENT", "MLNG"],
    "win": ["IDENT", "NEGP", "NEGN", "HMASK", "SINKP"],
    "mlp0": ["XIN", "ONES", "WOUT0", "W1_0", "W2_0"],
    "in1": ["ONES", "WF1", "WT1", "QKG1", "COS1", "SIN1", "BLK1"],
    "attn": [],
    "mlp1": ["ONES", "WOUT1", "W1_1", "W2_1"],
}
ALL_STAGES = ("mod", "in0", "mlstm", "win", "mlp0", "in1", "attn", "mlp1")


def build(stages=ALL_STAGES, dbg=()):
    nc = bass.Bass("TRN2", target_bir_lowering=False)
    Dm = {}
    need = set()
    for sname in stages:
        need |= set(STAGE_INPUTS[sname])
    for k, (shape, dt) in IN_SPECS.items():
        if k in need:
            Dm[k] = nc.dram_tensor(k, shape, dt, kind="ExternalInput").ap()
    for k, (shape, dt) in SCRATCH.items():
        kind = "ExternalOutput" if k in dbg else ("ExternalInput" if ("in:" + k) in dbg else "Internal")
        Dm[k] = nc.dram_tensor(k, shape, dt, kind=kind).ap()
    Dm["OUT_T"] = nc.dram_tensor("OUT_T", [1024, T_LAT], F32, kind="ExternalOutput").ap()
    if "D_G" in dbg:
        Dm["D_G"] = nc.dram_tensor("D_G", [128, 6, 272], F32, kind="ExternalOutput").ap()
        Dm["D_B"] = nc.dram_tensor("D_B", [128, 3, 2, 128], BF16, kind="ExternalOutput").ap()
        Dm["D_F"] = nc.dram_tensor("D_F", [128, 4, 2, 128], F32, kind="ExternalOutput").ap()
        Dm["D_S"] = nc.dram_tensor("D_S", [128, 4, 8], F32, kind="ExternalOutput").ap()
        Dm["D_VC"] = nc.dram_tensor("D_VC", [128, 2 * NT * 129], BF16, kind="ExternalOutput").ap()
        Dm["D_CB"] = nc.dram_tensor("D_CB", [128, 2 * NT * 129], BF16, kind="ExternalOutput").ap()
    with ExitStack() as st:
        P = Prog(nc, st)
        if "mod" in stages:
            with nc.named_scope("mod"):
                block_mod(nc, P, Dm)
        if "in0" in stages:
            with nc.named_scope("in0"):
                block_inproj(nc, P, Dm, 0)
        if "mlstm" in stages:
            with nc.named_scope("mlstm"):
                block_mlstm(nc, P, Dm)
        if "win" in stages:
            with nc.named_scope("win"):
                block_window(nc, P, Dm)
        if "mlp0" in stages:
            with nc.named_scope("mlp0"):
                block_outmlp(nc, P, Dm, 0)
        if "in1" in stages:
            with nc.named_scope("in1"):
                block_inproj(nc, P, Dm, 1)
        if "attn" in stages:
            with nc.named_scope("attn"):
                block_fullattn(nc, P, Dm)
        if "mlp1" in stages:
            with nc.named_scope("mlp1"):
                block_outmlp(nc, P, Dm, 1)
    return nc


def _fm(w, cols):
    sub = w[:, cols]
    return np.ascontiguousarray(sub.reshape(8, 128, 128).transpose(1, 0, 2).reshape(128, 1024))


def _tm(w, cols):
    sub = w[:, cols]
    return np.ascontiguousarray(sub.reshape(8, 128, len(cols)).transpose(1, 0, 2))


def _rope_tables(dh, quarter_layout):
    pairs = dh // 4
    inv = (np.float32(10000.0) ** (-np.arange(pairs, dtype=np.float32) / np.float32(pairs))).astype(np.float32)
    tt = np.arange(T_LAT)
    row = (tt // 64).astype(np.float32)
    col = (tt % 64).astype(np.float32)
    ang = np.concatenate([row[:, None] * inv[None, :], col[:, None] * inv[None, :]], axis=1).astype(np.float32)
    half = dh // 2
    cos = np.ones((128, T), np.float32)
    sin = np.zeros((128, T), np.float32)
    for p in range(128):
        j = p % half
        sgn = -1.0 if p < 64 else 1.0
        cos[p, T_CTX:] = np.cos(ang[:, j])
        sin[p, T_CTX:] = sgn * np.sin(ang[:, j])
    return cos, sin


def host_shared(inp):
    f = lambda a: np.ascontiguousarray(np.asarray(a, dtype=np.float32))
    S = {}
    S["ADAW"] = f(inp["ada_w"])
    S["ADABR"] = f(inp["ada_b"])
    ng = np.stack([inp["norm1_g"].reshape(2, 8, 128), inp["norm2_g"].reshape(2, 8, 128)], axis=1)
    S["NG"] = f(ng.transpose(3, 0, 1, 2))
    W = np.asarray(inp["ab_w_in"][0], np.float32)
    chunks = [np.arange(i * 128, (i + 1) * 128) for i in range(4)] + [512 + np.arange(i * 128, (i + 1) * 128) for i in range(4)]
    p = np.arange(128)
    quarter, r = p // 32, p % 32
    dimp = (quarter // 2) * 32 + r
    for i in range(4):
        chunks.append(2064 + (2 * i + quarter % 2) * 64 + dimp)
    for kv in range(2):
        chunks.append(2576 + kv * 64 + dimp)
    S["WF0"] = np.stack([_fm(W, c) for c in chunks])
    tcols = np.concatenate([np.arange(512, 1024), np.arange(1024, 1536), np.arange(1536, 2048), np.arange(2048, 2064), np.arange(2704, 2832)])
    S["WT0"] = _tm(W, tcols)
    S["GATEB"] = f(inp["ab_gate_b"][0][None, :])
    S["MLNG"] = f(inp["mlstm_norm_g"][0][None, :])
    gq = np.asarray(inp["swa_q_norm_g"][0], np.float32)[dimp]
    gk = np.asarray(inp["swa_k_norm_g"][0], np.float32)[dimp]
    sh = (p + 64) % 128
    S["QKG0"] = f(np.stack([gq, gq[sh], gk, gk[sh]], axis=1))
    sink = np.asarray(inp["swa_sink"][0], np.float32)
    S["SINKP"] = f(np.stack([sink[2 * i + (p >= 64)] for i in range(4)], axis=1))
    S["COS0"], S["SIN0"] = _rope_tables(64, True)
    S["BLK0"] = f(((p[:, None] // 32) % 2 == (p[None, :] // 32) % 2))
    S["WOUT0"] = _tm(np.asarray(inp["ab_w_out"][0], np.float32), np.arange(1024))
    W1 = np.asarray(inp["c_w_in"][0], np.float32)
    S["WF1"] = np.stack([_fm(W1, np.arange(i * 128, (i + 1) * 128)) for i in range(10)])
    S["WT1"] = _tm(W1, np.arange(1280, 1536))
    gq1 = np.asarray(inp["c_q_norm_g"][0], np.float32)
    gk1 = np.asarray(inp["c_k_norm_g"][0], np.float32)
    S["QKG1"] = f(np.stack([gq1, gq1[sh], gk1, gk1[sh]], axis=1))
    S["COS1"], S["SIN1"] = _rope_tables(128, False)
    S["BLK1"] = np.ones((128, 128), np.float32)
    S["WOUT1"] = _tm(np.asarray(inp["c_w_out"][0], np.float32), np.arange(1024))
    for l in range(2):
        w1 = np.asarray(inp["mlp_w1"][l], np.float32)
        S["W1_%d" % l] = np.ascontiguousarray(w1.reshape(8, 128, 32, 128).transpose(2, 1, 0, 3).reshape(32, 128, 1024))
        w2 = np.asarray(inp["mlp_w2"][l], np.float32)
        S["W2_%d" % l] = np.ascontiguousarray(w2.reshape(4, 8, 128, 8, 128).transpose(3, 0, 2, 1, 4).reshape(8, 4, 128, 1024))
    S["ONES"] = np.ones((128, 128), np.float32)
    S["IDENT"] = np.eye(128, dtype=np.float32)
    S["UTRI"] = np.triu(np.ones((128, 128), np.float32))
    S["LTRI"] = np.tril(np.ones((128, 128), np.float32))
    s_, t_ = p[:, None], p[None, :]
    S["NEGP"] = np.where(t_ <= s_, 0.0, NEG).astype(np.float32)
    S["NEGN"] = np.where(s_ <= t_, 0.0, NEG).astype(np.float32)
    S["HMASK"] = f(np.stack([(p // 32) % 2 == 0, (p // 32) % 2 == 1], axis=1))
    return S


def host_core(inp, b):
    x = np.asarray(inp["x"][b], np.float32)
    ctx = np.asarray(inp["ctx"][b], np.float32)
    m = {"XIN": np.ascontiguousarray(np.concatenate([ctx.T, x.T], axis=1))}
    cv = np.stack([np.asarray(inp["c"][b], np.float32).reshape(8, 128), np.asarray(inp["c_ctx"], np.float32).reshape(8, 128)], axis=-1)
    m["CV"] = np.ascontiguousarray(cv.transpose(1, 0, 2))
    return m


_NC_CACHE = {}


def kernel(**inputs):
    S = host_shared(inputs)
    in_maps = []
    for b in range(8):
        m = dict(S)
        m.update(host_core(inputs, b))
        in_maps.append(m)
    if "nc" not in _NC_CACHE:
        _NC_CACHE["nc"] = build()
    res = run_bass_kernel_spmd(_NC_CACHE["nc"], in_maps, core_ids=list(range(8)))
    out = np.stack([np.ascontiguousarray(res.results[b]["OUT_T"].T) for b in range(8)], axis=0)
    return out.astype(np.float32)
```

```python
import numpy as np
from contextlib import ExitStack
import concourse.bass as bass
import concourse.mybir as mybir
from concourse.bass_utils import run_bass_kernel_spmd
import ml_dtypes

F32 = mybir.dt.float32
BF16 = mybir.dt.bfloat16
AF = mybir.ActivationFunctionType
ALU = mybir.AluOpType

D = 1024
T_LAT = 4096
T_CTX = 256
T = T_LAT + T_CTX
NT = T // 128
EPS = 1e-6
NEG = -30000.0
DEBUG_OUT = False


class Prog:
    ENG = ["pe", "act", "dve", "pool", "sp"]
    NSLOT = 16
    NEPOCH = 6
    EPOCH_LEN = 16000

    def __init__(self, nc, st):
        self.nc = nc
        self.ops = {e: [] for e in self.ENG}
        self.cnt = {e: 0 for e in self.ENG}
        self.epoch = {e: 0 for e in self.ENG}
        self.waited = {e: {} for e in self.ENG}
        self.ndma = 0
        self.dma_cnt = {}
        self.rcnt = 0
        self.sems = {}
        for e in self.ENG:
            for ep in range(self.NEPOCH):
                self.sems[(e, ep)] = st.enter_context(nc.semaphore("s_%s_%d" % (e, ep)))
        for s in range(self.NSLOT):
            self.sems[("dma", s)] = st.enter_context(nc.semaphore("s_dma_%d" % s))
            self.sems[("dmp", s)] = st.enter_context(nc.semaphore("s_dmp_%d" % s))
        self.ndma_p = 0
        self.sems[("recip", 0)] = st.enter_context(nc.semaphore("s_recip"))
        self.all_dma = []
        self.pending = []
        self.pending_toks = set()

    def _waits(self, en, waits, force=False):
        ws = []
        for tok in waits:
            if tok is None:
                continue
            if isinstance(tok, list):
                ws += self._waits(en, tok, force)
                continue
            key, val = tok
            if key[0] == en and not force:
                continue
            if self.waited[en].get(key, 0) >= val:
                continue
            self.waited[en][key] = val
            ws.append((key, val))
        return ws

    def op(self, en, fn, waits=(), inc=True, recip=False, force=False):
        ws = self._waits(en, waits, force=(recip or force))
        tok = None
        if recip:
            self.rcnt += 1
            tok = (("recip", 0), self.rcnt)
        elif inc:
            if self.cnt[en] >= self.EPOCH_LEN:
                self.epoch[en] += 1
                self.cnt[en] = 0
            self.cnt[en] += 1
            tok = ((en, self.epoch[en]), self.cnt[en])
        self.ops[en].append((ws, fn, tok))
        return tok

    def dma(self, out, in_, waits=(), en="sp", store=None, **kw):
        if store is None:
            store = "dram" in type(out.tensor).__name__.lower()
        if en == "pool":
            slot = self.ndma_p % self.NSLOT
            self.ndma_p += 1
            key = ("dmp", slot)
            prev = self.dma_cnt.get(key, 0)
            w = [t for t in self._flat(waits)]
            if prev > 0:
                w.append((key, prev))
            self.dma_cnt[key] = prev + 16
            tok = (key, prev + 16)
            self.all_dma.append(tok)
            self.ops["pool"].append((self._waits("pool", w), ("dma", out, in_, kw), tok))
            return tok
        slot = self.ndma % self.NSLOT
        self.ndma += 1
        key = ("dma", slot)
        prev = self.dma_cnt.get(key, 0)
        w = [t for t in self._flat(waits)]
        if prev > 0:
            w.append((key, prev))
        self.dma_cnt[key] = prev + 16
        tok = (key, prev + 16)
        self.all_dma.append(tok)
        rec = (w, ("dma", out, in_, kw), tok)
        if store:
            self.pending.append(rec)
            self.pending_toks.add(tok)
        else:
            if any(t in self.pending_toks for t in w):
                self.flush_stores()
            self._push_dma(en, rec)
            self.flush_stores()
        return tok

    def _flat(self, waits):
        for t in waits:
            if t is None:
                continue
            if isinstance(t, list):
                yield from self._flat(t)
            else:
                yield t

    def _push_dma(self, en, rec):
        w, fn, tok = rec
        self.ops[en].append((self._waits(en, w), fn, tok))

    def flush_stores(self):
        for rec in self.pending:
            self._push_dma("sp", rec)
        self.pending = []
        self.pending_toks = set()

    def mm(self, out, lhsT, rhs, start=True, stop=True, waits=(), inc=False):
        return self.op("pe", lambda e: e.matmul(out, lhsT=lhsT, rhs=rhs, start=start, stop=stop), waits, inc)

    def transpose(self, out, in_, ident, waits=(), inc=True):
        return self.op("pe", lambda e: e.transpose(out=out, in_=in_, identity=ident), waits, inc)

    def act(self, out, in_, func, bias=None, scale=None, accum_out=None, waits=(), inc=True, force=False):
        kw = {}
        if bias is not None:
            kw["bias"] = bias
        if scale is not None:
            kw["scale"] = scale
        if accum_out is not None:
            kw["accum_out"] = accum_out
        return self.op("act", lambda e: e.activation(out=out, in_=in_, func=func, **kw), waits, inc, force=force)

    def ts(self, en, out, in0, s1, s2=None, op0=ALU.mult, op1=None, waits=(), inc=True):
        if op1 is None:
            return self.op(en, lambda e: e.tensor_scalar(out=out, in0=in0, scalar1=s1, scalar2=None, op0=op0), waits, inc)
        return self.op(en, lambda e: e.tensor_scalar(out=out, in0=in0, scalar1=s1, scalar2=s2, op0=op0, op1=op1), waits, inc)

    def tt(self, en, out, in0, in1, op, waits=(), inc=True):
        return self.op(en, lambda e: e.tensor_tensor(out=out, in0=in0, in1=in1, op=op), waits, inc)

    def stt(self, out, in0, scalar, in1, op0, op1, waits=(), inc=True):
        return self.op("dve", lambda e: e.scalar_tensor_tensor(out=out, in0=in0, scalar=scalar, in1=in1, op0=op0, op1=op1), waits, inc)

    def copy(self, en, out, in_, waits=(), inc=True):
        if en == "act":
            return self.act(out, in_, AF.Copy, waits=waits, inc=inc)
        return self.op(en, lambda e: e.tensor_copy(out=out, in_=in_), waits, inc)

    def memset(self, en, ap, val, waits=(), inc=True):
        return self.op(en, lambda e: e.memset(ap, val), waits, inc)

    def recip(self, out, in_, waits=()):
        return self.op("dve", lambda e: e.reciprocal(out=out, in_=in_), waits, recip=True)

    def emit(self, extra_final=()):
        nc = self.nc
        self.flush_stores()
        finals = list(self.all_dma) + [t for t in extra_final if t is not None]
        self.all_dma = []
        fin = {en: self._waits(en, finals) for en in self.ENG}
        ops = self.ops
        self.ops = {e: [] for e in self.ENG}
        sems = self.sems
        with nc.Block() as block:
            def run(en, eng):
                for ws, fn, tok in ops[en]:
                    for key, val in ws:
                        eng.wait_ge(sems[key], val)
                    if isinstance(fn, tuple):
                        _, out, in_, kw = fn
                        ins = eng.dma_start(out=out, in_=in_, **kw)
                        ins.then_inc(sems[tok[0]], 16)
                    else:
                        ins = fn(eng)
                        if tok is not None:
                            ins.then_inc(sems[tok[0]], 1)
                for key, val in fin[en]:
                    eng.wait_ge(sems[key], val)

            @block.tensor
            def _(e):
                run("pe", e)

            @block.scalar
            def _(e):
                run("act", e)

            @block.vector
            def _(e):
                run("dve", e)

            @block.gpsimd
            def _(e):
                run("pool", e)

            @block.sync
            def _(e):
                run("sp", e)


class Slot:
    def __init__(self, buf):
        self.buf = buf
        self.free = []


class Ring:
    def __init__(self, bufs):
        self.slots = [b if isinstance(b, Slot) else Slot(b) for b in bufs]
        self.i = 0

    @property
    def bufs(self):
        return [s_.buf for s_ in self.slots]

    def get(self):
        k = self.i % len(self.slots)
        self.i += 1
        return self.slots[k].buf, self.slots[k].free, k

    def rel(self, k, toks):
        self.slots[k].free = [t for t in toks if t is not None]


class Alloc:
    def __init__(self, nc, st):
        self.nc, self.st, self.n = nc, st, 0

    def sb(self, shape, dt, name=None):
        self.n += 1
        return self.st.enter_context(self.nc.sbuf_tensor("%s_%d_%d" % (name or "t", id(self) % 9973, self.n), list(shape), dt))

    def ps(self, shape=(128, 512), dt=F32, name=None):
        self.n += 1
        return self.st.enter_context(self.nc.psum_tensor("%s_%d_%d" % (name or "p", id(self) % 9973, self.n), list(shape), dt))

    def ring(self, n, shape, dt, name=None, psum=False):
        return Ring([(self.ps(shape, dt, name) if psum else self.sb(shape, dt, name)) for _ in range(n)])


class WStream:
    def __init__(self, P, wfr, wbr, srcs, depth=2, eng="act"):
        self.P, self.wfr, self.wbr, self.srcs, self.depth, self.eng = P, wfr, wbr, srcs, depth, eng
        self.q = []
        self.i = 0

    def _issue(self):
        P = self.P
        wf, ff, fk = self.wfr.get()
        ld = P.dma(wf[:], self.srcs[self.i], waits=ff)
        wb, bf_, bk = self.wbr.get()
        tc = P.copy(self.eng, wb[:], wf[:], waits=[ld] + bf_)
        self.wfr.rel(fk, [tc])
        self.q.append((wb, tc, bk))
        self.i += 1

    def next(self):
        while len(self.q) < self.depth + 1 and self.i < len(self.srcs):
            self._issue()
        return self.q.pop(0)


def subgroups(t0, t1, maxn=512):
    out = []
    t = t0
    while t < t1:
        n = min(maxn, t1 - t)
        out.append((t, n))
        t += n
    return out


def block_mod(nc, P, Dm):
    with ExitStack() as st:
        A = Alloc(nc, st)
        cv = A.sb([128, 8, 2], F32)
        sg = A.sb([128, 8, 2], F32)
        sc = A.sb([128, 8, 2], F32)
        adabr = A.sb([2, 2, 6144], F32)
        ng = A.sb([128, 2, 2, 8], F32)
        modrow = A.sb([2, 2, 6144], F32)
        ident = A.sb([128, 128], F32)
        modsb = A.sb([128, 2, 48, 2], F32)
        modv = A.sb([128, 2, 6, 8, 2], F32)
        awr = A.ring(2, [128, 8, 1536], F32, "aw")
        psr = A.ring(2, [128, 512], F32, "psmr", psum=True)
        pst = A.ps([128, 2, 48, 2], F32)
        l_cv = P.dma(cv[:], Dm["CV"])
        l_ab = [P.dma(adabr[:, l, :], Dm["ADABR"][l:l + 1, :].partition_broadcast(2)) for l in range(2)]
        l_ng = P.dma(ng[:], Dm["NG"])
        l_id = P.dma(ident[:], Dm["IDENT"])
        t_sc = P.tt("dve", sc[:], cv[:], sg[:], ALU.mult, waits=[P.act(sg[:], cv[:], AF.Sigmoid, waits=[l_cv])])
        t_ev = None
        for l in range(2):
            for q in range(4):
                aw, wfree, k = awr.get()
                src = Dm["ADAW"][l].rearrange("(kc p) n -> p kc n", p=128)
                ld = [P.dma(aw[:, 0:4, :], src[:, 0:4, q * 1536:(q + 1) * 1536], waits=wfree),
                      P.dma(aw[:, 4:8, :], src[:, 4:8, q * 1536:(q + 1) * 1536], waits=wfree)]
                for j in range(3):
                    ps, pf, pk = psr.get()
                    for kc in range(8):
                        tmm = P.mm(ps[0:2, :], sc[:, kc, :], aw[:, kc, j * 512:(j + 1) * 512], start=(kc == 0), stop=(kc == 7),
                                   waits=ld + [t_sc] + pf, inc=(kc == 7))
                    c0 = q * 1536 + j * 512
                    t_ev = P.tt("dve", modrow[:, l, c0:c0 + 512], ps[0:2, :], adabr[:, l, c0:c0 + 512], ALU.add, waits=[tmm] + l_ab)
                    psr.rel(pk, [t_ev])
                awr.rel(k, [tmm])
        t_T = None
        for l in range(2):
            for ch in range(48):
                t_T = P.transpose(pst[:, l, ch, :], modrow[:, l, ch * 128:(ch + 1) * 128], ident[0:2, 0:2], waits=[t_ev, l_id], inc=(l == 1 and ch == 47))
        t_ms = P.copy("dve", modsb[:].rearrange("p l c j -> p (l c j)"), pst[:].rearrange("p l c j -> p (l c j)"), waits=[t_T])
        for l in range(2):
            for j in range(2):
                for (kind, c0) in ((0, 0), (2, 16), (3, 24), (5, 40)):
                    P.copy("dve", modv[:, l, kind, :, j], modsb[:, l, c0:c0 + 8, j], inc=False)
                P.stt(modv[:, l, 1, :, j], modsb[:, l, 8:16, j], 1.0, ng[:, l, 0, :], ALU.add, ALU.mult, waits=[l_ng], inc=False)
                tk = P.stt(modv[:, l, 4, :, j], modsb[:, l, 32:40, j], 1.0, ng[:, l, 1, :], ALU.add, ALU.mult)
        P.dma(Dm["MODV"], modv[:], waits=[tk])
        P.emit()


def norm_modulate(nc, P, xsrc, hT, modv, kind_shift, kind_gs, sgs, consts, l_consts, col0=0):
    with ExitStack() as st:
        A = Alloc(nc, st)
        xr = A.ring(3, [128, 8, 512], F32, "x")
        sqr = A.ring(3, [128, 512], BF16, "sq")
        rsr = A.ring(3, [128, 512], F32, "rs")
        tmr = A.ring(3, [128, 512], F32, "tm")
        pss = A.ring(2, [128, 512], F32, "pss", psum=True)
        onesb = A.sb([128, 128], BF16, "onesb")
        t_ob = P.memset("pool", onesb[:], 1.0)
        epsb = consts["eps"]
        xs = xsrc.rearrange("(kc p) t -> p kc t", p=128)
        last = [None, None]

        def stage1(t0, n, j):
            xt, xfree, xk = xr.get()
            ld = P.dma(xt[:, :, 0:n], xs[:, :, t0:t0 + n], waits=xfree)
            ps, pfree, pk = pss.get()
            for kc in range(8):
                sq, sfree, sk = sqr.get()
                tq = P.act(sq[:, 0:n], xt[:, kc, 0:n], AF.Square, waits=[ld] + sfree)
                tm_ = P.mm(ps[:, 0:n], onesb[:], sq[:, 0:n], start=(kc == 0), stop=(kc == 7), waits=[tq, t_ob, l_consts] + pfree, inc=True)
                sqr.rel(sk, [tm_])
            rs, rfree, rk = rsr.get()
            P.act(rs[:, 0:n], ps[:, 0:n], AF.Ln, bias=epsb[:, 0:1], scale=1.0 / D, waits=[tm_] + rfree, inc=False)
            t_rs = P.act(rs[:, 0:n], rs[:, 0:n], AF.Exp, scale=-0.5)
            pss.rel(pk, [t_rs])
            return (t0, n, j, xt, xk, ld, rs, rk, t_rs)

        def stage2(S):
            t0, n, j, xt, xk, ld, rs, rk, t_rs = S
            for kc in range(8):
                tm, tfree, tk = tmr.get()
                t1 = P.stt(tm[:, 0:n], xt[:, kc, 0:n], modv[:, kind_gs, kc, j:j + 1], rs[:, 0:n], ALU.mult, ALU.mult, waits=[t_rs, ld] + tfree)
                if kc < 2:
                    t2 = P.ts("pool", hT[:, kc, t0 - col0:t0 - col0 + n], tm[:, 0:n], modv[:, kind_shift, kc, j:j + 1], op0=ALU.add, waits=[t1])
                else:
                    t2 = P.act(hT[:, kc, t0 - col0:t0 - col0 + n], tm[:, 0:n], AF.Identity, bias=modv[:, kind_shift, kc, j:j + 1], waits=[t1])
                tmr.rel(tk, [t2])
                last[0 if kc < 2 else 1] = t2
            xr.rel(xk, [t1])
            rsr.rel(rk, [t1])

        prev = None
        for sg in sgs:
            cur = stage1(*sg)
            if prev is not None:
                stage2(prev)
            prev = cur
        stage2(prev)
        P.emit()
    return last


def load_consts(nc, P, A, Dm, names):
    out, toks = {}, []
    for nm in names:
        src = Dm[nm]
        t = A.sb(list(src.shape), F32, nm)
        toks.append(P.dma(t[:], src))
        out[nm] = t
    return out, toks


IN_PARTS = ("tm", "copy", "rope")
ROPE_LVL = 4


def block_inproj(nc, P, Dm, l):
    xsrc = Dm["XIN"] if l == 0 else Dm["X1"]
    with ExitStack() as st:
        A = Alloc(nc, st)
        hT = A.sb([128, 8, T], BF16, "hT")
        ntm = 1680 if l == 0 else 256
        WT = A.sb([128, 8, ntm], BF16, "WT")
        modv = A.sb([128, 6, 8, 2], F32, "modv")
        ones = A.sb([128, 128], F32, "ones")
        epsb = A.sb([128, 1], F32, "eps")
        l_mod = P.dma(modv[:], Dm["MODV"][:, l])
        l_ones = P.dma(ones[:], Dm["ONES"])
        t_eps = P.memset("pool", epsb[:], EPS)
        consts = {"ones": ones, "eps": epsb}
        with ExitStack() as st2:
            A2 = Alloc(nc, st2)
            wsr = A2.ring(2, [128, ntm], F32, "wts")
            wt_toks = []
            for kc in range(8):
                ws, wfree, wk = wsr.get()
                ld = P.dma(ws[:], Dm["WT%d" % l][:, kc, :], waits=wfree)
                tk = P.copy("act", WT[:, kc, :], ws[:], waits=[ld])
                wsr.rel(wk, [tk])
                wt_toks.append(tk)
            sgs = [(0, 256, 1)] + [(t0, n, 0) for (t0, n) in subgroups(T_CTX, T)]
            t_h = norm_modulate(nc, P, xsrc, hT, modv, 0, 1, sgs, consts, [l_mod, l_ones, t_eps])
        with ExitStack() as st3:
            A3 = Alloc(nc, st3)
            dh = 64 if l == 0 else 128
            cosT = A3.sb([128, T], F32, "cos")
            sinT = A3.sb([128, T], F32, "sin")
            blk = A3.sb([128, 128], F32, "blk")
            gq = A3.sb([128, 4], F32, "gq")
            l_c = [P.dma(cosT[:], Dm["COS%d" % l]), P.dma(sinT[:], Dm["SIN%d" % l]), P.dma(blk[:], Dm["BLK%d" % l]),
                   P.dma(gq[:], Dm["QKG%d" % l])]
            wfr = A3.ring(3, [128, 1024], F32, "wf")
            wbr = A3.ring(3, [128, 1024], BF16, "wb")
            psr = A3.ring(3, [128, 512], F32, "psp", psum=True)
            ps2r = A3.ring(2, [128, 512], F32, "ps2", psum=True)
            stg = A3.ring(3, [128, 512], BF16, "stg")
            sqr = A3.ring(2, [128, 512], BF16, "sq")
            blkb = A3.sb([128, 128], BF16, "blkb")
            l_c.append(P.copy("pool", blkb[:], blk[:], waits=l_c))
            gqn = A3.sb([128, 4], F32, "gqn")
            l_c.append(P.ts("pool", gqn[:], gq[:], -1.0, op0=ALU.mult, waits=l_c))
            rsr = A3.ring(2, [128, 512], F32, "rs")
            ar = A3.ring(2, [128, 512], F32, "a")
            a0r = A3.ring(2, [128, 512], F32, "a0")
            br = A3.ring(2, [128, 512], F32, "b")
            tokd = [t_h] + wt_toks
            if "tm" not in IN_PARTS:
                pass
            elif l == 0:
                gb = A3.sb([128, 16], F32, "gb")
                l_gb = P.dma(gb[:], Dm["GATEB"].partition_broadcast(128))
                s_ka = A3.ring(2, [128, 512], BF16, "ska")
                s_va = A3.ring(2, [128, 512], BF16, "sva")
                s_oa = A3.ring(2, [128, 512], F32, "soa")
                s_gv = A3.ring(2, [128, 16], F32, "sga")
                s_vb = A3.ring(2, [128, 128], BF16, "svb")
                for c in range(NT):
                    tsl = slice(c * 128, (c + 1) * 128)
                    for gi, (c0, ncol) in enumerate(((0, 512), (512, 512), (1024, 512), (1536, 144))):
                        ps, pfree, pk = psr.get()
                        for kc in range(8):
                            tmm = P.mm(ps[:, 0:ncol], hT[:, kc, tsl], WT[:, kc, c0:c0 + ncol], start=(kc == 0), stop=(kc == 7),
                                       waits=tokd + pfree, inc=(kc == 7))
                        if gi == 0:
                            sbuf, sfree, sk = s_ka.get()
                            te = P.act(sbuf[:], ps[:, 0:512], AF.Copy, scale=128.0 ** -0.5, waits=[tmm] + sfree)
                            s_ka.rel(sk, [P.dma(Dm["KA"][tsl, :], sbuf[:], waits=[te])])
                        elif gi == 1:
                            sbuf, sfree, sk = s_va.get()
                            te = P.copy("dve", sbuf[:], ps[:, 0:512], waits=[tmm] + sfree)
                            s_va.rel(sk, [P.dma(Dm["VA"][tsl, :], sbuf[:], waits=[te])])
                        elif gi == 2:
                            sbuf, sfree, sk = s_oa.get()
                            te = P.act(sbuf[:], ps[:, 0:512], AF.Sigmoid, waits=[tmm] + sfree)
                            s_oa.rel(sk, [P.dma(Dm["SOA"][tsl, :], sbuf[:], waits=[te])])
                        else:
                            sbuf, sfree, sk = s_gv.get()
                            P.tt("dve", sbuf[:], ps[:, 0:16], gb[:], ALU.add, waits=[tmm, l_gb] + sfree, inc=False)
                            sb2, sfree2, sk2 = s_vb.get()
                            te = P.copy("dve", sb2[:], ps[:, 16:144], waits=sfree2)
                            s_gv.rel(sk, [P.dma(Dm["GA"][tsl, :], sbuf[:], waits=[te])])
                            s_vb.rel(sk2, [P.dma(Dm["VB"][tsl, :], sb2[:], waits=[te])])
                        psr.rel(pk, [te])
            else:
                s_v = A3.ring(2, [128, 256], BF16, "sv")
                for c in range(NT):
                    tsl = slice(c * 128, (c + 1) * 128)
                    ps, pfree, pk = psr.get()
                    for kc in range(8):
                        tmm = P.mm(ps[:, 0:256], hT[:, kc, tsl], WT[:, kc, :], start=(kc == 0), stop=(kc == 7), waits=tokd + pfree, inc=(kc == 7))
                    sbuf, sfree, sk = s_v.get()
                    te = P.copy("dve", sbuf[:], ps[:, 0:256], waits=[tmm] + sfree)
                    s_v.rel(sk, [P.dma(Dm["VC"][tsl, :], sbuf[:], waits=[te])])
                    psr.rel(pk, [te])
            if l == 0:
                specs = ([("copy", "QA_T", i, 1.0, True) for i in range(4)] + [("copy", "KA_T", i, 128.0 ** -0.5, True) for i in range(4)]
                         + [("rope", "QB_T", i, 0, True) for i in range(4)] + [("rope", "KB_T", i, 2, True) for i in range(2)])
            else:
                specs = [("rope", "QC_T", i, 0, False) for i in range(8)] + [("rope", "KC_T", i, 2, True) for i in range(2)]
            sg_all = [(0, 256)] + subgroups(T_CTX, T)
            ei = 0
            wst = WStream(P, wfr, wbr, [Dm["WF%d" % l][ci] for ci in range(len(specs))], depth=2)

            def rope_stage2(R):
                (n, t0, g0, a0, a0k, t_a0, sq, sqk, tq, dst_, di_, dcol) = R
                ps2, p2free, p2k = ps2r.get()
                tm2 = P.mm(ps2[:, 0:n], blkb[:], sq[:, 0:n], waits=[tq] + l_c + p2free, inc=True)
                sqr.rel(sqk, [tm2])
                rs, rfree, rk = rsr.get()
                P.act(rs[:, 0:n], ps2[:, 0:n], AF.Ln, bias=epsb[:, 0:1], scale=1.0 / dh, waits=[tm2] + rfree, inc=False)
                t_rs = P.act(rs[:, 0:n], rs[:, 0:n], AF.Exp, scale=-0.5)
                ps2r.rel(p2k, [t_rs])
                a, afree, ak = ar.get()
                b, bfree, bk = br.get()
                P.stt(a[:, 0:n], a0[:, 0:n], gq[:, g0:g0 + 1], cosT[:, t0:t0 + n], ALU.mult, ALU.mult, waits=[t_a0] + l_c + afree, inc=False)
                P.stt(b[0:64, 0:n], a0[64:128, 0:n], gqn[64:128, g0:g0 + 1], sinT[64:128, t0:t0 + n], ALU.mult, ALU.mult, waits=bfree, inc=False)
                t_b = P.stt(b[64:128, 0:n], a0[0:64, 0:n], gqn[0:64, g0:g0 + 1], sinT[0:64, t0:t0 + n], ALU.mult, ALU.mult)
                a0r.rel(a0k, [t_b, tq])
                t_ab = P.tt("dve", a[:, 0:n], a[:, 0:n], b[:, 0:n], ALU.add)
                ob, ofree, ok = stg.get()
                te = P.tt("pool", ob[:, 0:n], a[:, 0:n], rs[:, 0:n], ALU.mult, waits=[t_rs, t_ab] + ofree)
                ar.rel(ak, [te])
                br.rel(bk, [te])
                rsr.rel(rk, [te])
                td = P.dma(Dm[dst_][di_ * 128:(di_ + 1) * 128, dcol:dcol + n], ob[:, 0:n], waits=[te])
                stg.rel(ok, [td])

            pend_rope = None
            for ci, (kind, dst, di, par, with_ctx) in enumerate(specs):
                wb, tcast, wbk = wst.next()
                last_mm = None
                for (t0, n) in (sg_all if with_ctx else sg_all[1:]):
                    ps, pfree, pk = psr.get()
                    for kc in range(8):
                        tmm = P.mm(ps[:, 0:n], wb[:, kc * 128:(kc + 1) * 128], hT[:, kc, t0:t0 + n], start=(kc == 0), stop=(kc == 7),
                                   waits=[tcast, t_h] + pfree, inc=(kc == 7))
                    last_mm = tmm
                    dcol = t0 if with_ctx else t0 - T_CTX
                    if kind == "copy":
                        ob, ofree, ok = stg.get()
                        ei += 1
                        if ei % 2 == 0:
                            te = P.act(ob[:, 0:n], ps[:, 0:n], AF.Copy, scale=par, waits=[tmm] + ofree)
                        else:
                            te = P.ts("dve", ob[:, 0:n], ps[:, 0:n], par, op0=ALU.mult, waits=[tmm] + ofree)
                        psr.rel(pk, [te])
                        td = P.dma(Dm[dst][di * 128:(di + 1) * 128, dcol:dcol + n], ob[:, 0:n], waits=[te])
                        stg.rel(ok, [td])
                    else:
                        a0, a0free, a0k = a0r.get()
                        t_a0 = P.copy("act", a0[:, 0:n], ps[:, 0:n], waits=[tmm] + a0free)
                        psr.rel(pk, [t_a0])
                        sq, sqfree, sqk = sqr.get()
                        tq = P.act(sq[:, 0:n], a0[:, 0:n], AF.Square, waits=[t_a0] + sqfree)
                        if pend_rope is not None:
                            rope_stage2(pend_rope)
                        pend_rope = (n, t0, par, a0, a0k, t_a0, sq, sqk, tq, dst, di, dcol)
                wbr.rel(wbk, [last_mm])
            if pend_rope is not None:
                rope_stage2(pend_rope)
            P.emit()


def block_mlstm(nc, P, Dm):
    with ExitStack() as st:
        A = Alloc(nc, st)
        C, tl = load_consts(nc, P, A, Dm, ["ONES", "UTRI", "LTRI", "IDENT"])
        ones, utri, ltri, ident = C["ONES"], C["UTRI"], C["LTRI"], C["IDENT"]
        idb = A.sb([128, 128], BF16, "idb")
        t_idb = P.copy("pool", idb[:], ident[:], waits=tl)
        G = A.sb([128, NT, 16], F32, "G")
        LF = A.sb([128, 2, NT, 4], F32, "LF")
        LI = A.sb([128, 2, NT, 4], F32, "LI")
        CUM = A.sb([128, 2, NT, 4], F32, "CUM")
        COLF = A.sb([128, 2, NT, 4], F32, "COLF")
        ECOL = A.sb([128, 2, NT, 4], F32, "ECOL")
        CARRY = A.sb([128, 2, NT, 4], F32, "CARRY")
        mlg = A.sb([128, 512], F32, "mlg")
        epsb = A.sb([128, 1], F32, "eps")
        oneb = A.sb([128, 1], F32, "oneb")
        P.memset("pool", epsb[:], EPS, inc=False)
        t_c1 = P.memset("pool", oneb[:], 1.0)
        l_g = P.dma(G[:], Dm["GA"].rearrange("(c p) g -> p c g", p=128))
        l_mlg = P.dma(mlg[:], Dm["MLNG"].partition_broadcast(128))
        for d_, (ci, cf) in enumerate(((0, 4), (8, 12))):
            P.act(LF[:, d_], G[:, :, cf:cf + 4], AF.Exp, scale=-1.0, waits=[l_g], inc=False)
            P.act(LF[:, d_], LF[:, d_], AF.Ln, bias=oneb[:, 0:1], scale=1.0, waits=[t_c1], inc=False)
            t_lf = P.act(LF[:, d_], LF[:, d_], AF.Copy, scale=-1.0)
            t_li = P.copy("dve", LI[:, d_], G[:, :, ci:ci + 4], waits=[l_g])
        psg = A.ps([128, 512], F32, "psg")
        LFf = LF[:].rearrange("p d c h -> p d (c h)")
        P.mm(psg[:, 0:136], utri[:], LFf[:, 0, :], waits=[t_lf] + tl)
        P.mm(psg[:, 136:272], ltri[:], LFf[:, 1, :])
        t_cum = P.mm(psg[:, 272:408], ones[:], LFf[:, 0, :], inc=True)
        CUMf = CUM[:].rearrange("p d c h -> p (d c h)")
        t_cc = P.copy("dve", CUMf, psg[:, 0:272], waits=[t_cum])
        CARf = CARRY[:].rearrange("p d c h -> p d (c h)")
        t_e1 = P.act(CARf[:, 0, :], psg[:, 272:408], AF.Exp, waits=[t_cum, t_cc])
        t_cum2 = P.mm(psg[:, 272:408], ones[:], LFf[:, 1, :], waits=[t_e1, t_cc], inc=True)
        t_e2 = P.act(CARf[:, 1, :], psg[:, 272:408], AF.Exp, waits=[t_cum2])
        COLFf = COLF[:].rearrange("p d c h -> p (d c h)")
        LIf = LI[:].rearrange("p d c h -> p (d c h)")
        ECOLf = ECOL[:].rearrange("p d c h -> p (d c h)")
        P.tt("dve", COLFf, LIf, CUMf, ALU.subtract, waits=[t_li], inc=False)
        t_colf0 = P.copy("dve", ECOLf, CUMf)
        t_colf = P.act(COLFf, COLFf, AF.Exp, waits=[t_colf0])
        t_ecol = P.act(ECOLf, ECOLf, AF.Exp, scale=-1.0)
        gate_toks = [t_colf, t_ecol, t_e1, t_e2, t_idb]

        qTr = A.ring(2, [128, T], BF16, "qT")
        kTr = A.ring(2, [128, T], BF16, "kT")
        Ktr = A.ring(2, [128, NT, 128], BF16, "Kt")
        Var = A.ring(2, [128, NT, 136], BF16, "Va")
        SOr = A.ring(1, [128, NT, 128], F32, "SO")
        Vc = A.sb([128, 2, NT, 136], BF16, "Vc")
        CB = A.sb([128, 2, NT, 136], BF16, "CB")
        Cst = A.sb([128, 2, 3, 136], F32, "Cst")
        OUTr = A.ring(2, [128, T], BF16, "OUTS")
        sA, sB = Slot(A.ps([128, 512], F32, "psA")), Slot(A.ps([128, 512], F32, "psB"))
        psU = Ring([sA, sB])
        psS = A.ring(2, [128, 512], F32, "psS", psum=True)
        psO0 = Ring([Slot(A.ps([128, 512], F32, "psO0")), sA])
        psO1 = Ring([Slot(A.ps([128, 512], F32, "psO1")), sB])
        psT = A.ring(1, [128, 128], BF16, "psT", psum=True)
        Pfr = A.ring(2, [128, 128], BF16, "Pf")
        Pbr = A.ring(2, [128, 128], BF16, "Pb")
        Hfr = A.ring(2, [128, 128], F32, "Hf")
        hr = A.ring(2, [128, 128], F32, "h")
        hnr = A.ring(2, [128, 128], BF16, "hn")
        gsr = A.ring(2, [128, 128], F32, "gs")
        jkr = A.ring(2, [128, 128], F32, "jk")
        smr = A.ring(4, [128, 8], F32, "sm")
        vc_free = []
        for h in range(4):
            qT, f1, k1 = qTr.get()
            kT, f2, k2 = kTr.get()
            Kt, f3, k3 = Ktr.get()
            Va, f4, k4 = Var.get()
            SO, f5, k5 = SOr.get()
            hs = slice(h * 128, (h + 1) * 128)
            l_q = P.dma(qT[:], Dm["QA_T"][hs, :], waits=f1)
            l_k = P.dma(kT[:], Dm["KA_T"][hs, :], waits=f2)
            l_kt = P.dma(Kt[:], Dm["KA"].rearrange("(c p) n -> p c n", p=128)[:, :, hs], waits=f3)
            l_va = P.dma(Va[:, :, 0:128], Dm["VA"].rearrange("(c p) n -> p c n", p=128)[:, :, hs], waits=f4)
            t_on = P.memset("pool", Va[:, :, 128:129], 1.0, waits=f4)
            l_so = P.dma(SO[:], Dm["SOA"].rearrange("(c p) n -> p c n", p=128)[:, :, hs], waits=f5)
            vtok = {}
            for d_ in range(2):
                tv = P.tt("dve", Vc[:, d_, :, 0:129], Va[:, :, 0:129], COLF[:, d_, :, h:h + 1].broadcast_to([128, NT, 129]), ALU.mult,
                          waits=[l_va, t_on] + gate_toks + vc_free)
                for c in range(NT):
                    vtok[(d_, c)] = tv
            orders = [list(range(NT)), [1, 0] + list(range(NT - 1, 1, -1))]
            t0s = [P.memset("dve", Cst[:, d_, 0, 0:129], 0.0, waits=vc_free) for d_ in range(2)]
            P.memset("pool", CB[:, 0, orders[0][0], 0:129], 0.0, waits=vc_free, inc=False)
            t_cb0 = P.memset("pool", CB[:, 1, orders[1][0], 0:129], 0.0, waits=vc_free)
            cbtok = {(0, orders[0][0]): t_cb0, (1, orders[1][0]): t_cb0}
            cbhist = {0: [], 1: []}
            for i in range(NT - 1):
                for d_ in range(2):
                    c = orders[d_][i]
                    cn = orders[d_][i + 1]
                    pu, pufree, puk = psU.get()
                    tmm = P.mm(pu[:, 0:129], Kt[:, c, :], Vc[:, d_, c, 0:129], waits=[l_kt, vtok[(d_, c)]] + pufree, inc=True)
                    src = Cst[:, d_, i % 3, 0:129]
                    dst = Cst[:, d_, (i + 1) % 3, 0:129]
                    P.tt("dve", dst, src, pu[:, 0:129], ALU.add, waits=[tmm] + ([cbhist[d_][i - 3]] if i >= 3 else []), inc=False)
                    t_s = P.ts("dve", dst, dst, CARRY[:, d_, c, h:h + 1], op0=ALU.mult, waits=gate_toks)
                    psU.rel(puk, [t_s])
                    t_cb = P.copy("act", CB[:, d_, cn, 0:129], dst, waits=[t_s])
                    cbhist[d_].append(t_cb)
                    cbtok[(d_, cn)] = t_cb
            OUTS, fo, ko = OUTr.get()
            rd = []
            stA, stB = {}, {}

            def stageA(c):
                tsl = slice(c * 128, (c + 1) * 128)
                pS, psfree, psk = psS.get()
                t_S = P.mm(pS[:, 0:128], kT[:, tsl], qT[:, tsl], waits=[l_q, l_k] + psfree, inc=True)
                Pf, pff, pfk = Pfr.get()
                Pb, pbf, pbk = Pbr.get()
                P.tt("dve", Pf[:], pS[:, 0:128], utri[:], ALU.mult, waits=[t_S] + pff, inc=False)
                t_pb = P.tt("dve", Pb[:], pS[:, 0:128], ltri[:], ALU.mult, waits=pbf)
                psS.rel(psk, [t_pb])
                pO0, pof0, pok0 = psO0.get()
                pO1, pof1, pok1 = psO1.get()
                P.mm(pO0[:, 0:129], Pf[:], Vc[:, 0, c, 0:129], start=True, stop=False, waits=[t_pb, vtok[(0, c)]] + pof0)
                t_o0 = P.mm(pO0[:, 0:129], qT[:, tsl], CB[:, 0, c, 0:129], start=False, stop=True, waits=[cbtok[(0, c)]], inc=True)
                P.mm(pO1[:, 0:129], Pb[:], Vc[:, 1, c, 0:129], start=True, stop=False, waits=[t_pb, vtok[(1, c)]] + pof1)
                t_o1 = P.mm(pO1[:, 0:129], qT[:, tsl], CB[:, 1, c, 0:129], start=False, stop=True, waits=[cbtok[(1, c)]], inc=True)
                Pfr.rel(pfk, [t_o0])
                Pbr.rel(pbk, [t_o1])
                stA[c] = (pO0, pok0, pO1, pok1, t_o0, t_o1)
                return [t_o0, t_o1]

            def stageB(c):
                pO0, pok0, pO1, pok1, t_o0, t_o1 = stA.pop(c)
                sm, smf, smk = smr.get()
                rt = []
                for d_, (pOd, tk_) in enumerate(((pO0, t_o0), (pO1, t_o1))):
                    P.ts("dve", sm[:, 2 * d_:2 * d_ + 1], pOd[:, 128:129], -1.0, op0=ALU.mult, waits=[tk_] + smf, inc=False)
                    t_m = P.stt(sm[:, 2 * d_:2 * d_ + 1], sm[:, 2 * d_:2 * d_ + 1], ECOL[:, d_, c, h:h + 1], pOd[:, 128:129], ALU.max, ALU.max)
                    rt.append(P.recip(sm[:, 2 * d_ + 1:2 * d_ + 2], sm[:, 2 * d_:2 * d_ + 1], waits=[t_m]))
                Hf, hff, hfk = Hfr.get()
                hh, hhf, hhk = hr.get()
                t_hf = P.ts("dve", Hf[:], pO0[:, 0:128], sm[:, 1:2], op0=ALU.mult, waits=[rt[0]] + hff)
                psO0.rel(pok0, [t_hf])
                t_h = P.stt(hh[:], pO1[:, 0:128], sm[:, 3:4], Hf[:], ALU.mult, ALU.add, waits=[rt[1]] + hhf)
                psO1.rel(pok1, [t_h])
                Hfr.rel(hfk, [t_h])
                jk, jkf, jkk = jkr.get()
                t_sq = P.act(jk[:], hh[:], AF.Square, waits=[t_h] + jkf)
                stB[c] = (sm, smk, hh, hhk, jk, jkk, t_sq)

            def stageC(c):
                tsl = slice(c * 128, (c + 1) * 128)
                sm, smk, hh, hhk, jk, jkk, t_sq = stB.pop(c)
                t_ss = P.op("dve", lambda e, o_=sm[:, 4:5], i_=jk[:]: e.tensor_reduce(out=o_, in_=i_, axis=mybir.AxisListType.X, op=ALU.add), [t_sq])
                t_ln = P.act(sm[:, 5:6], sm[:, 4:5], AF.Ln, bias=epsb[:, 0:1], scale=1.0 / 128, waits=[t_ss])
                t_r = P.act(sm[:, 5:6], sm[:, 5:6], AF.Exp, scale=-0.5, waits=[t_ln], force=True)
                jkr.rel(jkk, [t_ss])
                gs, gsf, gsk = gsr.get()
                t_gs = P.tt("pool", gs[:], SO[:, c, :], mlg[:, hs], ALU.mult, waits=[l_so, l_mlg] + gsf)
                hn, hnf, hnk = hnr.get()
                t_hn = P.stt(hn[:], hh[:], sm[:, 5:6], gs[:], ALU.mult, ALU.mult, waits=[t_r, t_gs] + hnf)
                hr.rel(hhk, [t_hn])
                gsr.rel(gsk, [t_hn])
                smr.rel(smk, [t_hn])
                pT, ptf, ptk = psT.get()
                t_T = P.transpose(pT[:], hn[:], idb[:], waits=[t_hn, t_idb] + ptf)
                hnr.rel(hnk, [t_T])
                t_ev = P.copy("act", OUTS[:, tsl], pT[:], waits=[t_T] + fo)
                psT.rel(ptk, [t_ev])
                return t_ev, t_gs

            for c in range(NT + 2):
                if c < NT:
                    rd = stageA(c)
                if 0 <= c - 1 < NT:
                    stageB(c - 1)
                if 0 <= c - 2 < NT:
                    t_ev, t_gs = stageC(c - 2)
            vc_free = rd
            td = P.dma(Dm["MIX_T"][hs, :], OUTS[:], waits=[t_ev])
            OUTr.rel(ko, [td])
            qTr.rel(k1, rd)
            kTr.rel(k2, rd)
            Ktr.rel(k3, rd)
            Var.rel(k4, rd)
            SOr.rel(k5, [t_gs])
        if "D_G" in Dm:
            for i_, tl_ in enumerate((CUM, COLF, ECOL, CARRY, LF, LI)):
                P.dma(Dm["D_G"][:, i_, :], tl_[:].rearrange("p d c h -> p (d c h)"), waits=[td])
            P.dma(Dm["D_VC"].rearrange("p (a n) -> p a n", n=129), Vc[:].rearrange("p d c n -> p (d c) n")[:, :, 0:129], waits=[td])
            P.dma(Dm["D_CB"].rearrange("p (a n) -> p a n", n=129), CB[:].rearrange("p d c n -> p (d c) n")[:, :, 0:129], waits=[td])
            for i_, rg in enumerate((Pfr, Pbr, hnr)):
                for j_ in range(2):
                    P.dma(Dm["D_B"][:, i_, j_, :], rg.bufs[j_][:], waits=[td])
            for i_, rg in enumerate((Hfr, hr, gsr, jkr)):
                for j_ in range(2):
                    P.dma(Dm["D_F"][:, i_, j_, :], rg.bufs[j_][:], waits=[td])
            for j_ in range(4):
                P.dma(Dm["D_S"][:, j_, 0:6], smr.bufs[j_][:, 0:6], waits=[td])
        P.emit()


def block_window(nc, P, Dm):
    with ExitStack() as st:
        A = Alloc(nc, st)
        C, tl = load_consts(nc, P, A, Dm, ["IDENT", "NEGP", "NEGN", "HMASK", "SINKP"])
        idb = A.sb([128, 128], BF16, "idb")
        negp = A.sb([128, 128], BF16, "negp")
        negn = A.sb([128, 128], BF16, "negn")
        esink = A.sb([128, 4], F32, "esink")
        P.copy("pool", idb[:], C["IDENT"][:], waits=tl, inc=False)
        P.copy("pool", negp[:], C["NEGP"][:], inc=False)
        t_c = P.copy("pool", negn[:], C["NEGN"][:])
        t_es = P.act(esink[:], C["SINKP"][:], AF.Exp, waits=tl)
        QT = A.sb([128, 4, T], BF16, "QT")
        KTd = A.sb([128, 2, T], BF16, "KTd")
        KTm = A.sb([128, 2, 2, T], BF16, "KTm")
        VBt = A.sb([128, NT, 128], BF16, "VBt")
        VAe = A.sb([128, 2, 2, NT, 128], BF16, "VAe")
        ONe = A.sb([128, 2, 128], BF16, "ONe")
        OB = A.sb([128, 4, T], BF16, "OB")
        l_q = [P.dma(QT[:, i, :], Dm["QB_T"][i * 128:(i + 1) * 128, :]) for i in range(4)]
        l_k = [P.dma(KTd[:, i, :], Dm["KB_T"][i * 128:(i + 1) * 128, :]) for i in range(2)]
        l_v = P.dma(VBt[:], Dm["VB"].rearrange("(c p) n -> p c n", p=128))
        prep = []
        for kv in range(2):
            for e in range(2):
                en = "pool" if e == 0 else "dve"
                prep.append(P.ts(en, KTm[:, kv, e, :], KTd[:, kv, :], C["HMASK"][:, e:e + 1], op0=ALU.mult, waits=l_k + tl))
        P.memset("pool", VAe[:].rearrange("p a b c d -> p (a b c d)"), 0.0, inc=False)
        P.memset("pool", ONe[:].rearrange("p a b -> p (a b)"), 0.0, inc=False)
        for e in range(2):
            P.memset("pool", ONe[:, e, e * 64:(e + 1) * 64], 1.0, inc=False)
            for kv in range(2):
                tk = P.copy("pool", VAe[:, kv, e, :, e * 64:(e + 1) * 64], VBt[:, :, kv * 64:(kv + 1) * 64], waits=[l_v])
        prep.append(tk)
        prep += [t_c, t_es]
        psL = A.ring(2, [128, 512], F32, "psL", psum=True)
        psX = A.ring(2, [128, 512], F32, "psX", psum=True)
        psO = A.ring(2, [128, 512], F32, "psO", psum=True)
        psD = A.ring(2, [128, 512], F32, "psD", psum=True)
        PLr = A.ring(3, [128, 384], BF16, "PL")
        PXr = A.ring(3, [128, 256], BF16, "PX")
        dsr = A.ring(2, [128, 128], F32, "ds")
        rr = A.ring(2, [128, 128], F32, "rr")

        items = []
        for n in range(NT):
            for i in range(4):
                for e in range(2):
                    items.append((n, i, e))

        def kbs_of(n):
            if n < 2:
                return [], [0, 1]
            L = [kb for kb in (n - 1, n, n + 1) if 2 <= kb < NT]
            return L, [0, 1]

        def emit_S(it):
            n, i, e = it
            kv = i // 2
            qs = slice(n * 128, (n + 1) * 128)
            L, X = kbs_of(n)
            pl = plk = None
            tL = None
            if L:
                pl, plf, plk = psL.get()
                for idx, kb in enumerate(L):
                    masked = kb != n
                    tL = P.mm(pl[:, idx * 128:(idx + 1) * 128], KTm[:, kv, e, kb * 128:(kb + 1) * 128], QT[:, i, qs], start=True, stop=not masked,
                              waits=l_q + prep + plf, inc=(not masked and idx == len(L) - 1))
                    if masked:
                        tL = P.mm(pl[:, idx * 128:(idx + 1) * 128], idb[:], (negp if kb == n - 1 else negn)[:], start=False, stop=True,
                                  inc=(idx == len(L) - 1))
            px, pxf, pxk = psX.get()
            for idx, kb in enumerate(X):
                tX = P.mm(px[:, idx * 128:(idx + 1) * 128], KTm[:, kv, e, kb * 128:(kb + 1) * 128], QT[:, i, qs], start=True, stop=True,
                          waits=l_q + prep + pxf, inc=(idx == 1))
            return dict(it=it, L=L, X=X, pl=pl, plk=plk, tL=tL, px=px, pxk=pxk, tX=tX)

        def emit_exp(S):
            if S["L"]:
                PL, f, k = PLr.get()
                nl = len(S["L"]) * 128
                S["tPL"] = P.act(PL[:, 0:nl], S["pl"][:, 0:nl], AF.Exp, scale=0.125, waits=[S["tL"]] + f)
                psL.rel(S["plk"], [S["tPL"]])
                S["PL"], S["PLk"] = PL, k
            PX, f, k = PXr.get()
            S["tPX"] = P.act(PX[:], S["px"][:, 0:256], AF.Exp, scale=0.125, waits=[S["tX"]] + f)
            psX.rel(S["pxk"], [S["tPX"]])
            S["PX"], S["PXk"] = PX, k

        cur = {}

        def emit_PV(S):
            n, i, e = S["it"]
            kv = i // 2
            if e == 0:
                po, pof, pok = psO.get()
                pd, pdf, pdk = psD.get()
                cur["po"], cur["pok"], cur["pof"] = po, pok, pof + pdf
                cur["pd"], cur["pdk"] = pd, pdk
            po = cur["po"]
            pd = cur["pd"]
            seq = [("L", idx, kb) for idx, kb in enumerate(S["L"])] + [("X", idx, kb) for idx, kb in enumerate(S["X"])]
            last = None
            for j, (reg, idx, kb) in enumerate(seq):
                src = (S["PL"] if reg == "L" else S["PX"])[:, idx * 128:(idx + 1) * 128]
                wt = [S["tPL"] if reg == "L" else S["tPX"]] + (cur["pof"] if (e == 0 and j == 0) else [])
                first = (e == 0 and j == 0)
                fin = (e == 1 and j == len(seq) - 1)
                P.mm(po[:, 0:128], VAe[:, kv, e, kb, :], src, start=first, stop=fin, waits=wt)
                last = P.mm(pd[:, 0:128], ONe[:, e, :], src, start=first, stop=fin, inc=(j == len(seq) - 1))
            if S["L"]:
                PLr.rel(S["PLk"], [last])
            PXr.rel(S["PXk"], [last])
            if e == 1:
                ds, f, k = dsr.get()
                t_ds = P.ts("dve", ds[:], pd[:, 0:128], esink[:, i:i + 1], op0=ALU.add, waits=[last] + f)
                psD.rel(cur["pdk"], [t_ds])
                r_, f2, k2 = rr.get()
                t_r = P.recip(r_[:], ds[:], waits=f2)
                t_o = P.tt("dve", OB[:, i, n * 128:(n + 1) * 128], po[:, 0:128], r_[:], ALU.mult, waits=[t_r])
                psO.rel(cur["pok"], [t_o])
                dsr.rel(k, [t_r])
                rr.rel(k2, [t_o])
                cur["last_o"] = t_o

        prev = None
        for it in items:
            S = emit_S(it)
            emit_exp(S)
            if prev is not None:
                emit_PV(prev)
            prev = S
        emit_PV(prev)
        for i in range(4):
            P.dma(Dm["MIX_T"][512 + i * 128:512 + (i + 1) * 128, :], OB[:, i, :], waits=[cur["last_o"]])
        P.emit()


def block_outmlp(nc, P, Dm, l):
    xsrc = Dm["XIN"] if l == 0 else Dm["X1"]
    mix = Dm["MIX_T"] if l == 0 else Dm["MIX1_T"]
    dst = Dm["X1"] if l == 0 else Dm["OUT_T"]
    if l == 0:
        groups = [[(0, 256, 1), (256, 512, 0), (768, 320, 0)]] + [[(g * 1088, 384, 0), (g * 1088 + 384, 384, 0), (g * 1088 + 768, 320, 0)] for g in range(1, 4)]
        mixoff, dstoff = 0, 0
    else:
        groups = [[(T_CTX + g * 1024, 512, 0), (T_CTX + g * 1024 + 512, 512, 0)] for g in range(4)]
        mixoff, dstoff = T_CTX, T_CTX
    GT = 1088
    with ExitStack() as st:
        A = Alloc(nc, st)
        modv = A.sb([128, 6, 8, 2], F32, "modv")
        ones = A.sb([128, 128], F32, "ones")
        epsb = A.sb([128, 1], F32, "eps")
        l_mod = P.dma(modv[:], Dm["MODV"][:, l])
        l_ones = P.dma(ones[:], Dm["ONES"])
        t_eps = P.memset("pool", epsb[:], EPS)
        lc = [l_mod, l_ones, t_eps]
        wout = A.sb([128, 8, 1024], BF16, "wout")
        wfr = A.ring(4, [128, 1024], F32, "wf")
        wbr = A.ring(4, [128, 1024], BF16, "wb")
        wo_t = []
        for kc in range(8):
            wf, f, k = wfr.get()
            ld = P.dma(wf[:], Dm["WOUT%d" % l][:, kc, :], waits=f)
            tk = P.copy("act", wout[:, kc, :], wf[:], waits=[ld])
            wfr.rel(k, [tk])
            wo_t.append(tk)
        xmid = A.sb([128, 8, GT], F32, "xmid")
        h2T = A.sb([128, 8, GT], BF16, "h2T")
        h1 = A.sb([128, 32, GT], BF16, "h1")
        mxr = A.ring(2, [128, 8, 512], BF16, "mx")
        sqr = A.ring(2, [128, 512], BF16, "sq")
        rsr = A.ring(3, [128, 512], F32, "rs")
        onesb = A.sb([128, 128], BF16, "onesb")
        t_ob = P.memset("pool", onesb[:], 1.0)
        tmr = A.ring(2, [128, 512], F32, "tm")
        rlr = A.ring(2, [128, 512], F32, "rl")
        psr = A.ring(4, [128, 512], F32, "ps", psum=True)
        pss = A.ring(2, [128, 512], F32, "pss", psum=True)
        xm_free = []
        h1_free = []
        h2_free = []
        srcs = []
        for grp in groups:
            srcs += [Dm["W1_%d" % l][f_] for f_ in range(32)]
            srcs += [Dm["W2_%d" % l][nch, q] for nch in range(8) for q in range(4)]
        wst = WStream(P, wfr, wbr, srcs, depth=3)
        def load_x(grp_, waits_by_nch):
            g0_ = grp_[0][0]
            nt_ = grp_[-1][0] + grp_[-1][1] - g0_
            return [P.dma(xmid[:, nch_, 0:nt_], xsrc[nch_ * 128:(nch_ + 1) * 128, g0_:g0_ + nt_], waits=waits_by_nch[nch_]) for nch_ in range(8)]

        def load_x_one(grp_, nch_, waits_):
            g0_ = grp_[0][0]
            nt_ = grp_[-1][0] + grp_[-1][1] - g0_
            return P.dma(xmid[:, nch_, 0:nt_], xsrc[nch_ * 128:(nch_ + 1) * 128, g0_:g0_ + nt_], waits=waits_, en="pool")

        ldx_n = load_x(groups[0], [[] for _ in range(8)])
        for gi_, grp in enumerate(groups):
            g0 = grp[0][0]
            h2_toks = []
            ldx_next = [None] * 8

            def a_stage1(t0, n, j):
                o = t0 - g0
                mx, mf, mk = mxr.get()
                ldm = P.dma(mx[:, :, 0:n], mix.rearrange("(kc p) t -> p kc t", p=128)[:, :, t0 - mixoff:t0 - mixoff + n], waits=mf)
                ps2, p2f, p2k = pss.get()
                for nch in range(8):
                    ps, pf, pk = psr.get()
                    for kc in range(8):
                        tmm = P.mm(ps[:, 0:n], wout[:, kc, nch * 128:(nch + 1) * 128], mx[:, kc, 0:n], start=(kc == 0), stop=(kc == 7),
                                   waits=[ldm] + wo_t + pf, inc=(kc == 7))
                    t_x = P.stt(xmid[:, nch, o:o + n], ps[:, 0:n], modv[:, 2, nch, j:j + 1], xmid[:, nch, o:o + n], ALU.mult, ALU.add, waits=[tmm, ldx_n[nch]] + lc)
                    psr.rel(pk, [t_x])
                    sq, sf, sk = sqr.get()
                    tq = P.act(sq[:, 0:n], xmid[:, nch, o:o + n], AF.Square, waits=[t_x] + sf)
                    tm2 = P.mm(ps2[:, 0:n], onesb[:], sq[:, 0:n], start=(nch == 0), stop=(nch == 7), waits=[tq, t_ob] + lc + p2f, inc=True)
                    sqr.rel(sk, [tm2])
                mxr.rel(mk, [tmm])
                rs, rf, rk = rsr.get()
                P.act(rs[:, 0:n], ps2[:, 0:n], AF.Ln, bias=epsb[:, 0:1], scale=1.0 / D, waits=[tm2] + rf, inc=False)
                t_rs = P.act(rs[:, 0:n], rs[:, 0:n], AF.Exp, scale=-0.5)
                pss.rel(p2k, [t_rs])
                return (o, n, j, rs, rk, t_rs)

            def a_stage2(S):
                o, n, j, rs, rk, t_rs = S
                toks = [None, None]
                for kc in range(8):
                    tm, tf, tk = tmr.get()
                    t1 = P.stt(tm[:, 0:n], xmid[:, kc, o:o + n], modv[:, 4, kc, j:j + 1], rs[:, 0:n], ALU.mult, ALU.mult, waits=[t_rs] + tf)
                    if kc < 2:
                        t2 = P.ts("pool", h2T[:, kc, o:o + n], tm[:, 0:n], modv[:, 3, kc, j:j + 1], op0=ALU.add, waits=[t1] + h2_free)
                    else:
                        t2 = P.act(h2T[:, kc, o:o + n], tm[:, 0:n], AF.Identity, bias=modv[:, 3, kc, j:j + 1], waits=[t1] + h2_free)
                    tmr.rel(tk, [t2])
                    toks[0 if kc < 2 else 1] = t2
                rsr.rel(rk, [t1])
                h2_toks.append(toks)

            prev_a = None
            for sgd in grp:
                cur_a = a_stage1(*sgd)
                if prev_a is not None:
                    a_stage2(prev_a)
                prev_a = cur_a
            a_stage2(prev_a)
            last1 = None
            for f_ in range(32):
                wb, tc_, bk = wst.next()
                for si, (t0, n, j) in enumerate(grp):
                    o = t0 - g0
                    ps, pf, pk = psr.get()
                    for kc in range(8):
                        tmm = P.mm(ps[:, 0:n], wb[:, kc * 128:(kc + 1) * 128], h2T[:, kc, o:o + n], start=(kc == 0), stop=(kc == 7),
                                   waits=[tc_, h2_toks[si]] + pf, inc=(kc == 7))
                    rl, rf, rk = rlr.get()
                    t_r = P.act(rl[:, 0:n], ps[:, 0:n], AF.Relu, waits=[tmm] + rf)
                    psr.rel(pk, [t_r])
                    en = "dve"
                    t_h1 = P.tt(en, h1[:, f_, o:o + n], rl[:, 0:n], rl[:, 0:n], ALU.mult, waits=[t_r] + h1_free)
                    rlr.rel(rk, [t_h1])
                    last1 = t_h1
                wbr.rel(bk, [tmm])
            h2_free = [tmm]
            for nch in range(8):
                st_toks = []
                pst = []
                for si, (t0, n, j) in enumerate(grp):
                    ps, pf, pk = psr.get()
                    pst.append((ps, pf, pk))
                for q in range(4):
                    wb, tc_, bk = wst.next()
                    for si, (t0, n, j) in enumerate(grp):
                        o = t0 - g0
                        ps, pf, pk = pst[si]
                        for fi in range(8):
                            f_ = q * 8 + fi
                            tmm = P.mm(ps[:, 0:n], wb[:, fi * 128:(fi + 1) * 128], h1[:, f_, o:o + n], start=(f_ == 0), stop=(f_ == 31),
                                       waits=[tc_, last1] + pf, inc=(fi == 7))
                    wbr.rel(bk, [tmm])
                for si, (t0, n, j) in enumerate(grp):
                    o = t0 - g0
                    ps, pf, pk = pst[si]
                    t_x = P.stt(xmid[:, nch, o:o + n], ps[:, 0:n], modv[:, 5, nch, j:j + 1], xmid[:, nch, o:o + n], ALU.mult, ALU.add, waits=[tmm])
                    psr.rel(pk, [t_x])
                    if l == 0 or j == 0:
                        st_toks.append(P.dma(dst[nch * 128:(nch + 1) * 128, t0 - dstoff:t0 - dstoff + n], xmid[:, nch, o:o + n], waits=[t_x], en="pool"))
                    else:
                        st_toks.append(t_x)
                if gi_ + 1 < len(groups):
                    ldx_next[nch] = load_x_one(groups[gi_ + 1], nch, st_toks)
            h1_free = [tmm]
            ldx_n = ldx_next
        P.emit()


def block_fullattn(nc, P, Dm):
    SHIFT = -4.0
    with ExitStack() as st:
        A = Alloc(nc, st)
        KT = A.sb([128, 2, T], BF16, "KT")
        V = A.sb([128, NT, 256], BF16, "V")
        onb = A.sb([128, 128], BF16, "onb")
        shb = A.sb([128, 1], F32, "shb")
        t_on = P.memset("pool", onb[:], 1.0)
        t_sh = P.memset("pool", shb[:], SHIFT)
        l_k = [P.dma(KT[:, i, :], Dm["KC_T"][i * 128:(i + 1) * 128, :]) for i in range(2)]
        l_v = P.dma(V[:], Dm["VC"].rearrange("(c p) n -> p c n", p=128))
        lc = l_k + [l_v, t_on, t_sh]
        qr = A.ring(2, [128, 512], BF16, "q")
        psS = A.ring(2, [128, 1024], F32, "psS", psum=True)
        psO = A.ring(2, [128, 512], F32, "psO", psum=True)
        psD = A.ring(2, [128, 512], F32, "psD", psum=True)
        PTr = A.ring(3, [128, 1024], BF16, "PT")
        rr = A.ring(2, [128, 512], F32, "r")
        obr = A.ring(2, [128, 512], BF16, "ob")
        scale = 128.0 ** -0.5
        onf = A.sb([128, 128], F32, "onf")
        t_onf = P.memset("pool", onf[:], 1.0)
        accr = A.ring(2, [128, 1024], F32, "acc")
        NP = NT // 2
        for h in range(8):
            kv = h // 4
            for g in range(8):
                q, qf, qk = qr.get()
                l_q = P.dma(q[:], Dm["QC_T"][h * 128:(h + 1) * 128, g * 512:(g + 1) * 512], waits=qf)
                po, pof, pok = psO.get()
                pd, pdf, pdk = psD.get()
                acc, accf, acck = accr.get()
                pend = None
                st8 = {"first_d": True, "first_a": True, "t_acc": None}

                def pv(pend, last):
                    PT, tk_, j_, ptk = pend
                    rel = []
                    for hf in range(2):
                        kb_ = 2 * j_ + hf
                        first = (kb_ == 0)
                        t_ = P.mm(po[:], V[:, kb_, kv * 128:(kv + 1) * 128], PT[:, hf * 512:(hf + 1) * 512], start=first, stop=(last and hf == 1),
                                  waits=[tk_] + (pof if first else []), inc=(hf == 1))
                    rel.append(t_)
                    if j_ % 3 == 0:
                        for hf in range(2):
                            t2_ = P.mm(pd[:], onb[:], PT[:, hf * 512:(hf + 1) * 512], start=st8["first_d"], stop=False,
                                       waits=(pdf if st8["first_d"] else []), inc=(hf == 1))
                            st8["first_d"] = False
                        rel.append(t2_)
                    else:
                        if st8["first_a"]:
                            t2_ = P.copy("dve", acc[:], PT[:], waits=[tk_] + accf)
                            st8["first_a"] = False
                        else:
                            t2_ = P.tt("dve", acc[:], acc[:], PT[:], ALU.add, waits=[tk_])
                        st8["t_acc"] = t2_
                        rel.append(t2_)
                    PTr.rel(ptk, rel)
                    return t_
                for j in range(NP):
                    ps, psf, psk = psS.get()
                    for hf in range(2):
                        kb = 2 * j + hf
                        t_s = P.mm(ps[:, hf * 512:(hf + 1) * 512], KT[:, kv, kb * 128:(kb + 1) * 128], q[:], waits=[l_q] + lc + psf, inc=(hf == 1))
                    PT, ptf, ptk = PTr.get()
                    t_e = P.act(PT[:], ps[:], AF.Exp, bias=shb[:, 0:1], scale=scale, waits=[t_s] + ptf)
                    psS.rel(psk, [t_e])
                    if pend is not None:
                        pv(pend, False)
                    pend = (PT, t_e, j, ptk)
                pv(pend, True)
                P.mm(pd[:], onf[:], acc[:, 0:512], start=False, stop=False, waits=[st8["t_acc"], t_onf])
                t_last = P.mm(pd[:], onf[:], acc[:, 512:1024], start=False, stop=True, inc=True)
                accr.rel(acck, [t_last])
                qr.rel(qk, [t_last])
                r_, rf, rk = rr.get()
                t_r = P.recip(r_[:], pd[:], waits=[t_last] + rf)
                psD.rel(pdk, [t_r])
                ob, of, ok = obr.get()
                t_o = P.tt("dve", ob[:], po[:], r_[:], ALU.mult, waits=[t_r] + of)
                psO.rel(pok, [t_o])
                rr.rel(rk, [t_o])
                td = P.dma(Dm["MIX1_T"][h * 128:(h + 1) * 128, g * 512:(g + 1) * 512], ob[:], waits=[t_o])
                obr.rel(ok, [td])
        P.emit()


IN_SPECS = {
    "XIN": ([1024, T], F32), "CV": ([128, 8, 2], F32), "ADAW": ([2, 1024, 6144], F32), "ADABR": ([2, 6144], F32),
    "NG": ([128, 2, 2, 8], F32),
    "WF0": ([14, 128, 1024], F32), "WT0": ([128, 8, 1680], F32), "GATEB": ([1, 16], F32), "MLNG": ([1, 512], F32),
    "QKG0": ([128, 4], F32), "SINKP": ([128, 4], F32), "COS0": ([128, T], F32), "SIN0": ([128, T], F32), "BLK0": ([128, 128], F32),
    "WOUT0": ([128, 8, 1024], F32), "W1_0": ([32, 128, 1024], F32), "W2_0": ([8, 4, 128, 1024], F32),
    "WF1": ([10, 128, 1024], F32), "WT1": ([128, 8, 256], F32), "QKG1": ([128, 4], F32), "COS1": ([128, T], F32), "SIN1": ([128, T], F32),
    "BLK1": ([128, 128], F32), "WOUT1": ([128, 8, 1024], F32), "W1_1": ([32, 128, 1024], F32), "W2_1": ([8, 4, 128, 1024], F32),
    "ONES": ([128, 128], F32), "IDENT": ([128, 128], F32), "UTRI": ([128, 128], F32), "LTRI": ([128, 128], F32),
    "NEGP": ([128, 128], F32), "NEGN": ([128, 128], F32), "HMASK": ([128, 2], F32),
}
SCRATCH = {
    "MODV": ([128, 2, 6, 8, 2], F32),
    "QA_T": ([512, T], BF16), "KA_T": ([512, T], BF16), "KA": ([T, 512], BF16), "VA": ([T, 512], BF16), "SOA": ([T, 512], F32),
    "GA": ([T, 16], F32), "VB": ([T, 128], BF16), "QB_T": ([512, T], BF16), "KB_T": ([256, T], BF16),
    "MIX_T": ([1024, T], BF16), "X1": ([1024, T], F32),
    "QC_T": ([1024, T_LAT], BF16), "KC_T": ([256, T], BF16), "VC": ([T, 256], BF16), "MIX1_T": ([1024, T_LAT], BF16),
}
STAGE_INPUTS = {
    "mod": ["CV", "ADAW", "ADABR", "NG", "IDENT"],
    "in0": ["XIN", "ONES", "WF0", "WT0", "GATEB", "QKG0", "COS0", "SIN0", "BLK0"],
    "mlstm": ["ONES", "UTRI", "LTRI", "IDENT", "MLNG"],
    "win": ["IDENT", "NEGP", "NEGN", "HMASK", "SINKP"],
    "mlp0": ["XIN", "ONES", "WOUT0", "W1_0", "W2_0"],
    "in1": ["ONES", "WF1", "WT1", "QKG1", "COS1", "SIN1", "BLK1"],
    "attn": [],
    "mlp1": ["ONES", "WOUT1", "W1_1", "W2_1"],
}
ALL_STAGES = ("mod", "in0", "mlstm", "win", "mlp0", "in1", "attn", "mlp1")


def build(stages=ALL_STAGES, dbg=()):
    nc = bass.Bass("TRN2", target_bir_lowering=False)
    Dm = {}
    need = set()
    for sname in stages:
        need |= set(STAGE_INPUTS[sname])
    for k, (shape, dt) in IN_SPECS.items():
        if k in need:
            Dm[k] = nc.dram_tensor(k, shape, dt, kind="ExternalInput").ap()
    for k, (shape, dt) in SCRATCH.items():
        kind = "ExternalOutput" if k in dbg else ("ExternalInput" if ("in:" + k) in dbg else "Internal")
        Dm[k] = nc.dram_tensor(k, shape, dt, kind=kind).ap()
    Dm["OUT_T"] = nc.dram_tensor("OUT_T", [1024, T_LAT], F32, kind="ExternalOutput").ap()
    if "D_G" in dbg:
        Dm["D_G"] = nc.dram_tensor("D_G", [128, 6, 272], F32, kind="ExternalOutput").ap()
        Dm["D_B"] = nc.dram_tensor("D_B", [128, 3, 2, 128], BF16, kind="ExternalOutput").ap()
        Dm["D_F"] = nc.dram_tensor("D_F", [128, 4, 2, 128], F32, kind="ExternalOutput").ap()
        Dm["D_S"] = nc.dram_tensor("D_S", [128, 4, 8], F32, kind="ExternalOutput").ap()
        Dm["D_VC"] = nc.dram_tensor("D_VC", [128, 2 * NT * 129], BF16, kind="ExternalOutput").ap()
        Dm["D_CB"] = nc.dram_tensor("D_CB", [128, 2 * NT * 129], BF16, kind="ExternalOutput").ap()
    with ExitStack() as st:
        P = Prog(nc, st)
        if "mod" in stages:
            with nc.named_scope("mod"):
                block_mod(nc, P, Dm)
        if "in0" in stages:
            with nc.named_scope("in0"):
                block_inproj(nc, P, Dm, 0)
        if "mlstm" in stages:
            with nc.named_scope("mlstm"):
                block_mlstm(nc, P, Dm)
        if "win" in stages:
            with nc.named_scope("win"):
                block_window(nc, P, Dm)
        if "mlp0" in stages:
            with nc.named_scope("mlp0"):
                block_outmlp(nc, P, Dm, 0)
        if "in1" in stages:
            with nc.named_scope("in1"):
                block_inproj(nc, P, Dm, 1)
        if "attn" in stages:
            with nc.named_scope("attn"):
                block_fullattn(nc, P, Dm)
        if "mlp1" in stages:
            with nc.named_scope("mlp1"):
                block_outmlp(nc, P, Dm, 1)
    return nc


def _fm(w, cols):
    sub = w[:, cols]
    return np.ascontiguousarray(sub.reshape(8, 128, 128).transpose(1, 0, 2).reshape(128, 1024))


def _tm(w, cols):
    sub = w[:, cols]
    return np.ascontiguousarray(sub.reshape(8, 128, len(cols)).transpose(1, 0, 2))


def _rope_tables(dh, quarter_layout):
    pairs = dh // 4
    inv = (np.float32(10000.0) ** (-np.arange(pairs, dtype=np.float32) / np.float32(pairs))).astype(np.float32)
    tt = np.arange(T_LAT)
    row = (tt // 64).astype(np.float32)
    col = (tt % 64).astype(np.float32)
    ang = np.concatenate([row[:, None] * inv[None, :], col[:, None] * inv[None, :]], axis=1).astype(np.float32)
    half = dh // 2
    cos = np.ones((128, T), np.float32)
    sin = np.zeros((128, T), np.float32)
    for p in range(128):
        j = p % half
        sgn = -1.0 if p < 64 else 1.0
        cos[p, T_CTX:] = np.cos(ang[:, j])
        sin[p, T_CTX:] = sgn * np.sin(ang[:, j])
    return cos, sin


def host_shared(inp):
    f = lambda a: np.ascontiguousarray(np.asarray(a, dtype=np.float32))
    S = {}
    S["ADAW"] = f(inp["ada_w"])
    S["ADABR"] = f(inp["ada_b"])
    ng = np.stack([inp["norm1_g"].reshape(2, 8, 128), inp["norm2_g"].reshape(2, 8, 128)], axis=1)
    S["NG"] = f(ng.transpose(3, 0, 1, 2))
    W = np.asarray(inp["ab_w_in"][0], np.float32)
    chunks = [np.arange(i * 128, (i + 1) * 128) for i in range(4)] + [512 + np.arange(i * 128, (i + 1) * 128) for i in range(4)]
    p = np.arange(128)
    quarter, r = p // 32, p % 32
    dimp = (quarter // 2) * 32 + r
    for i in range(4):
        chunks.append(2064 + (2 * i + quarter % 2) * 64 + dimp)
    for kv in range(2):
        chunks.append(2576 + kv * 64 + dimp)
    S["WF0"] = np.stack([_fm(W, c) for c in chunks])
    tcols = np.concatenate([np.arange(512, 1024), np.arange(1024, 1536), np.arange(1536, 2048), np.arange(2048, 2064), np.arange(2704, 2832)])
    S["WT0"] = _tm(W, tcols)
    S["GATEB"] = f(inp["ab_gate_b"][0][None, :])
    S["MLNG"] = f(inp["mlstm_norm_g"][0][None, :])
    gq = np.asarray(inp["swa_q_norm_g"][0], np.float32)[dimp]
    gk = np.asarray(inp["swa_k_norm_g"][0], np.float32)[dimp]
    sh = (p + 64) % 128
    S["QKG0"] = f(np.stack([gq, gq[sh], gk, gk[sh]], axis=1))
    sink = np.asarray(inp["swa_sink"][0], np.float32)
    S["SINKP"] = f(np.stack([sink[2 * i + (p >= 64)] for i in range(4)], axis=1))
    S["COS0"], S["SIN0"] = _rope_tables(64, True)
    S["BLK0"] = f(((p[:, None] // 32) % 2 == (p[None, :] // 32) % 2))
    S["WOUT0"] = _tm(np.asarray(inp["ab_w_out"][0], np.float32), np.arange(1024))
    W1 = np.asarray(inp["c_w_in"][0], np.float32)
    S["WF1"] = np.stack([_fm(W1, np.arange(i * 128, (i + 1) * 128)) for i in range(10)])
    S["WT1"] = _tm(W1, np.arange(1280, 1536))
    gq1 = np.asarray(inp["c_q_norm_g"][0], np.float32)
    gk1 = np.asarray(inp["c_k_norm_g"][0], np.float32)
    S["QKG1"] = f(np.stack([gq1, gq1[sh], gk1, gk1[sh]], axis=1))
    S["COS1"], S["SIN1"] = _rope_tables(128, False)
    S["BLK1"] = np.ones((128, 128), np.float32)
    S["WOUT1"] = _tm(np.asarray(inp["c_w_out"][0], np.float32), np.arange(1024))
    for l in range(2):
        w1 = np.asarray(inp["mlp_w1"][l], np.float32)
        S["W1_%d" % l] = np.ascontiguousarray(w1.reshape(8, 128, 32, 128).transpose(2, 1, 0, 3).reshape(32, 128, 1024))
        w2 = np.asarray(inp["mlp_w2"][l], np.float32)
        S["W2_%d" % l] = np.ascontiguousarray(w2.reshape(4, 8, 128, 8, 128).transpose(3, 0, 2, 1, 4).reshape(8, 4, 128, 1024))
    S["ONES"] = np.ones((128, 128), np.float32)
    S["IDENT"] = np.eye(128, dtype=np.float32)
    S["UTRI"] = np.triu(np.ones((128, 128), np.float32))
    S["LTRI"] = np.tril(np.ones((128, 128), np.float32))
    s_, t_ = p[:, None], p[None, :]
    S["NEGP"] = np.where(t_ <= s_, 0.0, NEG).astype(np.float32)
    S["NEGN"] = np.where(s_ <= t_, 0.0, NEG).astype(np.float32)
    S["HMASK"] = f(np.stack([(p // 32) % 2 == 0, (p // 32) % 2 == 1], axis=1))
    return S


def host_core(inp, b):
    x = np.asarray(inp["x"][b], np.float32)
    ctx = np.asarray(inp["ctx"][b], np.float32)
    m = {"XIN": np.ascontiguousarray(np.concatenate([ctx.T, x.T], axis=1))}
    cv = np.stack([np.asarray(inp["c"][b], np.float32).reshape(8, 128), np.asarray(inp["c_ctx"], np.float32).reshape(8, 128)], axis=-1)
    m["CV"] = np.ascontiguousarray(cv.transpose(1, 0, 2))
    return m


_NC_CACHE = {}


def kernel(**inputs):
    S = host_shared(inputs)
    in_maps = []
    for b in range(8):
        m = dict(S)
        m.update(host_core(inputs, b))
        in_maps.append(m)
    if "nc" not in _NC_CACHE:
        _NC_CACHE["nc"] = build()
    res = run_bass_kernel_spmd(_NC_CACHE["nc"], in_maps, core_ids=list(range(8)))
    out = np.stack([np.ascontiguousarray(res.results[b]["OUT_T"].T) for b in range(8)], axis=0)
    return out.astype(np.float32)
```

```python
import numpy as np
from contextlib import ExitStack
import concourse.bass as bass
import concourse.mybir as mybir
from concourse.bass_utils import run_bass_kernel_spmd
import ml_dtypes

F32 = mybir.dt.float32
BF16 = mybir.dt.bfloat16
AF = mybir.ActivationFunctionType
ALU = mybir.AluOpType

D = 1024
T_LAT = 4096
T_CTX = 256
T = T_LAT + T_CTX
NT = T // 128
EPS = 1e-6
NEG = -30000.0
DEBUG_OUT = False


class Prog:
    ENG = ["pe", "act", "dve", "pool", "sp"]
    NSLOT = 16
    NEPOCH = 6
    EPOCH_LEN = 16000

    def __init__(self, nc, st):
        self.nc = nc
        self.ops = {e: [] for e in self.ENG}
        self.cnt = {e: 0 for e in self.ENG}
        self.epoch = {e: 0 for e in self.ENG}
        self.waited = {e: {} for e in self.ENG}
        self.ndma = 0
        self.dma_cnt = {}
        self.rcnt = 0
        self.sems = {}
        for e in self.ENG:
            for ep in range(self.NEPOCH):
                self.sems[(e, ep)] = st.enter_context(nc.semaphore("s_%s_%d" % (e, ep)))
        for s in range(self.NSLOT):
            self.sems[("dma", s)] = st.enter_context(nc.semaphore("s_dma_%d" % s))
            self.sems[("dmp", s)] = st.enter_context(nc.semaphore("s_dmp_%d" % s))
        self.ndma_p = 0
        self.sems[("recip", 0)] = st.enter_context(nc.semaphore("s_recip"))
        self.all_dma = []
        self.pending = []
        self.pending_toks = set()

    def _waits(self, en, waits, force=False):
        ws = []
        for tok in waits:
            if tok is None:
                continue
            if isinstance(tok, list):
                ws += self._waits(en, tok, force)
                continue
            key, val = tok
            if key[0] == en and not force:
                continue
            if self.waited[en].get(key, 0) >= val:
                continue
            self.waited[en][key] = val
            ws.append((key, val))
        return ws

    def op(self, en, fn, waits=(), inc=True, recip=False, force=False):
        ws = self._waits(en, waits, force=(recip or force))
        tok = None
        if recip:
            self.rcnt += 1
            tok = (("recip", 0), self.rcnt)
        elif inc:
            if self.cnt[en] >= self.EPOCH_LEN:
                self.epoch[en] += 1
                self.cnt[en] = 0
            self.cnt[en] += 1
            tok = ((en, self.epoch[en]), self.cnt[en])
        self.ops[en].append((ws, fn, tok))
        return tok

    def dma(self, out, in_, waits=(), en="sp", store=None, **kw):
        if store is None:
            store = "dram" in type(out.tensor).__name__.lower()
        if en == "pool":
            slot = self.ndma_p % self.NSLOT
            self.ndma_p += 1
            key = ("dmp", slot)
            prev = self.dma_cnt.get(key, 0)
            w = [t for t in self._flat(waits)]
            if prev > 0:
                w.append((key, prev))
            self.dma_cnt[key] = prev + 16
            tok = (key, prev + 16)
            self.all_dma.append(tok)
            self.ops["pool"].append((self._waits("pool", w), ("dma", out, in_, kw), tok))
            return tok
        slot = self.ndma % self.NSLOT
        self.ndma += 1
        key = ("dma", slot)
        prev = self.dma_cnt.get(key, 0)
        w = [t for t in self._flat(waits)]
        if prev > 0:
            w.append((key, prev))
        self.dma_cnt[key] = prev + 16
        tok = (key, prev + 16)
        self.all_dma.append(tok)
        rec = (w, ("dma", out, in_, kw), tok)
        if store:
            self.pending.append(rec)
            self.pending_toks.add(tok)
        else:
            if any(t in self.pending_toks for t in w):
                self.flush_stores()
            self._push_dma(en, rec)
            self.flush_stores()
        return tok

    def _flat(self, waits):
        for t in waits:
            if t is None:
                continue
            if isinstance(t, list):
                yield from self._flat(t)
            else:
                yield t

    def _push_dma(self, en, rec):
        w, fn, tok = rec
        self.ops[en].append((self._waits(en, w), fn, tok))

    def flush_stores(self):
        for rec in self.pending:
            self._push_dma("sp", rec)
        self.pending = []
        self.pending_toks = set()

    def mm(self, out, lhsT, rhs, start=True, stop=True, waits=(), inc=False):
        return self.op("pe", lambda e: e.matmul(out, lhsT=lhsT, rhs=rhs, start=start, stop=stop), waits, inc)

    def transpose(self, out, in_, ident, waits=(), inc=True):
        return self.op("pe", lambda e: e.transpose(out=out, in_=in_, identity=ident), waits, inc)

    def act(self, out, in_, func, bias=None, scale=None, accum_out=None, waits=(), inc=True, force=False):
        kw = {}
        if bias is not None:
            kw["bias"] = bias
        if scale is not None:
            kw["scale"] = scale
        if accum_out is not None:
            kw["accum_out"] = accum_out
        return self.op("act", lambda e: e.activation(out=out, in_=in_, func=func, **kw), waits, inc, force=force)

    def ts(self, en, out, in0, s1, s2=None, op0=ALU.mult, op1=None, waits=(), inc=True):
        if op1 is None:
            return self.op(en, lambda e: e.tensor_scalar(out=out, in0=in0, scalar1=s1, scalar2=None, op0=op0), waits, inc)
        return self.op(en, lambda e: e.tensor_scalar(out=out, in0=in0, scalar1=s1, scalar2=s2, op0=op0, op1=op1), waits, inc)

    def tt(self, en, out, in0, in1, op, waits=(), inc=True):
        return self.op(en, lambda e: e.tensor_tensor(out=out, in0=in0, in1=in1, op=op), waits, inc)

    def stt(self, out, in0, scalar, in1, op0, op1, waits=(), inc=True):
        return self.op("dve", lambda e: e.scalar_tensor_tensor(out=out, in0=in0, scalar=scalar, in1=in1, op0=op0, op1=op1), waits, inc)

    def copy(self, en, out, in_, waits=(), inc=True):
        if en == "act":
            return self.act(out, in_, AF.Copy, waits=waits, inc=inc)
        return self.op(en, lambda e: e.tensor_copy(out=out, in_=in_), waits, inc)

    def memset(self, en, ap, val, waits=(), inc=True):
        return self.op(en, lambda e: e.memset(ap, val), waits, inc)

    def recip(self, out, in_, waits=()):
        return self.op("dve", lambda e: e.reciprocal(out=out, in_=in_), waits, recip=True)

    def emit(self, extra_final=()):
        nc = self.nc
        self.flush_stores()
        finals = list(self.all_dma) + [t for t in extra_final if t is not None]
        self.all_dma = []
        fin = {en: self._waits(en, finals) for en in self.ENG}
        ops = self.ops
        self.ops = {e: [] for e in self.ENG}
        sems = self.sems
        with nc.Block() as block:
            def run(en, eng):
                for ws, fn, tok in ops[en]:
                    for key, val in ws:
                        eng.wait_ge(sems[key], val)
                    if isinstance(fn, tuple):
                        _, out, in_, kw = fn
                        ins = eng.dma_start(out=out, in_=in_, **kw)
                        ins.then_inc(sems[tok[0]], 16)
                    else:
                        ins = fn(eng)
                        if tok is not None:
                            ins.then_inc(sems[tok[0]], 1)
                for key, val in fin[en]:
                    eng.wait_ge(sems[key], val)

            @block.tensor
            def _(e):
                run("pe", e)

            @block.scalar
            def _(e):
                run("act", e)

            @block.vector
            def _(e):
                run("dve", e)

            @block.gpsimd
            def _(e):
                run("pool", e)

            @block.sync
            def _(e):
                run("sp", e)


class Slot:
    def __init__(self, buf):
        self.buf = buf
        self.free = []


class Ring:
    def __init__(self, bufs):
        self.slots = [b if isinstance(b, Slot) else Slot(b) for b in bufs]
        self.i = 0

    @property
    def bufs(self):
        return [s_.buf for s_ in self.slots]

    def get(self):
        k = self.i % len(self.slots)
        self.i += 1
        return self.slots[k].buf, self.slots[k].free, k

    def rel(self, k, toks):
        self.slots[k].free = [t for t in toks if t is not None]


class Alloc:
    def __init__(self, nc, st):
        self.nc, self.st, self.n = nc, st, 0

    def sb(self, shape, dt, name=None):
        self.n += 1
        return self.st.enter_context(self.nc.sbuf_tensor("%s_%d_%d" % (name or "t", id(self) % 9973, self.n), list(shape), dt))

    def ps(self, shape=(128, 512), dt=F32, name=None):
        self.n += 1
        return self.st.enter_context(self.nc.psum_tensor("%s_%d_%d" % (name or "p", id(self) % 9973, self.n), list(shape), dt))

    def ring(self, n, shape, dt, name=None, psum=False):
        return Ring([(self.ps(shape, dt, name) if psum else self.sb(shape, dt, name)) for _ in range(n)])


class WStream:
    def __init__(self, P, wfr, wbr, srcs, depth=2, eng="act"):
        self.P, self.wfr, self.wbr, self.srcs, self.depth, self.eng = P, wfr, wbr, srcs, depth, eng
        self.q = []
        self.i = 0

    def _issue(self):
        P = self.P
        wf, ff, fk = self.wfr.get()
        ld = P.dma(wf[:], self.srcs[self.i], waits=ff)
        wb, bf_, bk = self.wbr.get()
        tc = P.copy(self.eng, wb[:], wf[:], waits=[ld] + bf_)
        self.wfr.rel(fk, [tc])
        self.q.append((wb, tc, bk))
        self.i += 1

    def next(self):
        while len(self.q) < self.depth + 1 and self.i < len(self.srcs):
            self._issue()
        return self.q.pop(0)


def subgroups(t0, t1, maxn=512):
    out = []
    t = t0
    while t < t1:
        n = min(maxn, t1 - t)
        out.append((t, n))
        t += n
    return out


def block_mod(nc, P, Dm):
    with ExitStack() as st:
        A = Alloc(nc, st)
        cv = A.sb([128, 8, 2], F32)
        sg = A.sb([128, 8, 2], F32)
        sc = A.sb([128, 8, 2], F32)
        adabr = A.sb([2, 2, 6144], F32)
        ng = A.sb([128, 2, 2, 8], F32)
        modrow = A.sb([2, 2, 6144], F32)
        ident = A.sb([128, 128], F32)
        modsb = A.sb([128, 2, 48, 2], F32)
        modv = A.sb([128, 2, 6, 8, 2], F32)
        awr = A.ring(2, [128, 8, 1536], F32, "aw")
        psr = A.ring(2, [128, 512], F32, "psmr", psum=True)
        pst = A.ps([128, 2, 48, 2], F32)
        l_cv = P.dma(cv[:], Dm["CV"])
        l_ab = [P.dma(adabr[:, l, :], Dm["ADABR"][l:l + 1, :].partition_broadcast(2)) for l in range(2)]
        l_ng = P.dma(ng[:], Dm["NG"])
        l_id = P.dma(ident[:], Dm["IDENT"])
        t_sc = P.tt("dve", sc[:], cv[:], sg[:], ALU.mult, waits=[P.act(sg[:], cv[:], AF.Sigmoid, waits=[l_cv])])
        t_ev = None
        for l in range(2):
            for q in range(4):
                aw, wfree, k = awr.get()
                src = Dm["ADAW"][l].rearrange("(kc p) n -> p kc n", p=128)
                ld = [P.dma(aw[:, 0:4, :], src[:, 0:4, q * 1536:(q + 1) * 1536], waits=wfree),
                      P.dma(aw[:, 4:8, :], src[:, 4:8, q * 1536:(q + 1) * 1536], waits=wfree)]
                for j in range(3):
                    ps, pf, pk = psr.get()
                    for kc in range(8):
                        tmm = P.mm(ps[0:2, :], sc[:, kc, :], aw[:, kc, j * 512:(j + 1) * 512], start=(kc == 0), stop=(kc == 7),
                                   waits=ld + [t_sc] + pf, inc=(kc == 7))
                    c0 = q * 1536 + j * 512
                    t_ev = P.tt("dve", modrow[:, l, c0:c0 + 512], ps[0:2, :], adabr[:, l, c0:c0 + 512], ALU.add, waits=[tmm] + l_ab)
                    psr.rel(pk, [t_ev])
                awr.rel(k, [tmm])
        t_T = None
        for l in range(2):
            for ch in range(48):
                t_T = P.transpose(pst[:, l, ch, :], modrow[:, l, ch * 128:(ch + 1) * 128], ident[0:2, 0:2], waits=[t_ev, l_id], inc=(l == 1 and ch == 47))
        t_ms = P.copy("dve", modsb[:].rearrange("p l c j -> p (l c j)"), pst[:].rearrange("p l c j -> p (l c j)"), waits=[t_T])
        for l in range(2):
            for j in range(2):
                for (kind, c0) in ((0, 0), (2, 16), (3, 24), (5, 40)):
                    P.copy("dve", modv[:, l, kind, :, j], modsb[:, l, c0:c0 + 8, j], inc=False)
                P.stt(modv[:, l, 1, :, j], modsb[:, l, 8:16, j], 1.0, ng[:, l, 0, :], ALU.add, ALU.mult, waits=[l_ng], inc=False)
                tk = P.stt(modv[:, l, 4, :, j], modsb[:, l, 32:40, j], 1.0, ng[:, l, 1, :], ALU.add, ALU.mult)
        P.dma(Dm["MODV"], modv[:], waits=[tk])
        P.emit()


def norm_modulate(nc, P, xsrc, hT, modv, kind_shift, kind_gs, sgs, consts, l_consts, col0=0):
    with ExitStack() as st:
        A = Alloc(nc, st)
        xr = A.ring(3, [128, 8, 512], F32, "x")
        sqr = A.ring(3, [128, 512], BF16, "sq")
        rsr = A.ring(3, [128, 512], F32, "rs")
        tmr = A.ring(3, [128, 512], F32, "tm")
        pss = A.ring(2, [128, 512], F32, "pss", psum=True)
        onesb = A.sb([128, 128], BF16, "onesb")
        t_ob = P.memset("pool", onesb[:], 1.0)
        epsb = consts["eps"]
        xs = xsrc.rearrange("(kc p) t -> p kc t", p=128)
        last = [None, None]

        def stage1(t0, n, j):
            xt, xfree, xk = xr.get()
            ld = P.dma(xt[:, :, 0:n], xs[:, :, t0:t0 + n], waits=xfree)
            ps, pfree, pk = pss.get()
            for kc in range(8):
                sq, sfree, sk = sqr.get()
                tq = P.act(sq[:, 0:n], xt[:, kc, 0:n], AF.Square, waits=[ld] + sfree)
                tm_ = P.mm(ps[:, 0:n], onesb[:], sq[:, 0:n], start=(kc == 0), stop=(kc == 7), waits=[tq, t_ob, l_consts] + pfree, inc=True)
                sqr.rel(sk, [tm_])
            rs, rfree, rk = rsr.get()
            P.act(rs[:, 0:n], ps[:, 0:n], AF.Ln, bias=epsb[:, 0:1], scale=1.0 / D, waits=[tm_] + rfree, inc=False)
            t_rs = P.act(rs[:, 0:n], rs[:, 0:n], AF.Exp, scale=-0.5)
            pss.rel(pk, [t_rs])
            return (t0, n, j, xt, xk, ld, rs, rk, t_rs)

        def stage2(S):
            t0, n, j, xt, xk, ld, rs, rk, t_rs = S
            for kc in range(8):
                tm, tfree, tk = tmr.get()
                t1 = P.stt(tm[:, 0:n], xt[:, kc, 0:n], modv[:, kind_gs, kc, j:j + 1], rs[:, 0:n], ALU.mult, ALU.mult, waits=[t_rs, ld] + tfree)
                if kc < 2:
                    t2 = P.ts("pool", hT[:, kc, t0 - col0:t0 - col0 + n], tm[:, 0:n], modv[:, kind_shift, kc, j:j + 1], op0=ALU.add, waits=[t1])
                else:
                    t2 = P.act(hT[:, kc, t0 - col0:t0 - col0 + n], tm[:, 0:n], AF.Identity, bias=modv[:, kind_shift, kc, j:j + 1], waits=[t1])
                tmr.rel(tk, [t2])
                last[0 if kc < 2 else 1] = t2
            xr.rel(xk, [t1])
            rsr.rel(rk, [t1])

        prev = None
        for sg in sgs:
            cur = stage1(*sg)
            if prev is not None:
                stage2(prev)
            prev = cur
        stage2(prev)
        P.emit()
    return last


def load_consts(nc, P, A, Dm, names):
    out, toks = {}, []
    for nm in names:
        src = Dm[nm]
        t = A.sb(list(src.shape), F32, nm)
        toks.append(P.dma(t[:], src))
        out[nm] = t
    return out, toks


IN_PARTS = ("tm", "copy", "rope")
ROPE_LVL = 4


def block_inproj(nc, P, Dm, l):
    xsrc = Dm["XIN"] if l == 0 else Dm["X1"]
    with ExitStack() as st:
        A = Alloc(nc, st)
        hT = A.sb([128, 8, T], BF16, "hT")
        ntm = 1680 if l == 0 else 256
        WT = A.sb([128, 8, ntm], BF16, "WT")
        modv = A.sb([128, 6, 8, 2], F32, "modv")
        ones = A.sb([128, 128], F32, "ones")
        epsb = A.sb([128, 1], F32, "eps")
        l_mod = P.dma(modv[:], Dm["MODV"][:, l])
        l_ones = P.dma(ones[:], Dm["ONES"])
        t_eps = P.memset("pool", epsb[:], EPS)
        consts = {"ones": ones, "eps": epsb}
        with ExitStack() as st2:
            A2 = Alloc(nc, st2)
            wsr = A2.ring(2, [128, ntm], F32, "wts")
            wt_toks = []
            for kc in range(8):
                ws, wfree, wk = wsr.get()
                ld = P.dma(ws[:], Dm["WT%d" % l][:, kc, :], waits=wfree)
                tk = P.copy("act", WT[:, kc, :], ws[:], waits=[ld])
                wsr.rel(wk, [tk])
                wt_toks.append(tk)
            sgs = [(0, 256, 1)] + [(t0, n, 0) for (t0, n) in subgroups(T_CTX, T)]
            t_h = norm_modulate(nc, P, xsrc, hT, modv, 0, 1, sgs, consts, [l_mod, l_ones, t_eps])
        with ExitStack() as st3:
            A3 = Alloc(nc, st3)
            dh = 64 if l == 0 else 128
            cosT = A3.sb([128, T], F32, "cos")
            sinT = A3.sb([128, T], F32, "sin")
            blk = A3.sb([128, 128], F32, "blk")
            gq = A3.sb([128, 4], F32, "gq")
            l_c = [P.dma(cosT[:], Dm["COS%d" % l]), P.dma(sinT[:], Dm["SIN%d" % l]), P.dma(blk[:], Dm["BLK%d" % l]),
                   P.dma(gq[:], Dm["QKG%d" % l])]
            wfr = A3.ring(4, [128, 1024], F32, "wf")
            wbr = A3.ring(4, [128, 1024], BF16, "wb")
            psr = A3.ring(3, [128, 512], F32, "psp", psum=True)
            ps2r = A3.ring(2, [128, 512], F32, "ps2", psum=True)
            stg = A3.ring(3, [128, 512], BF16, "stg")
            sqr = A3.ring(2, [128, 512], BF16, "sq")
            blkb = A3.sb([128, 128], BF16, "blkb")
            l_c.append(P.copy("pool", blkb[:], blk[:], waits=l_c))
            gqn = A3.sb([128, 4], F32, "gqn")
            l_c.append(P.ts("pool", gqn[:], gq[:], -1.0, op0=ALU.mult, waits=l_c))
            rsr = A3.ring(2, [128, 512], F32, "rs")
            ar = A3.ring(2, [128, 512], F32, "a")
            a0r = A3.ring(2, [128, 512], F32, "a0")
            br = A3.ring(2, [128, 512], F32, "b")
            tokd = [t_h] + wt_toks
            if "tm" not in IN_PARTS:
                pass
            elif l == 0:
                gb = A3.sb([128, 16], F32, "gb")
                l_gb = P.dma(gb[:], Dm["GATEB"].partition_broadcast(128))
                s_ka = A3.ring(2, [128, 512], BF16, "ska")
                s_va = A3.ring(2, [128, 512], BF16, "sva")
                s_oa = A3.ring(2, [128, 512], F32, "soa")
                s_gv = A3.ring(2, [128, 16], F32, "sga")
                s_vb = A3.ring(2, [128, 128], BF16, "svb")
                for c in range(NT):
                    tsl = slice(c * 128, (c + 1) * 128)
                    for gi, (c0, ncol) in enumerate(((0, 512), (512, 512), (1024, 512), (1536, 144))):
                        ps, pfree, pk = psr.get()
                        for kc in range(8):
                            tmm = P.mm(ps[:, 0:ncol], hT[:, kc, tsl], WT[:, kc, c0:c0 + ncol], start=(kc == 0), stop=(kc == 7),
                                       waits=tokd + pfree, inc=(kc == 7))
                        if gi == 0:
                            sbuf, sfree, sk = s_ka.get()
                            te = P.act(sbuf[:], ps[:, 0:512], AF.Copy, scale=128.0 ** -0.5, waits=[tmm] + sfree)
                            s_ka.rel(sk, [P.dma(Dm["KA"][tsl, :], sbuf[:], waits=[te])])
                        elif gi == 1:
                            sbuf, sfree, sk = s_va.get()
                            te = P.copy("dve", sbuf[:], ps[:, 0:512], waits=[tmm] + sfree)
                            s_va.rel(sk, [P.dma(Dm["VA"][tsl, :], sbuf[:], waits=[te])])
                        elif gi == 2:
                            sbuf, sfree, sk = s_oa.get()
                            te = P.act(sbuf[:], ps[:, 0:512], AF.Sigmoid, waits=[tmm] + sfree)
                            s_oa.rel(sk, [P.dma(Dm["SOA"][tsl, :], sbuf[:], waits=[te])])
                        else:
                            sbuf, sfree, sk = s_gv.get()
                            P.tt("dve", sbuf[:], ps[:, 0:16], gb[:], ALU.add, waits=[tmm, l_gb] + sfree, inc=False)
                            sb2, sfree2, sk2 = s_vb.get()
                            te = P.copy("dve", sb2[:], ps[:, 16:144], waits=sfree2)
                            s_gv.rel(sk, [P.dma(Dm["GA"][tsl, :], sbuf[:], waits=[te])])
                            s_vb.rel(sk2, [P.dma(Dm["VB"][tsl, :], sb2[:], waits=[te])])
                        psr.rel(pk, [te])
            else:
                s_v = A3.ring(2, [128, 256], BF16, "sv")
                for c in range(NT):
                    tsl = slice(c * 128, (c + 1) * 128)
                    ps, pfree, pk = psr.get()
                    for kc in range(8):
                        tmm = P.mm(ps[:, 0:256], hT[:, kc, tsl], WT[:, kc, :], start=(kc == 0), stop=(kc == 7), waits=tokd + pfree, inc=(kc == 7))
                    sbuf, sfree, sk = s_v.get()
                    te = P.copy("dve", sbuf[:], ps[:, 0:256], waits=[tmm] + sfree)
                    s_v.rel(sk, [P.dma(Dm["VC"][tsl, :], sbuf[:], waits=[te])])
                    psr.rel(pk, [te])
            if l == 0:
                specs = ([("copy", "QA_T", i, 1.0, True) for i in range(4)] + [("copy", "KA_T", i, 128.0 ** -0.5, True) for i in range(4)]
                         + [("rope", "QB_T", i, 0, True) for i in range(4)] + [("rope", "KB_T", i, 2, True) for i in range(2)])
            else:
                specs = [("rope", "QC_T", i, 0, False) for i in range(8)] + [("rope", "KC_T", i, 2, True) for i in range(2)]
            sg_all = [(0, 256)] + subgroups(T_CTX, T)
            ei = 0
            wst = WStream(P, wfr, wbr, [Dm["WF%d" % l][ci] for ci in range(len(specs))], depth=3)

            def rope_stage2(R):
                (n, t0, g0, a0, a0k, t_a0, sq, sqk, tq, dst_, di_, dcol) = R
                ps2, p2free, p2k = ps2r.get()
                tm2 = P.mm(ps2[:, 0:n], blkb[:], sq[:, 0:n], waits=[tq] + l_c + p2free, inc=True)
                sqr.rel(sqk, [tm2])
                rs, rfree, rk = rsr.get()
                P.act(rs[:, 0:n], ps2[:, 0:n], AF.Ln, bias=epsb[:, 0:1], scale=1.0 / dh, waits=[tm2] + rfree, inc=False)
                t_rs = P.act(rs[:, 0:n], rs[:, 0:n], AF.Exp, scale=-0.5)
                ps2r.rel(p2k, [t_rs])
                a, afree, ak = ar.get()
                b, bfree, bk = br.get()
                P.stt(a[:, 0:n], a0[:, 0:n], gq[:, g0:g0 + 1], cosT[:, t0:t0 + n], ALU.mult, ALU.mult, waits=[t_a0] + l_c + afree, inc=False)
                P.stt(b[0:64, 0:n], a0[64:128, 0:n], gqn[64:128, g0:g0 + 1], sinT[64:128, t0:t0 + n], ALU.mult, ALU.mult, waits=bfree, inc=False)
                t_b = P.stt(b[64:128, 0:n], a0[0:64, 0:n], gqn[0:64, g0:g0 + 1], sinT[0:64, t0:t0 + n], ALU.mult, ALU.mult)
                a0r.rel(a0k, [t_b, tq])
                t_ab = P.tt("dve", a[:, 0:n], a[:, 0:n], b[:, 0:n], ALU.add)
                ob, ofree, ok = stg.get()
                te = P.tt("pool", ob[:, 0:n], a[:, 0:n], rs[:, 0:n], ALU.mult, waits=[t_rs, t_ab] + ofree)
                ar.rel(ak, [te])
                br.rel(bk, [te])
                rsr.rel(rk, [te])
                td = P.dma(Dm[dst_][di_ * 128:(di_ + 1) * 128, dcol:dcol + n], ob[:, 0:n], waits=[te])
                stg.rel(ok, [td])

            pend_rope = None
            for ci, (kind, dst, di, par, with_ctx) in enumerate(specs):
                wb, tcast, wbk = wst.next()
                last_mm = None
                for (t0, n) in (sg_all if with_ctx else sg_all[1:]):
                    ps, pfree, pk = psr.get()
                    for kc in range(8):
                        tmm = P.mm(ps[:, 0:n], wb[:, kc * 128:(kc + 1) * 128], hT[:, kc, t0:t0 + n], start=(kc == 0), stop=(kc == 7),
                                   waits=[tcast, t_h] + pfree, inc=(kc == 7))
                    last_mm = tmm
                    dcol = t0 if with_ctx else t0 - T_CTX
                    if kind == "copy":
                        ob, ofree, ok = stg.get()
                        ei += 1
                        if ei % 2 == 0:
                            te = P.act(ob[:, 0:n], ps[:, 0:n], AF.Copy, scale=par, waits=[tmm] + ofree)
                        else:
                            te = P.ts("dve", ob[:, 0:n], ps[:, 0:n], par, op0=ALU.mult, waits=[tmm] + ofree)
                        psr.rel(pk, [te])
                        td = P.dma(Dm[dst][di * 128:(di + 1) * 128, dcol:dcol + n], ob[:, 0:n], waits=[te])
                        stg.rel(ok, [td])
                    else:
                        a0, a0free, a0k = a0r.get()
                        t_a0 = P.copy("act", a0[:, 0:n], ps[:, 0:n], waits=[tmm] + a0free)
                        psr.rel(pk, [t_a0])
                        sq, sqfree, sqk = sqr.get()
                        tq = P.act(sq[:, 0:n], a0[:, 0:n], AF.Square, waits=[t_a0] + sqfree)
                        if pend_rope is not None:
                            rope_stage2(pend_rope)
                        pend_rope = (n, t0, par, a0, a0k, t_a0, sq, sqk, tq, dst, di, dcol)
                wbr.rel(wbk, [last_mm])
            if pend_rope is not None:
                rope_stage2(pend_rope)
            P.emit()


def block_mlstm(nc, P, Dm):
    with ExitStack() as st:
        A = Alloc(nc, st)
        C, tl = load_consts(nc, P, A, Dm, ["ONES", "UTRI", "LTRI", "IDENT"])
        ones, utri, ltri, ident = C["ONES"], C["UTRI"], C["LTRI"], C["IDENT"]
        idb = A.sb([128, 128], BF16, "idb")
        t_idb = P.copy("pool", idb[:], ident[:], waits=tl)
        G = A.sb([128, NT, 16], F32, "G")
        LF = A.sb([128, 2, NT, 4], F32, "LF")
        LI = A.sb([128, 2, NT, 4], F32, "LI")
        CUM = A.sb([128, 2, NT, 4], F32, "CUM")
        COLF = A.sb([128, 2, NT, 4], F32, "COLF")
        ECOL = A.sb([128, 2, NT, 4], F32, "ECOL")
        CARRY = A.sb([128, 2, NT, 4], F32, "CARRY")
        mlg = A.sb([128, 512], F32, "mlg")
        epsb = A.sb([128, 1], F32, "eps")
        oneb = A.sb([128, 1], F32, "oneb")
        P.memset("pool", epsb[:], EPS, inc=False)
        t_c1 = P.memset("pool", oneb[:], 1.0)
        l_g = P.dma(G[:], Dm["GA"].rearrange("(c p) g -> p c g", p=128))
        l_mlg = P.dma(mlg[:], Dm["MLNG"].partition_broadcast(128))
        for d_, (ci, cf) in enumerate(((0, 4), (8, 12))):
            P.act(LF[:, d_], G[:, :, cf:cf + 4], AF.Exp, scale=-1.0, waits=[l_g], inc=False)
            P.act(LF[:, d_], LF[:, d_], AF.Ln, bias=oneb[:, 0:1], scale=1.0, waits=[t_c1], inc=False)
            t_lf = P.act(LF[:, d_], LF[:, d_], AF.Copy, scale=-1.0)
            t_li = P.copy("dve", LI[:, d_], G[:, :, ci:ci + 4], waits=[l_g])
        psg = A.ps([128, 512], F32, "psg")
        LFf = LF[:].rearrange("p d c h -> p d (c h)")
        P.mm(psg[:, 0:136], utri[:], LFf[:, 0, :], waits=[t_lf] + tl)
        P.mm(psg[:, 136:272], ltri[:], LFf[:, 1, :])
        t_cum = P.mm(psg[:, 272:408], ones[:], LFf[:, 0, :], inc=True)
        CUMf = CUM[:].rearrange("p d c h -> p (d c h)")
        t_cc = P.copy("dve", CUMf, psg[:, 0:272], waits=[t_cum])
        CARf = CARRY[:].rearrange("p d c h -> p d (c h)")
        t_e1 = P.act(CARf[:, 0, :], psg[:, 272:408], AF.Exp, waits=[t_cum, t_cc])
        t_cum2 = P.mm(psg[:, 272:408], ones[:], LFf[:, 1, :], waits=[t_e1, t_cc], inc=True)
        t_e2 = P.act(CARf[:, 1, :], psg[:, 272:408], AF.Exp, waits=[t_cum2])
        COLFf = COLF[:].rearrange("p d c h -> p (d c h)")
        LIf = LI[:].rearrange("p d c h -> p (d c h)")
        ECOLf = ECOL[:].rearrange("p d c h -> p (d c h)")
        P.tt("dve", COLFf, LIf, CUMf, ALU.subtract, waits=[t_li], inc=False)
        t_colf0 = P.copy("dve", ECOLf, CUMf)
        t_colf = P.act(COLFf, COLFf, AF.Exp, waits=[t_colf0])
        t_ecol = P.act(ECOLf, ECOLf, AF.Exp, scale=-1.0)
        gate_toks = [t_colf, t_ecol, t_e1, t_e2, t_idb]

        qTr = A.ring(2, [128, T], BF16, "qT")
        kTr = A.ring(2, [128, T], BF16, "kT")
        Ktr = A.ring(2, [128, NT, 128], BF16, "Kt")
        Var = A.ring(2, [128, NT, 136], BF16, "Va")
        SOr = A.ring(1, [128, NT, 128], F32, "SO")
        Vc = A.sb([128, 2, NT, 136], BF16, "Vc")
        CB = A.sb([128, 2, NT, 136], BF16, "CB")
        Cst = A.sb([128, 2, 3, 136], F32, "Cst")
        OUTr = A.ring(2, [128, T], BF16, "OUTS")
        sA, sB = Slot(A.ps([128, 512], F32, "psA")), Slot(A.ps([128, 512], F32, "psB"))
        psU = Ring([sA, sB])
        psS = A.ring(2, [128, 512], F32, "psS", psum=True)
        psO0 = Ring([Slot(A.ps([128, 512], F32, "psO0")), sA])
        psO1 = Ring([Slot(A.ps([128, 512], F32, "psO1")), sB])
        psT = A.ring(1, [128, 128], BF16, "psT", psum=True)
        Pfr = A.ring(2, [128, 128], BF16, "Pf")
        Pbr = A.ring(2, [128, 128], BF16, "Pb")
        Hfr = A.ring(2, [128, 128], F32, "Hf")
        hr = A.ring(2, [128, 128], F32, "h")
        hnr = A.ring(2, [128, 128], BF16, "hn")
        gsr = A.ring(2, [128, 128], F32, "gs")
        jkr = A.ring(2, [128, 128], F32, "jk")
        smr = A.ring(4, [128, 8], F32, "sm")
        vc_free = []
        for h in range(4):
            qT, f1, k1 = qTr.get()
            kT, f2, k2 = kTr.get()
            Kt, f3, k3 = Ktr.get()
            Va, f4, k4 = Var.get()
            SO, f5, k5 = SOr.get()
            hs = slice(h * 128, (h + 1) * 128)
            l_q = P.dma(qT[:], Dm["QA_T"][hs, :], waits=f1)
            l_k = P.dma(kT[:], Dm["KA_T"][hs, :], waits=f2)
            l_kt = P.dma(Kt[:], Dm["KA"].rearrange("(c p) n -> p c n", p=128)[:, :, hs], waits=f3)
            l_va = P.dma(Va[:, :, 0:128], Dm["VA"].rearrange("(c p) n -> p c n", p=128)[:, :, hs], waits=f4)
            t_on = P.memset("pool", Va[:, :, 128:129], 1.0, waits=f4)
            l_so = P.dma(SO[:], Dm["SOA"].rearrange("(c p) n -> p c n", p=128)[:, :, hs], waits=f5)
            vtok = {}
            for d_ in range(2):
                tv = P.tt("dve", Vc[:, d_, :, 0:129], Va[:, :, 0:129], COLF[:, d_, :, h:h + 1].broadcast_to([128, NT, 129]), ALU.mult,
                          waits=[l_va, t_on] + gate_toks + vc_free)
                for c in range(NT):
                    vtok[(d_, c)] = tv
            orders = [list(range(NT)), [1, 0] + list(range(NT - 1, 1, -1))]
            t0s = [P.memset("dve", Cst[:, d_, 0, 0:129], 0.0, waits=vc_free) for d_ in range(2)]
            P.memset("pool", CB[:, 0, orders[0][0], 0:129], 0.0, waits=vc_free, inc=False)
            t_cb0 = P.memset("pool", CB[:, 1, orders[1][0], 0:129], 0.0, waits=vc_free)
            cbtok = {(0, orders[0][0]): t_cb0, (1, orders[1][0]): t_cb0}
            cbhist = {0: [], 1: []}
            for i in range(NT - 1):
                for d_ in range(2):
                    c = orders[d_][i]
                    cn = orders[d_][i + 1]
                    pu, pufree, puk = psU.get()
                    tmm = P.mm(pu[:, 0:129], Kt[:, c, :], Vc[:, d_, c, 0:129], waits=[l_kt, vtok[(d_, c)]] + pufree, inc=True)
                    src = Cst[:, d_, i % 3, 0:129]
                    dst = Cst[:, d_, (i + 1) % 3, 0:129]
                    P.tt("dve", dst, src, pu[:, 0:129], ALU.add, waits=[tmm] + ([cbhist[d_][i - 3]] if i >= 3 else []), inc=False)
                    t_s = P.ts("dve", dst, dst, CARRY[:, d_, c, h:h + 1], op0=ALU.mult, waits=gate_toks)
                    psU.rel(puk, [t_s])
                    t_cb = P.copy("act", CB[:, d_, cn, 0:129], dst, waits=[t_s])
                    cbhist[d_].append(t_cb)
                    cbtok[(d_, cn)] = t_cb
            OUTS, fo, ko = OUTr.get()
            rd = []
            stA, stB = {}, {}

            def stageA(c):
                tsl = slice(c * 128, (c + 1) * 128)
                pS, psfree, psk = psS.get()
                t_S = P.mm(pS[:, 0:128], kT[:, tsl], qT[:, tsl], waits=[l_q, l_k] + psfree, inc=True)
                Pf, pff, pfk = Pfr.get()
                Pb, pbf, pbk = Pbr.get()
                P.tt("dve", Pf[:], pS[:, 0:128], utri[:], ALU.mult, waits=[t_S] + pff, inc=False)
                t_pb = P.tt("dve", Pb[:], pS[:, 0:128], ltri[:], ALU.mult, waits=pbf)
                psS.rel(psk, [t_pb])
                pO0, pof0, pok0 = psO0.get()
                pO1, pof1, pok1 = psO1.get()
                P.mm(pO0[:, 0:129], Pf[:], Vc[:, 0, c, 0:129], start=True, stop=False, waits=[t_pb, vtok[(0, c)]] + pof0)
                t_o0 = P.mm(pO0[:, 0:129], qT[:, tsl], CB[:, 0, c, 0:129], start=False, stop=True, waits=[cbtok[(0, c)]], inc=True)
                P.mm(pO1[:, 0:129], Pb[:], Vc[:, 1, c, 0:129], start=True, stop=False, waits=[t_pb, vtok[(1, c)]] + pof1)
                t_o1 = P.mm(pO1[:, 0:129], qT[:, tsl], CB[:, 1, c, 0:129], start=False, stop=True, waits=[cbtok[(1, c)]], inc=True)
                Pfr.rel(pfk, [t_o0])
                Pbr.rel(pbk, [t_o1])
                stA[c] = (pO0, pok0, pO1, pok1, t_o0, t_o1)
                return [t_o0, t_o1]

            def stageB(c):
                pO0, pok0, pO1, pok1, t_o0, t_o1 = stA.pop(c)
                sm, smf, smk = smr.get()
                rt = []
                for d_, (pOd, tk_) in enumerate(((pO0, t_o0), (pO1, t_o1))):
                    P.ts("dve", sm[:, 2 * d_:2 * d_ + 1], pOd[:, 128:129], -1.0, op0=ALU.mult, waits=[tk_] + smf, inc=False)
                    t_m = P.stt(sm[:, 2 * d_:2 * d_ + 1], sm[:, 2 * d_:2 * d_ + 1], ECOL[:, d_, c, h:h + 1], pOd[:, 128:129], ALU.max, ALU.max)
                    rt.append(P.recip(sm[:, 2 * d_ + 1:2 * d_ + 2], sm[:, 2 * d_:2 * d_ + 1], waits=[t_m]))
                Hf, hff, hfk = Hfr.get()
                hh, hhf, hhk = hr.get()
                t_hf = P.ts("dve", Hf[:], pO0[:, 0:128], sm[:, 1:2], op0=ALU.mult, waits=[rt[0]] + hff)
                psO0.rel(pok0, [t_hf])
                t_h = P.stt(hh[:], pO1[:, 0:128], sm[:, 3:4], Hf[:], ALU.mult, ALU.add, waits=[rt[1]] + hhf)
                psO1.rel(pok1, [t_h])
                Hfr.rel(hfk, [t_h])
                jk, jkf, jkk = jkr.get()
                t_sq = P.act(jk[:], hh[:], AF.Square, waits=[t_h] + jkf)
                stB[c] = (sm, smk, hh, hhk, jk, jkk, t_sq)

            def stageC(c):
                tsl = slice(c * 128, (c + 1) * 128)
                sm, smk, hh, hhk, jk, jkk, t_sq = stB.pop(c)
                t_ss = P.op("dve", lambda e, o_=sm[:, 4:5], i_=jk[:]: e.tensor_reduce(out=o_, in_=i_, axis=mybir.AxisListType.X, op=ALU.add), [t_sq])
                t_ln = P.act(sm[:, 5:6], sm[:, 4:5], AF.Ln, bias=epsb[:, 0:1], scale=1.0 / 128, waits=[t_ss])
                t_r = P.act(sm[:, 5:6], sm[:, 5:6], AF.Exp, scale=-0.5, waits=[t_ln], force=True)
                jkr.rel(jkk, [t_ss])
                gs, gsf, gsk = gsr.get()
                t_gs = P.tt("pool", gs[:], SO[:, c, :], mlg[:, hs], ALU.mult, waits=[l_so, l_mlg] + gsf)
                hn, hnf, hnk = hnr.get()
                t_hn = P.stt(hn[:], hh[:], sm[:, 5:6], gs[:], ALU.mult, ALU.mult, waits=[t_r, t_gs] + hnf)
                hr.rel(hhk, [t_hn])
                gsr.rel(gsk, [t_hn])
                smr.rel(smk, [t_hn])
                pT, ptf, ptk = psT.get()
                t_T = P.transpose(pT[:], hn[:], idb[:], waits=[t_hn, t_idb] + ptf)
                hnr.rel(hnk, [t_T])
                t_ev = P.copy("act", OUTS[:, tsl], pT[:], waits=[t_T] + fo)
                psT.rel(ptk, [t_ev])
                return t_ev, t_gs

            for c in range(NT + 2):
                if c < NT:
                    rd = stageA(c)
                if 0 <= c - 1 < NT:
                    stageB(c - 1)
                if 0 <= c - 2 < NT:
                    t_ev, t_gs = stageC(c - 2)
            vc_free = rd
            td = P.dma(Dm["MIX_T"][hs, :], OUTS[:], waits=[t_ev])
            OUTr.rel(ko, [td])
            qTr.rel(k1, rd)
            kTr.rel(k2, rd)
            Ktr.rel(k3, rd)
            Var.rel(k4, rd)
            SOr.rel(k5, [t_gs])
        if "D_G" in Dm:
            for i_, tl_ in enumerate((CUM, COLF, ECOL, CARRY, LF, LI)):
                P.dma(Dm["D_G"][:, i_, :], tl_[:].rearrange("p d c h -> p (d c h)"), waits=[td])
            P.dma(Dm["D_VC"].rearrange("p (a n) -> p a n", n=129), Vc[:].rearrange("p d c n -> p (d c) n")[:, :, 0:129], waits=[td])
            P.dma(Dm["D_CB"].rearrange("p (a n) -> p a n", n=129), CB[:].rearrange("p d c n -> p (d c) n")[:, :, 0:129], waits=[td])
            for i_, rg in enumerate((Pfr, Pbr, hnr)):
                for j_ in range(2):
                    P.dma(Dm["D_B"][:, i_, j_, :], rg.bufs[j_][:], waits=[td])
            for i_, rg in enumerate((Hfr, hr, gsr, jkr)):
                for j_ in range(2):
                    P.dma(Dm["D_F"][:, i_, j_, :], rg.bufs[j_][:], waits=[td])
            for j_ in range(4):
                P.dma(Dm["D_S"][:, j_, 0:6], smr.bufs[j_][:, 0:6], waits=[td])
        P.emit()


def block_window(nc, P, Dm):
    with ExitStack() as st:
        A = Alloc(nc, st)
        C, tl = load_consts(nc, P, A, Dm, ["IDENT", "NEGP", "NEGN", "HMASK", "SINKP"])
        idb = A.sb([128, 128], BF16, "idb")
        negp = A.sb([128, 128], BF16, "negp")
        negn = A.sb([128, 128], BF16, "negn")
        esink = A.sb([128, 4], F32, "esink")
        P.copy("pool", idb[:], C["IDENT"][:], waits=tl, inc=False)
        P.copy("pool", negp[:], C["NEGP"][:], inc=False)
        t_c = P.copy("pool", negn[:], C["NEGN"][:])
        t_es = P.act(esink[:], C["SINKP"][:], AF.Exp, waits=tl)
        QT = A.sb([128, 4, T], BF16, "QT")
        KTd = A.sb([128, 2, T], BF16, "KTd")
        KTm = A.sb([128, 2, 2, T], BF16, "KTm")
        VBt = A.sb([128, NT, 128], BF16, "VBt")
        VAe = A.sb([128, 2, 2, NT, 128], BF16, "VAe")
        ONe = A.sb([128, 2, 128], BF16, "ONe")
        OB = A.sb([128, 4, T], BF16, "OB")
        l_q = [P.dma(QT[:, i, :], Dm["QB_T"][i * 128:(i + 1) * 128, :]) for i in range(4)]
        l_k = [P.dma(KTd[:, i, :], Dm["KB_T"][i * 128:(i + 1) * 128, :]) for i in range(2)]
        l_v = P.dma(VBt[:], Dm["VB"].rearrange("(c p) n -> p c n", p=128))
        prep = []
        for kv in range(2):
            for e in range(2):
                en = "pool" if e == 0 else "dve"
                prep.append(P.ts(en, KTm[:, kv, e, :], KTd[:, kv, :], C["HMASK"][:, e:e + 1], op0=ALU.mult, waits=l_k + tl))
        P.memset("pool", VAe[:].rearrange("p a b c d -> p (a b c d)"), 0.0, inc=False)
        P.memset("pool", ONe[:].rearrange("p a b -> p (a b)"), 0.0, inc=False)
        for e in range(2):
            P.memset("pool", ONe[:, e, e * 64:(e + 1) * 64], 1.0, inc=False)
            for kv in range(2):
                tk = P.copy("pool", VAe[:, kv, e, :, e * 64:(e + 1) * 64], VBt[:, :, kv * 64:(kv + 1) * 64], waits=[l_v])
        prep.append(tk)
        prep += [t_c, t_es]
        psL = A.ring(2, [128, 512], F32, "psL", psum=True)
        psX = A.ring(2, [128, 512], F32, "psX", psum=True)
        psO = A.ring(2, [128, 512], F32, "psO", psum=True)
        psD = A.ring(2, [128, 512], F32, "psD", psum=True)
        PLr = A.ring(3, [128, 384], BF16, "PL")
        PXr = A.ring(3, [128, 256], BF16, "PX")
        dsr = A.ring(2, [128, 128], F32, "ds")
        rr = A.ring(2, [128, 128], F32, "rr")

        items = []
        for n in range(NT):
            for i in range(4):
                for e in range(2):
                    items.append((n, i, e))

        def kbs_of(n):
            if n < 2:
                return [], [0, 1]
            L = [kb for kb in (n - 1, n, n + 1) if 2 <= kb < NT]
            return L, [0, 1]

        def emit_S(it):
            n, i, e = it
            kv = i // 2
            qs = slice(n * 128, (n + 1) * 128)
            L, X = kbs_of(n)
            pl = plk = None
            tL = None
            if L:
                pl, plf, plk = psL.get()
                for idx, kb in enumerate(L):
                    masked = kb != n
                    tL = P.mm(pl[:, idx * 128:(idx + 1) * 128], KTm[:, kv, e, kb * 128:(kb + 1) * 128], QT[:, i, qs], start=True, stop=not masked,
                              waits=l_q + prep + plf, inc=(not masked and idx == len(L) - 1))
                    if masked:
                        tL = P.mm(pl[:, idx * 128:(idx + 1) * 128], idb[:], (negp if kb == n - 1 else negn)[:], start=False, stop=True,
                                  inc=(idx == len(L) - 1))
            px, pxf, pxk = psX.get()
            for idx, kb in enumerate(X):
                tX = P.mm(px[:, idx * 128:(idx + 1) * 128], KTm[:, kv, e, kb * 128:(kb + 1) * 128], QT[:, i, qs], start=True, stop=True,
                          waits=l_q + prep + pxf, inc=(idx == 1))
            return dict(it=it, L=L, X=X, pl=pl, plk=plk, tL=tL, px=px, pxk=pxk, tX=tX)

        def emit_exp(S):
            if S["L"]:
                PL, f, k = PLr.get()
                nl = len(S["L"]) * 128
                S["tPL"] = P.act(PL[:, 0:nl], S["pl"][:, 0:nl], AF.Exp, scale=0.125, waits=[S["tL"]] + f)
                psL.rel(S["plk"], [S["tPL"]])
                S["PL"], S["PLk"] = PL, k
            PX, f, k = PXr.get()
            S["tPX"] = P.act(PX[:], S["px"][:, 0:256], AF.Exp, scale=0.125, waits=[S["tX"]] + f)
            psX.rel(S["pxk"], [S["tPX"]])
            S["PX"], S["PXk"] = PX, k

        cur = {}

        def emit_PV(S):
            n, i, e = S["it"]
            kv = i // 2
            if e == 0:
                po, pof, pok = psO.get()
                pd, pdf, pdk = psD.get()
                cur["po"], cur["pok"], cur["pof"] = po, pok, pof + pdf
                cur["pd"], cur["pdk"] = pd, pdk
            po = cur["po"]
            pd = cur["pd"]
            seq = [("L", idx, kb) for idx, kb in enumerate(S["L"])] + [("X", idx, kb) for idx, kb in enumerate(S["X"])]
            last = None
            for j, (reg, idx, kb) in enumerate(seq):
                src = (S["PL"] if reg == "L" else S["PX"])[:, idx * 128:(idx + 1) * 128]
                wt = [S["tPL"] if reg == "L" else S["tPX"]] + (cur["pof"] if (e == 0 and j == 0) else [])
                first = (e == 0 and j == 0)
                fin = (e == 1 and j == len(seq) - 1)
                P.mm(po[:, 0:128], VAe[:, kv, e, kb, :], src, start=first, stop=fin, waits=wt)
                last = P.mm(pd[:, 0:128], ONe[:, e, :], src, start=first, stop=fin, inc=(j == len(seq) - 1))
            if S["L"]:
                PLr.rel(S["PLk"], [last])
            PXr.rel(S["PXk"], [last])
            if e == 1:
                ds, f, k = dsr.get()
                t_ds = P.ts("dve", ds[:], pd[:, 0:128], esink[:, i:i + 1], op0=ALU.add, waits=[last] + f)
                psD.rel(cur["pdk"], [t_ds])
                r_, f2, k2 = rr.get()
                t_r = P.recip(r_[:], ds[:], waits=f2)
                t_o = P.tt("dve", OB[:, i, n * 128:(n + 1) * 128], po[:, 0:128], r_[:], ALU.mult, waits=[t_r])
                psO.rel(cur["pok"], [t_o])
                dsr.rel(k, [t_r])
                rr.rel(k2, [t_o])
                cur["last_o"] = t_o

        prev = None
        for it in items:
            S = emit_S(it)
            emit_exp(S)
            if prev is not None:
                emit_PV(prev)
            prev = S
        emit_PV(prev)
        for i in range(4):
            P.dma(Dm["MIX_T"][512 + i * 128:512 + (i + 1) * 128, :], OB[:, i, :], waits=[cur["last_o"]])
        P.emit()


def block_outmlp(nc, P, Dm, l):
    xsrc = Dm["XIN"] if l == 0 else Dm["X1"]
    mix = Dm["MIX_T"] if l == 0 else Dm["MIX1_T"]
    dst = Dm["X1"] if l == 0 else Dm["OUT_T"]
    if l == 0:
        groups = [[(0, 256, 1), (256, 512, 0), (768, 320, 0)]] + [[(g * 1088, 384, 0), (g * 1088 + 384, 384, 0), (g * 1088 + 768, 320, 0)] for g in range(1, 4)]
        mixoff, dstoff = 0, 0
    else:
        groups = [[(T_CTX + g * 1024, 512, 0), (T_CTX + g * 1024 + 512, 512, 0)] for g in range(4)]
        mixoff, dstoff = T_CTX, T_CTX
    GT = 1088
    with ExitStack() as st:
        A = Alloc(nc, st)
        modv = A.sb([128, 6, 8, 2], F32, "modv")
        ones = A.sb([128, 128], F32, "ones")
        epsb = A.sb([128, 1], F32, "eps")
        l_mod = P.dma(modv[:], Dm["MODV"][:, l])
        l_ones = P.dma(ones[:], Dm["ONES"])
        t_eps = P.memset("pool", epsb[:], EPS)
        lc = [l_mod, l_ones, t_eps]
        wout = A.sb([128, 8, 1024], BF16, "wout")
        wfr = A.ring(4, [128, 1024], F32, "wf")
        wbr = A.ring(4, [128, 1024], BF16, "wb")
        wo_t = []
        for kc in range(8):
            wf, f, k = wfr.get()
            ld = P.dma(wf[:], Dm["WOUT%d" % l][:, kc, :], waits=f)
            tk = P.copy("act", wout[:, kc, :], wf[:], waits=[ld])
            wfr.rel(k, [tk])
            wo_t.append(tk)
        xmid = A.sb([128, 8, GT], F32, "xmid")
        h2T = A.sb([128, 8, GT], BF16, "h2T")
        h1 = A.sb([128, 32, GT], BF16, "h1")
        mxr = A.ring(2, [128, 8, 512], BF16, "mx")
        sqr = A.ring(2, [128, 512], BF16, "sq")
        rsr = A.ring(3, [128, 512], F32, "rs")
        onesb = A.sb([128, 128], BF16, "onesb")
        t_ob = P.memset("pool", onesb[:], 1.0)
        tmr = A.ring(2, [128, 512], F32, "tm")
        rlr = A.ring(2, [128, 512], F32, "rl")
        psr = A.ring(4, [128, 512], F32, "ps", psum=True)
        pss = A.ring(2, [128, 512], F32, "pss", psum=True)
        xm_free = []
        h1_free = []
        h2_free = []
        srcs = []
        for grp in groups:
            srcs += [Dm["W1_%d" % l][f_] for f_ in range(32)]
            srcs += [Dm["W2_%d" % l][nch, q] for nch in range(8) for q in range(4)]
        wst = WStream(P, wfr, wbr, srcs, depth=3)
        def load_x(grp_, waits_by_nch):
            g0_ = grp_[0][0]
            nt_ = grp_[-1][0] + grp_[-1][1] - g0_
            return [P.dma(xmid[:, nch_, 0:nt_], xsrc[nch_ * 128:(nch_ + 1) * 128, g0_:g0_ + nt_], waits=waits_by_nch[nch_]) for nch_ in range(8)]

        def load_x_one(grp_, nch_, waits_):
            g0_ = grp_[0][0]
            nt_ = grp_[-1][0] + grp_[-1][1] - g0_
            return P.dma(xmid[:, nch_, 0:nt_], xsrc[nch_ * 128:(nch_ + 1) * 128, g0_:g0_ + nt_], waits=waits_, en="pool")

        ldx_n = load_x(groups[0], [[] for _ in range(8)])
        for gi_, grp in enumerate(groups):
            g0 = grp[0][0]
            h2_toks = []
            ldx_next = [None] * 8

            def a_stage1(t0, n, j):
                o = t0 - g0
                mx, mf, mk = mxr.get()
                ldm = P.dma(mx[:, :, 0:n], mix.rearrange("(kc p) t -> p kc t", p=128)[:, :, t0 - mixoff:t0 - mixoff + n], waits=mf)
                ps2, p2f, p2k = pss.get()
                for nch in range(8):
                    ps, pf, pk = psr.get()
                    for kc in range(8):
                        tmm = P.mm(ps[:, 0:n], wout[:, kc, nch * 128:(nch + 1) * 128], mx[:, kc, 0:n], start=(kc == 0), stop=(kc == 7),
                                   waits=[ldm] + wo_t + pf, inc=(kc == 7))
                    t_x = P.stt(xmid[:, nch, o:o + n], ps[:, 0:n], modv[:, 2, nch, j:j + 1], xmid[:, nch, o:o + n], ALU.mult, ALU.add, waits=[tmm, ldx_n[nch]] + lc)
                    psr.rel(pk, [t_x])
                    sq, sf, sk = sqr.get()
                    tq = P.act(sq[:, 0:n], xmid[:, nch, o:o + n], AF.Square, waits=[t_x] + sf)
                    tm2 = P.mm(ps2[:, 0:n], onesb[:], sq[:, 0:n], start=(nch == 0), stop=(nch == 7), waits=[tq, t_ob] + lc + p2f, inc=True)
                    sqr.rel(sk, [tm2])
                mxr.rel(mk, [tmm])
                rs, rf, rk = rsr.get()
                P.act(rs[:, 0:n], ps2[:, 0:n], AF.Ln, bias=epsb[:, 0:1], scale=1.0 / D, waits=[tm2] + rf, inc=False)
                t_rs = P.act(rs[:, 0:n], rs[:, 0:n], AF.Exp, scale=-0.5)
                pss.rel(p2k, [t_rs])
                return (o, n, j, rs, rk, t_rs)

            def a_stage2(S):
                o, n, j, rs, rk, t_rs = S
                toks = [None, None]
                for kc in range(8):
                    tm, tf, tk = tmr.get()
                    t1 = P.stt(tm[:, 0:n], xmid[:, kc, o:o + n], modv[:, 4, kc, j:j + 1], rs[:, 0:n], ALU.mult, ALU.mult, waits=[t_rs] + tf)
                    if kc < 2:
                        t2 = P.ts("pool", h2T[:, kc, o:o + n], tm[:, 0:n], modv[:, 3, kc, j:j + 1], op0=ALU.add, waits=[t1] + h2_free)
                    else:
                        t2 = P.act(h2T[:, kc, o:o + n], tm[:, 0:n], AF.Identity, bias=modv[:, 3, kc, j:j + 1], waits=[t1] + h2_free)
                    tmr.rel(tk, [t2])
                    toks[0 if kc < 2 else 1] = t2
                rsr.rel(rk, [t1])
                h2_toks.append(toks)

            prev_a = None
            for sgd in grp:
                cur_a = a_stage1(*sgd)
                if prev_a is not None:
                    a_stage2(prev_a)
                prev_a = cur_a
            a_stage2(prev_a)
            last1 = None
            for f_ in range(32):
                wb, tc_, bk = wst.next()
                for si, (t0, n, j) in enumerate(grp):
                    o = t0 - g0
                    ps, pf, pk = psr.get()
                    for kc in range(8):
                        tmm = P.mm(ps[:, 0:n], wb[:, kc * 128:(kc + 1) * 128], h2T[:, kc, o:o + n], start=(kc == 0), stop=(kc == 7),
                                   waits=[tc_, h2_toks[si]] + pf, inc=(kc == 7))
                    rl, rf, rk = rlr.get()
                    t_r = P.act(rl[:, 0:n], ps[:, 0:n], AF.Relu, waits=[tmm] + rf)
                    psr.rel(pk, [t_r])
                    en = "dve"
                    t_h1 = P.tt(en, h1[:, f_, o:o + n], rl[:, 0:n], rl[:, 0:n], ALU.mult, waits=[t_r] + h1_free)
                    rlr.rel(rk, [t_h1])
                    last1 = t_h1
                wbr.rel(bk, [tmm])
            h2_free = [tmm]
            for nch in range(8):
                st_toks = []
                pst = []
                for si, (t0, n, j) in enumerate(grp):
                    ps, pf, pk = psr.get()
                    pst.append((ps, pf, pk))
                for q in range(4):
                    wb, tc_, bk = wst.next()
                    for si, (t0, n, j) in enumerate(grp):
                        o = t0 - g0
                        ps, pf, pk = pst[si]
                        for fi in range(8):
                            f_ = q * 8 + fi
                            tmm = P.mm(ps[:, 0:n], wb[:, fi * 128:(fi + 1) * 128], h1[:, f_, o:o + n], start=(f_ == 0), stop=(f_ == 31),
                                       waits=[tc_, last1] + pf, inc=(fi == 7))
                    wbr.rel(bk, [tmm])
                for si, (t0, n, j) in enumerate(grp):
                    o = t0 - g0
                    ps, pf, pk = pst[si]
                    t_x = P.stt(xmid[:, nch, o:o + n], ps[:, 0:n], modv[:, 5, nch, j:j + 1], xmid[:, nch, o:o + n], ALU.mult, ALU.add, waits=[tmm])
                    psr.rel(pk, [t_x])
                    if l == 0 or j == 0:
                        st_toks.append(P.dma(dst[nch * 128:(nch + 1) * 128, t0 - dstoff:t0 - dstoff + n], xmid[:, nch, o:o + n], waits=[t_x], en="pool"))
                    else:
                        st_toks.append(t_x)
                if gi_ + 1 < len(groups):
                    ldx_next[nch] = load_x_one(groups[gi_ + 1], nch, st_toks)
            h1_free = [tmm]
            ldx_n = ldx_next
        P.emit()


def block_fullattn(nc, P, Dm):
    SHIFT = -4.0
    with ExitStack() as st:
        A = Alloc(nc, st)
        KT = A.sb([128, 2, T], BF16, "KT")
        V = A.sb([128, NT, 256], BF16, "V")
        onb = A.sb([128, 128], BF16, "onb")
        shb = A.sb([128, 1], F32, "shb")
        t_on = P.memset("pool", onb[:], 1.0)
        t_sh = P.memset("pool", shb[:], SHIFT)
        l_k = [P.dma(KT[:, i, :], Dm["KC_T"][i * 128:(i + 1) * 128, :]) for i in range(2)]
        l_v = P.dma(V[:], Dm["VC"].rearrange("(c p) n -> p c n", p=128))
        lc = l_k + [l_v, t_on, t_sh]
        qr = A.ring(2, [128, 512], BF16, "q")
        psS = A.ring(2, [128, 1024], F32, "psS", psum=True)
        psO = A.ring(2, [128, 512], F32, "psO", psum=True)
        psD = A.ring(2, [128, 512], F32, "psD", psum=True)
        PTr = A.ring(3, [128, 1024], BF16, "PT")
        rr = A.ring(2, [128, 512], F32, "r")
        obr = A.ring(2, [128, 512], BF16, "ob")
        scale = 128.0 ** -0.5
        onf = A.sb([128, 128], F32, "onf")
        t_onf = P.memset("pool", onf[:], 1.0)
        accr = A.ring(2, [128, 1024], F32, "acc")
        NP = NT // 2
        for h in range(8):
            kv = h // 4
            for g in range(8):
                q, qf, qk = qr.get()
                l_q = P.dma(q[:], Dm["QC_T"][h * 128:(h + 1) * 128, g * 512:(g + 1) * 512], waits=qf)
                po, pof, pok = psO.get()
                pd, pdf, pdk = psD.get()
                acc, accf, acck = accr.get()
                pend = None
                st8 = {"first_d": True, "first_a": True, "t_acc": None}

                def pv(pend, last):
                    PT, tk_, j_, ptk = pend
                    rel = []
                    for hf in range(2):
                        kb_ = 2 * j_ + hf
                        first = (kb_ == 0)
                        t_ = P.mm(po[:], V[:, kb_, kv * 128:(kv + 1) * 128], PT[:, hf * 512:(hf + 1) * 512], start=first, stop=(last and hf == 1),
                                  waits=[tk_] + (pof if first else []), inc=(hf == 1))
                    rel.append(t_)
                    if j_ % 3 == 0:
                        for hf in range(2):
                            t2_ = P.mm(pd[:], onb[:], PT[:, hf * 512:(hf + 1) * 512], start=st8["first_d"], stop=False,
                                       waits=(pdf if st8["first_d"] else []), inc=(hf == 1))
                            st8["first_d"] = False
                        rel.append(t2_)
                    else:
                        if st8["first_a"]:
                            t2_ = P.copy("dve", acc[:], PT[:], waits=[tk_] + accf)
                            st8["first_a"] = False
                        else:
                            t2_ = P.tt("dve", acc[:], acc[:], PT[:], ALU.add, waits=[tk_])
                        st8["t_acc"] = t2_
                        rel.append(t2_)
                    PTr.rel(ptk, rel)
                    return t_
                for j in range(NP):
                    ps, psf, psk = psS.get()
                    for hf in range(2):
                        kb = 2 * j + hf
                        t_s = P.mm(ps[:, hf * 512:(hf + 1) * 512], KT[:, kv, kb * 128:(kb + 1) * 128], q[:], waits=[l_q] + lc + psf, inc=(hf == 1))
                    PT, ptf, ptk = PTr.get()
                    t_e = P.act(PT[:], ps[:], AF.Exp, bias=shb[:, 0:1], scale=scale, waits=[t_s] + ptf)
                    psS.rel(psk, [t_e])
                    if pend is not None:
                        pv(pend, False)
                    pend = (PT, t_e, j, ptk)
                pv(pend, True)
                P.mm(pd[:], onf[:], acc[:, 0:512], start=False, stop=False, waits=[st8["t_acc"], t_onf])
                t_last = P.mm(pd[:], onf[:], acc[:, 512:1024], start=False, stop=True, inc=True)
                accr.rel(acck, [t_last])
                qr.rel(qk, [t_last])
                r_, rf, rk = rr.get()
                t_r = P.recip(r_[:], pd[:], waits=[t_last] + rf)
                psD.rel(pdk, [t_r])
                ob, of, ok = obr.get()
                t_o = P.tt("dve", ob[:], po[:], r_[:], ALU.mult, waits=[t_r] + of)
                psO.rel(pok, [t_o])
                rr.rel(rk, [t_o])
                td = P.dma(Dm["MIX1_T"][h * 128:(h + 1) * 128, g * 512:(g + 1) * 512], ob[:], waits=[t_o])
                obr.rel(ok, [td])
        P.emit()


IN_SPECS = {
    "XIN": ([1024, T], F32), "CV": ([128, 8, 2], F32), "ADAW": ([2, 1024, 6144], F32), "ADABR": ([2, 6144], F32),
    "NG": ([128, 2, 2, 8], F32),
    "WF0": ([14, 128, 1024], F32), "WT0": ([128, 8, 1680], F32), "GATEB": ([1, 16], F32), "MLNG": ([1, 512], F32),
    "QKG0": ([128, 4], F32), "SINKP": ([128, 4], F32), "COS0": ([128, T], F32), "SIN0": ([128, T], F32), "BLK0": ([128, 128], F32),
    "WOUT0": ([128, 8, 1024], F32), "W1_0": ([32, 128, 1024], F32), "W2_0": ([8, 4, 128, 1024], F32),
    "WF1": ([10, 128, 1024], F32), "WT1": ([128, 8, 256], F32), "QKG1": ([128, 4], F32), "COS1": ([128, T], F32), "SIN1": ([128, T], F32),
    "BLK1": ([128, 128], F32), "WOUT1": ([128, 8, 1024], F32), "W1_1": ([32, 128, 1024], F32), "W2_1": ([8, 4, 128, 1024], F32),
    "ONES": ([128, 128], F32), "IDENT": ([128, 128], F32), "UTRI": ([128, 128], F32), "LTRI": ([128, 128], F32),
    "NEGP": ([128, 128], F32), "NEGN": ([128, 128], F32), "HMASK": ([128, 2], F32),
}
SCRATCH = {
    "MODV": ([128, 2, 6, 8, 2], F32),
    "QA_T": ([512, T], BF16), "KA_T": ([512, T], BF16), "KA": ([T, 512], BF16), "VA": ([T, 512], BF16), "SOA": ([T, 512], F32),
    "GA": ([T, 16], F32), "VB": ([T, 128], BF16), "QB_T": ([512, T], BF16), "KB_T": ([256, T], BF16),
    "MIX_T": ([1024, T], BF16), "X1": ([1024, T], F32),
    "QC_T": ([1024, T_LAT], BF16), "KC_T": ([256, T], BF16), "VC": ([T, 256], BF16), "MIX1_T": ([1024, T_LAT], BF16),
}
STAGE_INPUTS = {
    "mod": ["CV", "ADAW", "ADABR", "NG", "IDENT"],
    "in0": ["XIN", "ONES", "WF0", "WT0", "GATEB", "QKG0", "COS0", "SIN0", "BLK0"],
    "mlstm": ["ONES", "UTRI", "LTRI", "IDENT", "MLNG"],
    "win": ["IDENT", "NEGP", "NEGN", "HMASK", "SINKP"],
    "mlp0": ["XIN", "ONES", "WOUT0", "W1_0", "W2_0"],
    "in1": ["ONES", "WF1", "WT1", "QKG1", "COS1", "SIN1", "BLK1"],
    "attn": [],
    "mlp1": ["ONES", "WOUT1", "W1_1", "W2_1"],
}
ALL_STAGES = ("mod", "in0", "mlstm", "win", "mlp0", "in1", "attn", "mlp1")


def build(stages=ALL_STAGES, dbg=()):
    nc = bass.Bass("TRN2", target_bir_lowering=False)
    Dm = {}
    need = set()
    for sname in stages:
        need |= set(STAGE_INPUTS[sname])
    for k, (shape, dt) in IN_SPECS.items():
        if k in need:
            Dm[k] = nc.dram_tensor(k, shape, dt, kind="ExternalInput").ap()
    for k, (shape, dt) in SCRATCH.items():
        kind = "ExternalOutput" if k in dbg else ("ExternalInput" if ("in:" + k) in dbg else "Internal")
        Dm[k] = nc.dram_tensor(k, shape, dt, kind=kind).ap()
    Dm["OUT_T"] = nc.dram_tensor("OUT_T", [1024, T_LAT], F32, kind="ExternalOutput").ap()
    if "D_G" in dbg:
        Dm["D_G"] = nc.dram_tensor("D_G", [128, 6, 272], F32, kind="ExternalOutput").ap()
        Dm["D_B"] = nc.dram_tensor("D_B", [128, 3, 2, 128], BF16, kind="ExternalOutput").ap()
        Dm["D_F"] = nc.dram_tensor("D_F", [128, 4, 2, 128], F32, kind="ExternalOutput").ap()
        Dm["D_S"] = nc.dram_tensor("D_S", [128, 4, 8], F32, kind="ExternalOutput").ap()
        Dm["D_VC"] = nc.dram_tensor("D_VC", [128, 2 * NT * 129], BF16, kind="ExternalOutput").ap()
        Dm["D_CB"] = nc.dram_tensor("D_CB", [128, 2 * NT * 129], BF16, kind="ExternalOutput").ap()
    with ExitStack() as st:
        P = Prog(nc, st)
        if "mod" in stages:
            with nc.named_scope("mod"):
                block_mod(nc, P, Dm)
        if "in0" in stages:
            with nc.named_scope("in0"):
                block_inproj(nc, P, Dm, 0)
        if "mlstm" in stages:
            with nc.named_scope("mlstm"):
                block_mlstm(nc, P, Dm)
        if "win" in stages:
            with nc.named_scope("win"):
                block_window(nc, P, Dm)
        if "mlp0" in stages:
            with nc.named_scope("mlp0"):
                block_outmlp(nc, P, Dm, 0)
        if "in1" in stages:
            with nc.named_scope("in1"):
                block_inproj(nc, P, Dm, 1)
        if "attn" in stages:
            with nc.named_scope("attn"):
                block_fullattn(nc, P, Dm)
        if "mlp1" in stages:
            with nc.named_scope("mlp1"):
                block_outmlp(nc, P, Dm, 1)
    return nc


def _fm(w, cols):
    sub = w[:, cols]
    return np.ascontiguousarray(sub.reshape(8, 128, 128).transpose(1, 0, 2).reshape(128, 1024))


def _tm(w, cols):
    sub = w[:, cols]
    return np.ascontiguousarray(sub.reshape(8, 128, len(cols)).transpose(1, 0, 2))


def _rope_tables(dh, quarter_layout):
    pairs = dh // 4
    inv = (np.float32(10000.0) ** (-np.arange(pairs, dtype=np.float32) / np.float32(pairs))).astype(np.float32)
    tt = np.arange(T_LAT)
    row = (tt // 64).astype(np.float32)
    col = (tt % 64).astype(np.float32)
    ang = np.concatenate([row[:, None] * inv[None, :], col[:, None] * inv[None, :]], axis=1).astype(np.float32)
    half = dh // 2
    cos = np.ones((128, T), np.float32)
    sin = np.zeros((128, T), np.float32)
    for p in range(128):
        j = p % half
        sgn = -1.0 if p < 64 else 1.0
        cos[p, T_CTX:] = np.cos(ang[:, j])
        sin[p, T_CTX:] = sgn * np.sin(ang[:, j])
    return cos, sin


def host_shared(inp):
    f = lambda a: np.ascontiguousarray(np.asarray(a, dtype=np.float32))
    S = {}
    S["ADAW"] = f(inp["ada_w"])
    S["ADABR"] = f(inp["ada_b"])
    ng = np.stack([inp["norm1_g"].reshape(2, 8, 128), inp["norm2_g"].reshape(2, 8, 128)], axis=1)
    S["NG"] = f(ng.transpose(3, 0, 1, 2))
    W = np.asarray(inp["ab_w_in"][0], np.float32)
    chunks = [np.arange(i * 128, (i + 1) * 128) for i in range(4)] + [512 + np.arange(i * 128, (i + 1) * 128) for i in range(4)]
    p = np.arange(128)
    quarter, r = p // 32, p % 32
    dimp = (quarter // 2) * 32 + r
    for i in range(4):
        chunks.append(2064 + (2 * i + quarter % 2) * 64 + dimp)
    for kv in range(2):
        chunks.append(2576 + kv * 64 + dimp)
    S["WF0"] = np.stack([_fm(W, c) for c in chunks])
    tcols = np.concatenate([np.arange(512, 1024), np.arange(1024, 1536), np.arange(1536, 2048), np.arange(2048, 2064), np.arange(2704, 2832)])
    S["WT0"] = _tm(W, tcols)
    S["GATEB"] = f(inp["ab_gate_b"][0][None, :])
    S["MLNG"] = f(inp["mlstm_norm_g"][0][None, :])
    gq = np.asarray(inp["swa_q_norm_g"][0], np.float32)[dimp]
    gk = np.asarray(inp["swa_k_norm_g"][0], np.float32)[dimp]
    sh = (p + 64) % 128
    S["QKG0"] = f(np.stack([gq, gq[sh], gk, gk[sh]], axis=1))
    sink = np.asarray(inp["swa_sink"][0], np.float32)
    S["SINKP"] = f(np.stack([sink[2 * i + (p >= 64)] for i in range(4)], axis=1))
    S["COS0"], S["SIN0"] = _rope_tables(64, True)
    S["BLK0"] = f(((p[:, None] // 32) % 2 == (p[None, :] // 32) % 2))
    S["WOUT0"] = _tm(np.asarray(inp["ab_w_out"][0], np.float32), np.arange(1024))
    W1 = np.asarray(inp["c_w_in"][0], np.float32)
    S["WF1"] = np.stack([_fm(W1, np.arange(i * 128, (i + 1) * 128)) for i in range(10)])
    S["WT1"] = _tm(W1, np.arange(1280, 1536))
    gq1 = np.asarray(inp["c_q_norm_g"][0], np.float32)
    gk1 = np.asarray(inp["c_k_norm_g"][0], np.float32)
    S["QKG1"] = f(np.stack([gq1, gq1[sh], gk1, gk1[sh]], axis=1))
    S["COS1"], S["SIN1"] = _rope_tables(128, False)
    S["BLK1"] = np.ones((128, 128), np.float32)
    S["WOUT1"] = _tm(np.asarray(inp["c_w_out"][0], np.float32), np.arange(1024))
    for l in range(2):
        w1 = np.asarray(inp["mlp_w1"][l], np.float32)
        S["W1_%d" % l] = np.ascontiguousarray(w1.reshape(8, 128, 32, 128).transpose(2, 1, 0, 3).reshape(32, 128, 1024))
        w2 = np.asarray(inp["mlp_w2"][l], np.float32)
        S["W2_%d" % l] = np.ascontiguousarray(w2.reshape(4, 8, 128, 8, 128).transpose(3, 0, 2, 1, 4).reshape(8, 4, 128, 1024))
    S["ONES"] = np.ones((128, 128), np.float32)
    S["IDENT"] = np.eye(128, dtype=np.float32)
    S["UTRI"] = np.triu(np.ones((128, 128), np.float32))
    S["LTRI"] = np.tril(np.ones((128, 128), np.float32))
    s_, t_ = p[:, None], p[None, :]
    S["NEGP"] = np.where(t_ <= s_, 0.0, NEG).astype(np.float32)
    S["NEGN"] = np.where(s_ <= t_, 0.0, NEG).astype(np.float32)
    S["HMASK"] = f(np.stack([(p // 32) % 2 == 0, (p // 32) % 2 == 1], axis=1))
    return S


def host_core(inp, b):
    x = np.asarray(inp["x"][b], np.float32)
    ctx = np.asarray(inp["ctx"][b], np.float32)
    m = {"XIN": np.ascontiguousarray(np.concatenate([ctx.T, x.T], axis=1))}
    cv = np.stack([np.asarray(inp["c"][b], np.float32).reshape(8, 128), np.asarray(inp["c_ctx"], np.float32).reshape(8, 128)], axis=-1)
    m["CV"] = np.ascontiguousarray(cv.transpose(1, 0, 2))
    return m


_NC_CACHE = {}


def kernel(**inputs):
    S = host_shared(inputs)
    in_maps = []
    for b in range(8):
        m = dict(S)
        m.update(host_core(inputs, b))
        in_maps.append(m)
    if "nc" not in _NC_CACHE:
        _NC_CACHE["nc"] = build()
    res = run_bass_kernel_spmd(_NC_CACHE["nc"], in_maps, core_ids=list(range(8)))
    out = np.stack([np.ascontiguousarray(res.results[b]["OUT_T"].T) for b in range(8)], axis=0)
    return out.astype(np.float32)
```
